# Optimizing a Trainium2 kernel written in Bass

```python
import jax, jax.numpy as jnp
from jax import lax
import numpy as np

D_MODEL = 1024
BATCH = 8
SEQ = 4096
DEPTH = 2

N_MIXERS = 2
N_ATTN_LAYERS = (DEPTH + 1) // 2
N_SGU_LAYERS = DEPTH // 2

HEAD_DIM = 64
N_Q_HEADS = D_MODEL // HEAD_DIM
N_KV_HEADS = N_Q_HEADS // 4
GQA_GROUP = N_Q_HEADS // N_KV_HEADS
WINDOW = 128
Q_BLOCK = WINDOW
ROPE_THETA = 10000.0
Q_WIDTH = N_Q_HEADS * HEAD_DIM
KV_WIDTH = N_KV_HEADS * HEAD_DIM
QKV_WIDTH = Q_WIDTH + 2 * KV_WIDTH

SGU_WIDTH = D_MODEL
SGU_GROUPS = 8
SGU_GROUP_DIM = SGU_WIDTH // SGU_GROUPS
SGU_CHUNK = 128

D_FF = ((8 * D_MODEL + 3 * 256 - 1) // (3 * 256)) * 256

EPS = 1e-6

kernel_name = "hybrid_swa_sink_gqa_chunked_sgu_swiglu"


def rmsnorm(x, g):
    xf = x.astype(jnp.float32)
    y = xf * lax.rsqrt(jnp.mean(xf * xf, axis=-1, keepdims=True) + EPS)
    return (y * g.astype(jnp.float32)).astype(x.dtype)


def layernorm(x, g, b):
    xf = x.astype(jnp.float32)
    mu = jnp.mean(xf, axis=-1, keepdims=True)
    var = jnp.mean(jnp.square(xf - mu), axis=-1, keepdims=True)
    y = (xf - mu) * lax.rsqrt(var + EPS)
    return (y * g.astype(jnp.float32) + b.astype(jnp.float32)).astype(x.dtype)


def rope(x, pos):
    half = HEAD_DIM // 2
    inv_freq = ROPE_THETA ** (-(jnp.arange(half, dtype=jnp.float32) * 2.0) / HEAD_DIM)
    ang = pos.astype(jnp.float32)[:, None] * inv_freq[None, :]
    cos = jnp.cos(ang)[None, :, None, :].astype(x.dtype)
    sin = jnp.sin(ang)[None, :, None, :].astype(x.dtype)
    x1, x2 = x[..., :half], x[..., half:]
    return jnp.concatenate([x1 * cos - x2 * sin, x2 * cos + x1 * sin], axis=-1)


def swa_sink_attention(h, w_qkv, b_qkv, sinks, w_o, b_o):
    B, S, _ = h.shape
    nb = S // Q_BLOCK
    T = Q_BLOCK
    qkv = h @ w_qkv + b_qkv
    q = qkv[..., :Q_WIDTH].reshape(B, S, N_Q_HEADS, HEAD_DIM)
    k = qkv[..., Q_WIDTH:Q_WIDTH + KV_WIDTH].reshape(B, S, N_KV_HEADS, HEAD_DIM)
    v = qkv[..., Q_WIDTH + KV_WIDTH:].reshape(B, S, N_KV_HEADS, HEAD_DIM)
    pos = jnp.arange(S, dtype=jnp.int32)
    q = rope(q, pos)
    k = rope(k, pos)
    qb = q.reshape(B, nb, T, N_KV_HEADS, GQA_GROUP, HEAD_DIM)
    kb = k.reshape(B, nb, T, N_KV_HEADS, HEAD_DIM)
    vb = v.reshape(B, nb, T, N_KV_HEADS, HEAD_DIM)
    pad = jnp.zeros_like(kb[:, :1])
    kk = jnp.concatenate([jnp.concatenate([pad, kb[:, :-1]], axis=1), kb], axis=2)
    vv = jnp.concatenate([jnp.concatenate([pad, vb[:, :-1]], axis=1), vb], axis=2)
    scale = HEAD_DIM ** -0.5
    s = jnp.einsum('bnqhgd,bnkhd->bnhgqk', qb, kk).astype(jnp.float32) * scale
    qpos = jnp.arange(T)[:, None] + T
    kpos = jnp.arange(2 * T)[None, :]
    band = (kpos <= qpos) & (qpos - kpos < WINDOW)
    blk = jnp.arange(nb)[:, None, None]
    valid = band[None] & ((blk * T + kpos[None] - T) >= 0)
    s = jnp.where(valid[None, :, None, None], s, -jnp.inf)
    sink = sinks.astype(jnp.float32).reshape(1, 1, N_KV_HEADS, GQA_GROUP, 1, 1)
    m = jnp.maximum(jnp.max(s, axis=-1, keepdims=True), sink)
    p = jnp.exp(s - m)
    denom = jnp.sum(p, axis=-1, keepdims=True) + jnp.exp(sink - m)
    probs = (p / denom).astype(vv.dtype)
    o = jnp.einsum('bnhgqk,bnkhd->bnqhgd', probs, vv).reshape(B, S, Q_WIDTH)
    return o @ w_o + b_o


def chunked_sgu(h, w_in, ln_g, ln_b, w_spatial, b_spatial, w_out):
    B, S, _ = h.shape
    nc = S // SGU_CHUNK
    z = jax.nn.gelu(h @ w_in)
    u, v = z[..., :SGU_WIDTH], z[..., SGU_WIDTH:]
    v = layernorm(v, ln_g, ln_b)
    vg = v.reshape(B, nc, SGU_CHUNK, SGU_GROUPS, SGU_GROUP_DIM)
    causal = jnp.tril(jnp.ones((SGU_CHUNK, SGU_CHUNK), dtype=w_spatial.dtype))
    ws = w_spatial * causal[None]
    mixed = jnp.einsum('gts,bnsgd->bntgd', ws, vg) + b_spatial.T[None, None, :, :, None]
    y = u * mixed.reshape(B, S, SGU_WIDTH)
    return y @ w_out


def swiglu(h, w_gate_up, w_down):
    gu = h @ w_gate_up
    return (jax.nn.silu(gu[..., :D_FF]) * gu[..., D_FF:]) @ w_down


def setup_inputs(seed: int = 0) -> dict:
    key = jax.random.key(seed)
    ks = jax.random.split(key, 20)
    f32 = jnp.float32

    def nrm(k, shape, scale):
        return jax.random.normal(k, shape, f32) * scale

    def gain(k, shape):
        return 1.0 + 0.02 * jax.random.normal(k, shape, f32)

    NA, NS = N_ATTN_LAYERS, N_SGU_LAYERS
    return {
        "x": jax.random.normal(ks[0], (BATCH, SEQ, D_MODEL), f32),
        "norm_mix_pre": gain(ks[1], (DEPTH, D_MODEL)),
        "norm_mix_post": gain(ks[2], (DEPTH, D_MODEL)),
        "norm_ffn_pre": gain(ks[3], (DEPTH, D_MODEL)),
        "norm_ffn_post": gain(ks[4], (DEPTH, D_MODEL)),
        "attn_w_qkv": nrm(ks[5], (NA, D_MODEL, QKV_WIDTH), D_MODEL ** -0.5),
        "attn_b_qkv": nrm(ks[6], (NA, QKV_WIDTH), 0.02),
        "attn_sinks": nrm(ks[7], (NA, N_Q_HEADS), 1.0),
        "attn_w_o": nrm(ks[8], (NA, Q_WIDTH, D_MODEL), Q_WIDTH ** -0.5),
        "attn_b_o": nrm(ks[9], (NA, D_MODEL), 0.02),
        "sgu_w_in": nrm(ks[10], (NS, D_MODEL, 2 * SGU_WIDTH), D_MODEL ** -0.5),
        "sgu_ln_g": gain(ks[11], (NS, SGU_WIDTH)),
        "sgu_ln_b": nrm(ks[12], (NS, SGU_WIDTH), 0.02),
        "sgu_w_spatial": nrm(ks[13], (NS, SGU_GROUPS, SGU_CHUNK, SGU_CHUNK), SGU_CHUNK ** -0.5),
        "sgu_b_spatial": 1.0 + nrm(ks[14], (NS, SGU_GROUPS, SGU_CHUNK), 0.02),
        "sgu_w_out": nrm(ks[15], (NS, SGU_WIDTH, D_MODEL), SGU_WIDTH ** -0.5),
        "ffn_w_gate_up": nrm(ks[16], (DEPTH, D_MODEL, 2 * D_FF), D_MODEL ** -0.5),
        "ffn_w_down": nrm(ks[17], (DEPTH, D_FF, D_MODEL), D_FF ** -0.5),
    }


def reference(x, norm_mix_pre, norm_mix_post, norm_ffn_pre, norm_ffn_post,
              attn_w_qkv, attn_b_qkv, attn_sinks, attn_w_o, attn_b_o,
              sgu_w_in, sgu_ln_g, sgu_ln_b, sgu_w_spatial, sgu_b_spatial, sgu_w_out,
              ffn_w_gate_up, ffn_w_down):
    for i in range(DEPTH):
        h = rmsnorm(x, norm_mix_pre[i])
        j = i // N_MIXERS
        if i % N_MIXERS == 0:
            m = swa_sink_attention(h, attn_w_qkv[j], attn_b_qkv[j], attn_sinks[j],
                                   attn_w_o[j], attn_b_o[j])
        else:
            m = chunked_sgu(h, sgu_w_in[j], sgu_ln_g[j], sgu_ln_b[j],
                            sgu_w_spatial[j], sgu_b_spatial[j], sgu_w_out[j])
        x = x + rmsnorm(m, norm_mix_post[i])
        h = rmsnorm(x, norm_ffn_pre[i])
        x = x + rmsnorm(swiglu(h, ffn_w_gate_up[i], ffn_w_down[i]), norm_ffn_post[i])
    return x
```

```python
import numpy as np
import concourse.bass as bass
import concourse.mybir as mybir
from concourse.bass_utils import run_bass_kernel_spmd

F32 = mybir.dt.float32
BF16 = mybir.dt.bfloat16
AF = mybir.ActivationFunctionType
ALU = mybir.AluOpType

NT = 1024
NB = 8
R = 6
SLAB = 2048
NF = 22
EPS = 1e-6
NSLAB = 98
NROW = 256 + 16 + 3 * 1024

VC = {}
_c = 0
for _n, _w in [("gmp0", 8), ("gmo0", 8), ("gfp0", 8), ("gfo0", 8),
               ("gmp1", 8), ("gmo1", 8), ("gfp1", 8), ("gfo1", 8),
               ("bo", 8), ("bq", 8), ("bqs", 8), ("bk", 2), ("bks", 2), ("bog", 8)]:
    VC[_n] = _c
    _c += _w
NV = _c


class Sem:
    def __init__(self, h):
        self.h = h
        self.count = 0


class Op:
    __slots__ = ("eng", "fn", "deps", "tok", "used", "sem")


class Sched:
    def __init__(self):
        self.ops = {e: [] for e in ("pe", "act", "dve", "pool", "sp")}
        self.lastw = {}
        self.rd = {}
        self.pending = {}
        self.lastop = {}

    def add(self, eng, fn, reads=(), writes=(), dma_sem=None):
        op = Op()
        op.eng = eng
        op.fn = fn
        op.used = False
        op.sem = dma_sem
        op.tok = None
        deps = []
        for k in reads:
            w = self.lastw.get(k)
            if w is not None:
                deps.append(w)
        for k in writes:
            w = self.lastw.get(k)
            if w is not None:
                deps.append(w)
            r = self.rd.get(k)
            if r:
                deps.extend(r.values())
        p = self.pending.pop(eng, None)
        if p:
            deps.extend(p)
        op.deps = list(set(deps))
        rkey = eng if dma_sem is None else id(op)
        for k in reads:
            self.rd.setdefault(k, {})[rkey] = op
        for k in writes:
            self.lastw[k] = op
            self.rd[k] = {}
        if dma_sem is not None:
            dma_sem.count += 16
            op.tok = (dma_sem, dma_sem.count)
        self.ops[eng].append(op)
        self.lastop[eng] = op
        if eng == "pool" and dma_sem is None:
            self.lastop["poolc"] = op
        return op

    def fence(self):
        snap = [self.lastop[e] for e in ("pe", "act", "dve", "poolc") if e in self.lastop]
        self.last_fence = list(snap)
        for e in ("pe", "act", "dve", "pool"):
            self.pending[e] = list(snap)

    def fence_pe(self):
        snap = [self.lastop["pe"]]
        self.last_fence = list(snap)
        for e in ("act", "dve", "pool"):
            self.pending[e] = list(snap)

    def finalize(self, esems):
        for e, lst in self.ops.items():
            for op in lst:
                for d in op.deps:
                    if d.eng == "pe" and e == "pe" and d.sem is None:
                        continue
                    d.used = True
        for e, lst in self.ops.items():
            cnt = 0
            for op in lst:
                if op.sem is None and op.used:
                    cnt += 1
                    op.tok = (esems[e], cnt)

    def run(self, e, eng):
        waited = {}
        for op in self.ops[e]:
            need = {}
            for d in op.deps:
                if d.eng == "pe" and e == "pe" and d.sem is None:
                    continue
                s, v = d.tok
                if need.get(s, 0) < v:
                    need[s] = v
            for s, v in need.items():
                if waited.get(s, 0) < v:
                    eng.wait_ge(s.h, v)
                    waited[s] = v
            ins = op.fn(eng)
            if ins is None:
                continue
            if op.sem is not None:
                ins.then_inc(op.sem.h, 16)
            elif op.used:
                ins.then_inc(op.tok[0].h, 1)


class Rot:
    def __init__(self, items):
        self.items = list(items)
        self.i = 0

    def next(self):
        v = self.items[self.i % len(self.items)]
        self.i += 1
        return v


def build(n_st, stop_after=None):
    nc = bass.Bass("TRN2", target_bir_lowering=False)
    TOK = n_st * NT
    xT_d = nc.dram_tensor("xT", [8, 128, TOK], F32, kind="ExternalInput").ap()
    wst_d = nc.dram_tensor("wst", [NSLAB, 128, SLAB], F32, kind="ExternalInput").ap()
    vecs_d = nc.dram_tensor("vecs", [128, NV], F32, kind="ExternalInput").ap()
    rope_d = nc.dram_tensor("rope", [2, 128, TOK], F32, kind="ExternalInput").ap()
    rows_d = nc.dram_tensor("rows", [1, NROW], F32, kind="ExternalInput").ap()
    wsT_d = nc.dram_tensor("wsT", [128, 1024], F32, kind="ExternalInput").ap()
    cm_d = nc.dram_tensor("cmask", [128, 1024 + 128], F32, kind="ExternalInput").ap()
    yT_d = nc.dram_tensor("yT", [8, 128, TOK], F32, kind="ExternalOutput").ap()
    scr_d = nc.dram_tensor("wscr", [NSLAB, 128, SLAB], BF16, kind="Internal").ap()

    ARENA = 30720
    import contextlib
    with contextlib.ExitStack() as es:
        def sb(name, shape, dt):
            return es.enter_context(nc.sbuf_tensor(name, shape, dt))

        def sem(name):
            return Sem(es.enter_context(nc.semaphore(name)))

        xs = sb("xT_sb", [128, 4, 8, 512], F32)
        hm = sb("hm", [128, 8192], F32)
        sq = sb("sq", [128, 3, 512], BF16)
        rstd = sb("rstd", [128, 2, 512], F32)
        ring = sb("ring", [128, R, SLAB], BF16)
        ropeb = sb("ropeb", [128, 2, 2, 512], F32)
        vecs = sb("vecs_sb", [128, NV], F32)
        sinks = sb("sinks", [128, 16], F32)
        esink = sb("esink", [128, 16], F32)
        bvt = sb("bvt", [128, 256], F32)
        maskT = sb("maskT", [128, 2, 512], BF16)
        ones = sb("ones", [128, 128], BF16)
        sel = sb("sel", [1, 128], BF16)
        esrow = sb("esrow", [1, 2048], BF16)
        carryK = sb("carryK", [128, 2, 128], BF16)
        carryV = sb("carryV", [128, 4, 128], BF16)
        WsT = sb("WsT", [128, 8, 128], BF16)
        small = sb("small", [128, 64], F32)
        arena = sb("arena", [128, ARENA], BF16)
        ps = [es.enter_context(nc.psum_tensor("ps%d" % i, [128, 512], F32)) for i in range(8)]

        esems = {e: sem("s_" + e) for e in ("pe", "act", "dve", "pool", "sp")}
        ring_sems = [sem("ring%d" % i) for i in range(R)]
        scr_sems = [sem("scr%d" % i) for i in range(R)]
        xl_sems = [sem("xl%d" % i) for i in range(4)]
        st_sems = [sem("store%d" % i) for i in range(4)]
        rope_sems = [sem("rope%d" % i) for i in range(2)]
        c_sems = [sem("const%d" % i) for i in range(10)]
        sg_sems = [sem("sguc%d" % i) for i in range(3)]

        block = es.enter_context(nc.Block())
        S = Sched()

        hT = hm[:, 0:4096].bitcast(BF16).rearrange("p (c t) -> p c t", c=8)
        mT = hm[:].rearrange("p (g c t) -> p g c t", g=2, c=8)

        class Carver:
            def __init__(self):
                self.off = 0

            def take(self, nelem, dt):
                nb = nelem * (4 if dt == F32 else 2)
                a = self.off // 2
                self.off += nb
                assert self.off <= ARENA * 2, "arena overflow"
                v = arena[:, a:a + nb // 2]
                return v.bitcast(F32) if dt == F32 else v

        def mm(out, lhsT, rhs, start, stop, reads, writes):
            S.add("pe", lambda e, o=out, l=lhsT, r=rhs, a=start, b=stop: e.matmul(o, l, r, start=a, stop=b),
                  reads, writes)

        def act(out, in_, func, reads, writes, scale=None, bias=None):
            kw = {}
            if scale is not None:
                kw["scale"] = scale
            if bias is not None:
                kw["bias"] = bias
            S.add("act", lambda e, o=out, i=in_, f=func, kw=kw: e.activation(out=o, in_=i, func=f, **kw),
                  reads, writes)

        def tt(out, in0, in1, op, reads, writes, eng="dve"):
            S.add(eng, lambda e, o=out, a=in0, b=in1, p=op: e.tensor_tensor(out=o, in0=a, in1=b, op=p),
                  reads, writes)

        def stt(out, in0, scalar, in1, op0, op1, reads, writes):
            S.add("dve", lambda e, o=out, a=in0, s=scalar, b=in1, p0=op0, p1=op1:
                  e.scalar_tensor_tensor(out=o, in0=a, scalar=s, in1=b, op0=p0, op1=p1), reads, writes)

        def ts(out, in0, s1, s2, op0, op1, reads, writes):
            S.add("dve", lambda e, o=out, a=in0, x=s1, y=s2, p0=op0, p1=op1:
                  e.tensor_scalar(out=o, in0=a, scalar1=x, scalar2=y, op0=p0, op1=p1), reads, writes)

        def dma(q, out, in_, reads, writes, s, extra_deps=None):
            op = S.add(q, lambda e, o=out, i=in_: e.dma_start(out=o, in_=i), reads, writes, dma_sem=s)
            if extra_deps:
                op.deps = list(set(op.deps) | set(extra_deps))
            return op

        vcol = lambda name, c: vecs[:, VC[name] + c:VC[name] + c + 1]

        wstate = {"issued": 0, "used": 0}
        SEQ = (list(range(0, 10)) + list(range(10, 32)) + list(range(32, 48)) * 2 +
               list(range(48, 60)) + list(range(60, 82)) + list(range(82, 98)) * 2)
        NSEQ = len(SEQ)
        TOTAL_SLABS = NSEQ * n_st
        scr_stored = set()

        def slab_width(j):
            if (32 <= j < 48) or (82 <= j < 98):
                return 11 * 128
            return SLAB

        def w_issue():
            i = wstate["issued"]
            if i >= TOTAL_SLABS:
                return
            slot = i % R
            j = SEQ[i % NSEQ]
            wd = slab_width(j)
            if j not in scr_stored:
                dma("pool", ring[:, slot, 0:wd], wst_d[j, :, 0:wd], [], [("w", slot)], ring_sems[slot])
            else:
                dma("pool", ring[:, slot, 0:wd], scr_d[j, :, 0:wd], [("scr", j)], [("w", slot)], ring_sems[slot])
            wstate["issued"] += 1

        def w_get(count):
            n = wstate["used"]
            while wstate["issued"] < min(n + R, TOTAL_SLABS):
                w_issue()
            wstate["used"] += count
            for i in range(n, n + count):
                j = SEQ[i % NSEQ]
                if j not in scr_stored:
                    wd = slab_width(j)
                    dma("sp", scr_d[j, :, 0:wd], ring[:, i % R, 0:wd], [("w", i % R)], [("scr", j)],
                        scr_sems[i % R])
                    scr_stored.add(j)
            return [(ring[:, (n + i) % R, :], ("w", (n + i) % R)) for i in range(count)]

        def w_next():
            return w_get(1)[0]

        dma("sp", vecs[:, :], vecs_d[:, :], [], [("vecs",)], c_sems[0])
        dma("sp", bvt[:, :], rows_d[0:1, 0:256].partition_broadcast(128), [], [("bvt",)], c_sems[1])
        dma("sp", sinks[:, :], rows_d[0:1, 256:272].partition_broadcast(128), [], [("sinks",)], c_sems[2])
        _cv0 = Carver()
        wstage = _cv0.take(1024, F32)
        tril = _cv0.take(128, F32)
        dma("sp", wstage[:, :], wsT_d[:, :], [], [("wstage",)], c_sems[6])
        dma("sp", tril[:, :], cm_d[:, 1024:1152], [], [("tril",)], c_sems[7])
        dma("pool", maskT[:].rearrange("p a b -> p (a b)"), cm_d[:, 0:1024], [], [("maskT",)], c_sems[8])
        for _ in range(R):
            w_issue()
        S.add("dve", lambda e: e.memset(ones[:], 1.0), [], [("ones",)])
        act(esink[:, :], sinks[:, :], AF.Exp, [("sinks",)], [("esink",)])
        S.add("dve", lambda e: e.memset(sel[0:1, 0:64], 0.0), [], [("sel",)])
        S.add("dve", lambda e: e.memset(sel[0:1, 64:128], 1.0), [], [("sel",)])
        for h in range(16):
            ts(esrow[0:1, h * 128:(h + 1) * 128], ones[0:1, 0:128], esink[0:1, h:h + 1], None, ALU.mult, ALU.bypass,
               [("ones",), ("esink",)], [("esrow",)])
        for g in range(8):
            tt(WsT[:, g, :], wstage[:, g * 128:(g + 1) * 128], tril[:, :], ALU.mult,
               [("wstage",), ("tril",)], [("WsT",)])
        tt(vecs[:, VC["bog"]:VC["bog"] + 8], vecs[:, VC["bo"]:VC["bo"] + 8], vecs[:, VC["gmo0"]:VC["gmo0"] + 8],
           ALU.mult, [("vecs",)], [("vecs",)])
        S.fence()

        sq_rot = Rot([0, 1, 2])
        cur = {"st": 0}

        def xslot(grp):
            return (2 * cur["st"] + grp) % 4

        def X(grp, c):
            return xs[:, xslot(grp), c, :]

        def xkey(grp, c):
            return ("x", xslot(grp), c)

        pre_done = {"v": False}
        deferred = []

        def run_deferred():
            while deferred:
                deferred.pop(0)()

        def prenorm_group(gname, grp, st=None, bank=None):
            if st is None:
                st = cur["st"]
            if bank is None:
                bank = 6 + grp
            slot = (2 * st + grp) % 4
            cs = slice(grp * 512, (grp + 1) * 512)
            for c in range(8):
                r = sq_rot.next()
                act(sq[:, r, :], xs[:, slot, c, :], AF.Square, [("x", slot, c)], [("sq", r)])
                mm(ps[bank][:, :], ones[:, :], sq[:, r, :], c == 0, c == 7,
                   [("ones",), ("sq", r)], [("ps", bank)])
            act(rstd[:, grp, :], ps[bank][:, :], AF.Ln, [("ps", bank)], [("rstd", grp)],
                scale=1.0 / 1024.0, bias=EPS)
            act(rstd[:, grp, :], rstd[:, grp, :], AF.Exp, [("rstd", grp)], [("rstd", grp)], scale=-0.5)
            for c in range(8):
                stt(hT[:, c, cs], xs[:, slot, c, :], vcol(gname, c), rstd[:, grp, :], ALU.mult, ALU.mult,
                    [("x", slot, c), ("rstd", grp), ("vecs",)], [("h", grp, c), ("m", 0, c)])

        def prenorm(gname):
            if pre_done["v"]:
                pre_done["v"] = False
                return
            for grp in range(2):
                prenorm_group(gname, grp)

        class PostNorm:
            def __init__(self, gname, bias_name=None, bg_name=None):
                self.gname = gname
                self.bias_name = bias_name
                self.bg_name = bg_name
                self.pending = None
                self.count = [0, 0]

            def tile(self, bank, grp, c):
                r = sq_rot.next()
                b_sq = vcol(self.bias_name, c) if self.bias_name else None
                b_mg = vcol(self.bg_name, c) if self.bg_name else None
                act(sq[:, r, :], ps[bank][:, :], AF.Square, [("ps", bank), ("vecs",)], [("sq", r)], bias=b_sq)
                mkeys = [("m", grp, c)] + ([("h", 0, c), ("h", 1, c)] if grp == 0 else [])
                act(mT[:, grp, c, :], ps[bank][:, :], AF.Identity, [("ps", bank), ("vecs",)], mkeys,
                    scale=vcol(self.gname, c), bias=b_mg)
                self.flush()
                self.pending = (r, grp)

            def flush(self):
                if self.pending is None:
                    return
                r, grp = self.pending
                n = self.count[grp]
                mm(ps[6 + grp][:, :], ones[:, :], sq[:, r, :], n == 0, n == 7,
                   [("ones",), ("sq", r)], [("ps", 6 + grp)])
                self.count[grp] += 1
                self.pending = None

            def post_group(self, grp, final_store=None):
                act(rstd[:, grp, :], ps[6 + grp][:, :], AF.Ln, [("ps", 6 + grp)], [("rstd", grp)],
                    scale=1.0 / 1024.0, bias=EPS)
                act(rstd[:, grp, :], rstd[:, grp, :], AF.Exp, [("rstd", grp)], [("rstd", grp)], scale=-0.5)
                for c in range(8):
                    tt(mT[:, grp, c, :], mT[:, grp, c, :], rstd[:, grp, :], ALU.mult,
                       [("m", grp, c), ("rstd", grp)], [("m", grp, c)])
                    tt(X(grp, c), X(grp, c), mT[:, grp, c, :], ALU.add,
                       [xkey(grp, c), ("m", grp, c)], [xkey(grp, c)])
                if final_store is not None:
                    final_store(grp)

        def out_phase(pn, tile_fn, nxt, final_store=None, c_pre=4):
            for c in range(8):
                pn.tile(tile_fn(0, c), 0, c)
            pn.tile(tile_fn(1, 0), 1, 0)
            pn.post_group(0, final_store)
            if nxt is not None and nxt[1] != cur["st"]:
                pn.tile(tile_fn(1, 1), 1, 1)
                pn.flush()
                prenorm_group(nxt[0], 0, nxt[1])
                pn.tile(tile_fn(1, 2), 1, 2)
                pn.flush()
                prenorm_group(nxt[0], 1, nxt[1], bank=6)
                for c in range(3, 8):
                    pn.tile(tile_fn(1, c), 1, c)
                pn.flush()
                pn.post_group(1, final_store)
                pre_done["v"] = True
                return
            for c in range(1, 8):
                pn.tile(tile_fn(1, c), 1, c)
                if c == c_pre and nxt is not None:
                    pn.flush()
                    prenorm_group(nxt[0], 0, nxt[1])
            pn.flush()
            pn.post_group(1, final_store)
            if nxt is not None:
                deferred.append(lambda: prenorm_group(nxt[0], 1, nxt[1]))
                pre_done["v"] = True

        def attention(st):
            cv = Carver()
            qoT = cv.take(8 * NT, BF16).rearrange("p (s t) -> p s t", s=8)
            kT = cv.take(2 * NT, BF16).rearrange("p (s t) -> p s t", s=2)
            Vx = cv.take(NB * 4 * 128, BF16).rearrange("p (b g d) -> p b g d", b=NB, g=4)
            eT = cv.take(3 * 2 * 512, BF16).rearrange("p (u a t) -> p u a t", u=3, a=2)
            PT = cv.take(3 * 2 * 512, BF16).rearrange("p (u a t) -> p u a t", u=3, a=2)
            qs = cv.take(2 * 512, F32).rearrange("p (u t) -> p u t", u=2)
            t1 = cv.take(2 * 512, F32).rearrange("p (u t) -> p u t", u=2)
            t2 = cv.take(2 * 512, F32).rearrange("p (u t) -> p u t", u=2)
            lnd = cv.take(2 * 512, F32).rearrange("p (u t) -> p u t", u=2)
            Rr = cv.take(2 * 512, F32).rearrange("p (u t) -> p u t", u=2)

            prenorm("gmp0")
            S.add("dve", lambda e: e.memset(Vx[:, :, :, 64:128], 1.0), [], [("v", b) for b in range(NB)])
            if st == 0:
                S.add("dve", lambda e: e.memset(carryV[:, :, :], 0.0), [], [("vc",)])
                S.add("dve", lambda e: e.memset(carryK[:, :, :], 0.0), [], [("kc",)])

            bank_rot = Rot([0, 1, 2, 3])
            tmp_rot = Rot([0, 1])

            def qk_tile(wv, wk, ti, grp, bias_col, bias_sw_col, out_ap, out_keys):
                cs = slice(grp * 512, (grp + 1) * 512)
                bank = bank_rot.next()
                for k in range(8):
                    mm(ps[bank][:, :], wv[:, (ti * 8 + k) * 128:(ti * 8 + k + 1) * 128], hT[:, k, cs],
                       k == 0, k == 7, [wk, ("h", grp, k)], [("ps", bank)])
                u = tmp_rot.next()
                for a in range(4):
                    b = a ^ 1
                    act(qs[32 * a:32 * a + 32, u, :], ps[bank][32 * b:32 * b + 32, :], AF.Copy,
                        [("ps", bank)], [("qs", u, a)])
                stt(t1[:, u, :], ps[bank][:, :], bias_col, ropeb[:, grp, 0, :], ALU.add, ALU.mult,
                    [("ps", bank), ("rope", grp), ("vecs",)] + [("qs", u, a) for a in range(4)], [("t1", u)])
                stt(t2[:, u, :], qs[:, u, :], bias_sw_col, ropeb[:, grp, 1, :], ALU.add, ALU.mult,
                    [("qs", u, a) for a in range(4)] + [("rope", grp), ("vecs",)], [("t2", u)])
                tt(out_ap, t1[:, u, :], t2[:, u, :], ALU.add, [("t1", u), ("t2", u)], out_keys)

            def q_tile(wv, wk, j, ti, grp):
                s = 2 * j + ti
                cs = slice(grp * 512, (grp + 1) * 512)
                keys = [("q", s, b) for b in range(grp * 4, grp * 4 + 4)]
                qk_tile(wv, wk, ti, grp, vcol("bq", s), vcol("bqs", s), qoT[:, s, cs], keys)

            first = w_get(4)
            for grp in range(2):
                for j in range(4):
                    for ti in range(2):
                        q_tile(first[j][0], first[j][1], j, ti, grp)
                    if grp == 0 and j == 0:
                        run_deferred()
            wv, wk = w_next()
            for ti in range(2):
                for grp in range(2):
                    cs = slice(grp * 512, (grp + 1) * 512)
                    keys = [("k", ti, b) for b in range(grp * 4, grp * 4 + 4)]
                    qk_tile(wv, wk, ti, grp, vcol("bk", ti), vcol("bks", ti), kT[:, ti, cs], keys)
            wv, wk = w_next()
            for b0 in range(0, NB, 4):
                vb_banks = [4, 5, 6, 7]
                grp = b0 // 4
                for k in range(8):
                    for bb in range(4):
                        b = b0 + bb
                        mm(ps[vb_banks[bb]][:, 0:256], hT[:, k, b * 128:(b + 1) * 128], wv[:, k * 256:(k + 1) * 256],
                           k == 0, k == 7, [wk, ("h", grp, k)], [("ps", vb_banks[bb])])
                for bb in range(4):
                    b = b0 + bb
                    tt(Vx[:, b, :, 0:64], ps[vb_banks[bb]][:, 0:256].rearrange("p (g d) -> p g d", g=4),
                       bvt[:, :].rearrange("p (g d) -> p g d", g=4), ALU.add,
                       [("ps", vb_banks[bb]), ("bvt",)], [("v", b)])

            sc_rot = Rot([(0, 1), (2, 3)])
            pv_rot = Rot([4, 5])
            iters = [(b, g) for b in range(NB) for g in range(4)]

            def stage1(i):
                b, g = iters[i]
                has_prev = (st * NB + b) > 0
                half = g % 2
                hs = slice(half * 64, half * 64 + 64)
                s0 = (g // 2) * 4
                kt = g // 2
                bs = slice(b * 128, (b + 1) * 128)
                q_rhs = qoT[hs, s0:s0 + 4, bs]
                q_keys = [("q", s0 + jj, b) for jj in range(4)]
                bp, bc = sc_rot.next()
                u = i % 3
                if has_prev:
                    if b > 0:
                        kprev, kpk = kT[hs, kt, (b - 1) * 128:b * 128], ("k", kt, b - 1)
                    else:
                        kprev, kpk = carryK[hs, kt, :], ("kc",)
                    mm(ps[bp][:, :], kprev, q_rhs, True, True, [kpk] + q_keys, [("ps", bp)])
                mm(ps[bc][:, :], kT[hs, kt, bs], q_rhs, True, True, [("k", kt, b)] + q_keys, [("ps", bc)])
                if has_prev:
                    act(eT[:, u, 0, :], ps[bp][:, :], AF.Exp, [("ps", bp)], [("e", u, 0)], scale=0.125)
                act(eT[:, u, 1, :], ps[bc][:, :], AF.Exp, [("ps", bc)], [("e", u, 1)], scale=0.125)
                if has_prev:
                    tt(PT[:, u, :, :], eT[:, u, :, :], maskT[:, :, :], ALU.mult,
                       [("e", u, 0), ("e", u, 1), ("maskT",)], [("P", u)])
                else:
                    tt(PT[:, u, 1, :], eT[:, u, 1, :], maskT[:, 1, :], ALU.mult,
                       [("e", u, 1), ("maskT",)], [("P", u)])

            def stage2(i):
                b, g = iters[i]
                has_prev = (st * NB + b) > 0
                half = g % 2
                hs = slice(half * 64, half * 64 + 64)
                s0 = (g // 2) * 4
                bs = slice(b * 128, (b + 1) * 128)
                q_keys = [("q", s0 + jj, b) for jj in range(4)]
                u = i % 3
                w = i % 2
                pvb = pv_rot.next()
                if has_prev:
                    if b > 0:
                        vprev, vpk = Vx[:, b - 1, g, :], ("v", b - 1)
                    else:
                        vprev, vpk = carryV[:, g, :], ("vc",)
                    mm(ps[pvb][:, :], vprev, PT[:, u, 0, :], True, False, [vpk, ("P", u)], [("ps", pvb)])
                mm(ps[pvb][:, :], Vx[:, b, g, :], PT[:, u, 1, :], not has_prev, False,
                   [("v", b), ("P", u)], [("ps", pvb)])
                mm(ps[pvb][:, :], sel[0:1, :], esrow[0:1, g * 512:(g + 1) * 512], False, True,
                   [("sel",), ("esrow",)], [("ps", pvb)])
                act(lnd[64:128, w, :], ps[pvb][64:128, :], AF.Ln, [("ps", pvb)], [("lnd", w)])
                act(Rr[0:64, w, :], lnd[64:128, w, :], AF.Exp, [("lnd", w)], [("R", w)], scale=-1.0)
                tt(qoT[hs, s0:s0 + 4, bs], ps[pvb][0:64, :].rearrange("p (j t) -> p j t", j=4),
                   Rr[0:64, w, :].rearrange("p (j t) -> p j t", j=4), ALU.mult,
                   [("ps", pvb), ("R", w)], q_keys)

            for i in range(len(iters)):
                stage1(i)
                if i >= 1:
                    stage2(i - 1)
            stage2(len(iters) - 1)
            S.add("dve", lambda e: e.tensor_copy(out=carryK[:, :, :], in_=kT[:, :, (NB - 1) * 128:NB * 128]),
                  [("k", 0, NB - 1), ("k", 1, NB - 1)], [("kc",)])
            S.add("dve", lambda e: e.tensor_copy(out=carryV[:, :, :], in_=Vx[:, NB - 1, :, :]),
                  [("v", NB - 1)], [("vc",)])

            pn = PostNorm("gmo0", "bo", "bog")
            orot = Rot([0, 1, 2, 3])
            wo_slabs = w_get(4)

            def wo_tile(grp, c):
                wv, wk = wo_slabs[c // 2]
                mi = c % 2
                cs = slice(grp * 512, (grp + 1) * 512)
                bank = orot.next()
                for k in range(8):
                    mm(ps[bank][:, :], wv[:, (mi * 8 + k) * 128:(mi * 8 + k + 1) * 128], qoT[:, k, cs],
                       k == 0, k == 7, [wk] + [("q", k, b) for b in range(grp * 4, grp * 4 + 4)],
                       [("ps", bank)])
                return bank

            out_phase(pn, wo_tile, ("gfp0", st))
            S.fence_pe()

        def ffn(st, layer, final_store=None):
            cv = Carver()
            gT = cv.take(NF * NT, BF16).rearrange("p (f t) -> p f t", f=NF)
            sg = cv.take(2 * 512, F32).rearrange("p (u t) -> p u t", u=2)
            prenorm("gfp%d" % layer)
            grot = Rot([(0, 1), (2, 3)])
            urot = Rot([0, 1])
            def gu_tile(wv, wk, f, grp):
                cs = slice(grp * 512, (grp + 1) * 512)
                bg_, bu_ = grot.next()
                u = urot.next()
                for k in range(8):
                    mm(ps[bg_][:, :], wv[:, k * 128:(k + 1) * 128], hT[:, k, cs], k == 0, k == 7,
                       [wk, ("h", grp, k)], [("ps", bg_)])
                for k in range(8):
                    mm(ps[bu_][:, :], wv[:, (8 + k) * 128:(9 + k) * 128], hT[:, k, cs], k == 0, k == 7,
                       [wk, ("h", grp, k)], [("ps", bu_)])
                act(sg[:, u, :], ps[bg_][:, :], AF.Silu, [("ps", bg_)], [("sg", u)])
                tt(gT[:, f, cs], ps[bu_][:, :], sg[:, u, :], ALU.mult, [("ps", bu_), ("sg", u)],
                   [("g", f, grp)])

            NSK = 4
            first = w_get(NSK)
            for grp in range(2):
                for f in range(NSK):
                    gu_tile(first[f][0], first[f][1], f, grp)
                    if grp == 0 and f == 1:
                        run_deferred()
            for f in range(NSK, NF):
                wv, wk = w_next()
                for grp in range(2):
                    gu_tile(wv, wk, f, grp)
            pn = PostNorm("gfo%d" % layer)
            orot = Rot([0, 1, 2, 3])

            def dn_tile(grp, c):
                (wv0, wk0), (wv1, wk1) = w_get(2)
                cs = slice(grp * 512, (grp + 1) * 512)
                bank = orot.next()
                for f in range(NF):
                    wv, wk = (wv0, wk0) if f < 11 else (wv1, wk1)
                    kk = f % 11
                    mm(ps[bank][:, :], wv[:, kk * 128:(kk + 1) * 128], gT[:, f, cs], f == 0, f == NF - 1,
                       [wk, ("g", f, grp)], [("ps", bank)])
                return bank

            if layer == 0:
                nxt = ("gmp1", st)
            else:
                nxt = ("gmp0", st + 1) if st + 1 < n_st else None
            out_phase(pn, dn_tile, nxt, final_store, c_pre=3)
            S.fence_pe()

        def sgu(st):
            cv = Carver()
            uT = cv.take(8 * NT, BF16).rearrange("p (c t) -> p c t", c=8)
            vf = cv.take(3 * 1024, F32).rearrange("p (u t) -> p u t", u=3)
            vb = cv.take(3 * 1024, BF16).rearrange("p (u t) -> p u t", u=3)
            tmp = cv.take(2 * 512, F32).rearrange("p (u t) -> p u t", u=2)
            lng = cv.take(1024, F32)
            lnb = cv.take(1024, F32)
            bsp = cv.take(1024, F32)
            fdeps = list(S.last_fence)
            dma("sp", lng[:, :], rows_d[0:1, 272:1296].partition_broadcast(128), [], [("lng",)], sg_sems[0], fdeps)
            dma("sp", lnb[:, :], rows_d[0:1, 1296:2320].partition_broadcast(128), [], [("lnb",)], sg_sems[1], fdeps)
            dma("sp", bsp[:, :], rows_d[0:1, 2320:3344].partition_broadcast(128), [], [("bsp",)], sg_sems[2], fdeps)
            prenorm("gmp1")
            brot = Rot([0, 1, 2, 3])

            def u_tile(wv, wk, i, mi, grp):
                c = 2 * i + mi
                cs = slice(grp * 512, (grp + 1) * 512)
                bank = brot.next()
                for k in range(8):
                    mm(ps[bank][:, :], wv[:, (mi * 8 + k) * 128:(mi * 8 + k + 1) * 128], hT[:, k, cs],
                       k == 0, k == 7, [wk, ("h", grp, k)], [("ps", bank)])
                act(uT[:, c, cs], ps[bank][:, :], AF.Gelu_apprx_tanh, [("ps", bank)],
                    [("u", c, b) for b in range(grp * 4, grp * 4 + 4)])

            first = w_get(4)
            for grp in range(2):
                for i in range(4):
                    for mi in range(2):
                        u_tile(first[i][0], first[i][1], i, mi, grp)
                    if grp == 0 and i == 0:
                        run_deferred()
            vslabs = w_get(4)
            mrot = Rot([(4, 5), (6, 7)])
            trot = Rot([0, 1])

            def stageA(b):
                grp = b // 4
                bs = slice(b * 128, (b + 1) * 128)
                u = b % 3
                for vh in range(2):
                    bank = brot.next()
                    for k in range(8):
                        wv, wk = vslabs[vh * 2 + k // 4]
                        kk = k % 4
                        mm(ps[bank][:, :], hT[:, k, bs], wv[:, kk * 512:(kk + 1) * 512], k == 0, k == 7,
                           [wk, ("h", grp, k)], [("ps", bank)])
                    act(vf[:, u, vh * 512:(vh + 1) * 512], ps[bank][:, :], AF.Gelu_apprx_tanh,
                        [("ps", bank)], [("vf", u, vh)])
                o = 16 * u
                S.add("dve", lambda e, o=o, u=u: e.bn_stats(out=small[:, o:o + 6], in_=vf[:, u, 0:512]),
                      [("vf", u, 0)], [("bn", u, 0)])
                S.add("dve", lambda e, o=o, u=u: e.bn_stats(out=small[:, o + 6:o + 12], in_=vf[:, u, 512:1024]),
                      [("vf", u, 1)], [("bn", u, 1)])
                S.add("dve", lambda e, o=o: e.bn_aggr(out=small[:, o + 12:o + 14], in_=small[:, o:o + 12]),
                      [("bn", u, 0), ("bn", u, 1)], [("mv", u)])
                o2 = 48 + 4 * u
                act(small[:, o2:o2 + 1], small[:, o + 13:o + 14], AF.Ln, [("mv", u)], [("lnv", u)], bias=EPS)
                act(small[:, o2 + 1:o2 + 2], small[:, o2:o2 + 1], AF.Exp, [("lnv", u)], [("rs", u)], scale=-0.5)
                ts(small[:, o2 + 2:o2 + 3], small[:, o + 12:o + 13], -1.0, small[:, o2 + 1:o2 + 2],
                   ALU.mult, ALU.mult, [("mv", u), ("rs", u)], [("nmr", u)])
                vkeys = [("vf", u, 0), ("vf", u, 1)]
                act(vf[:, u, :], vf[:, u, :], AF.Identity, vkeys + [("rs", u), ("nmr", u)], vkeys,
                    scale=small[:, o2 + 1:o2 + 2], bias=small[:, o2 + 2:o2 + 3])
                tt(vf[:, u, :], vf[:, u, :], lng[:, :], ALU.mult, vkeys + [("lng",)], vkeys, eng="pool")
                tt(vb[:, u, :], vf[:, u, :], lnb[:, :], ALU.add, vkeys + [("lnb",)], [("vb", u)], eng="pool")

            def stageB(b):
                bs = slice(b * 128, (b + 1) * 128)
                u = b % 3
                banks = mrot.next()
                for gg in range(8):
                    bank = banks[gg // 4]
                    col = (gg % 4) * 128
                    mm(ps[bank][:, col:col + 128], vb[:, u, gg * 128:(gg + 1) * 128], WsT[:, gg, :], True, True,
                       [("vb", u), ("WsT",)], [("ps", bank)])
                for hb in range(2):
                    bank = banks[hb]
                    tu = trot.next()
                    tt(tmp[:, tu, :], ps[bank][:, :], bsp[:, hb * 512:(hb + 1) * 512], ALU.add,
                       [("ps", bank), ("bsp",)], [("tmp", tu)])
                    ukeys = [("u", 4 * hb + jj, b) for jj in range(4)]
                    tt(uT[:, 4 * hb:4 * hb + 4, bs], uT[:, 4 * hb:4 * hb + 4, bs],
                       tmp[:, tu, :].rearrange("p (j t) -> p j t", j=4), ALU.mult,
                       [("tmp", tu)] + ukeys, ukeys)

            for b in range(NB):
                stageA(b)
                if b >= 2:
                    stageB(b - 2)
            stageB(NB - 2)
            stageB(NB - 1)
            pn = PostNorm("gmo1")
            orot = Rot([0, 1, 2, 3])
            wo_slabs = w_get(4)

            def so_tile(grp, c):
                wv, wk = wo_slabs[c // 2]
                mi = c % 2
                cs = slice(grp * 512, (grp + 1) * 512)
                bank = orot.next()
                for k in range(8):
                    mm(ps[bank][:, :], wv[:, (mi * 8 + k) * 128:(mi * 8 + k + 1) * 128], uT[:, k, cs],
                       k == 0, k == 7, [wk] + [("u", k, b) for b in range(grp * 4, grp * 4 + 4)],
                       [("ps", bank)])
                return bank

            out_phase(pn, so_tile, ("gfp1", st))
            S.fence_pe()

        store_ops = []

        def x_load(st, grp):
            slot = (2 * st + grp) % 4
            t0 = st * NT + grp * 512
            dma("sp", xs[:, slot, :, :], xT_d[:, :, t0:t0 + 512].rearrange("c p t -> p c t"),
                [], [("x", slot, c) for c in range(8)], xl_sems[slot])

        def rope_load(st):
            for grp in range(2):
                t0 = st * NT + grp * 512
                dma("sp", ropeb[:, grp, :, :], rope_d[:, :, t0:t0 + 512].rearrange("a p t -> p a t"),
                    [], [("rope", grp)], rope_sems[grp])

        x_load(0, 0)
        x_load(0, 1)
        rope_load(0)
        for st in range(n_st):
            cur["st"] = st

            def final_store(grp, st=st):
                slot = (2 * st + grp) % 4
                t0 = st * NT + grp * 512
                dma("sp", yT_d[:, :, t0:t0 + 512].rearrange("c p t -> p c t"), xs[:, slot, :, :],
                    [("x", slot, c) for c in range(8)], [], st_sems[slot])
                store_ops.append(S.lastop["sp"])

            def prefetch_next():
                if st + 1 < n_st:
                    x_load(st + 1, 0)
                    x_load(st + 1, 1)
                    rope_load(st + 1)

            def last_phase():
                prefetch_next()
                ffn(st, 1, final_store)

            phases = [lambda: attention(st), lambda: ffn(st, 0), lambda: sgu(st), last_phase]
            nph = 4 if stop_after is None else stop_after
            for pi in range(nph):
                phases[pi]()
            if stop_after is not None and stop_after < 4:
                pre_done["v"] = False
                deferred.clear()
                while wstate["used"] < NSEQ * (st + 1):
                    w_next()
                prefetch_next()
                for grp in range(2):
                    final_store(grp)
        fin = S.add("sp", lambda e: None, [], [])
        fin.deps = list(store_ops)

        S.finalize(esems)

        @block.tensor
        def _(e):
            S.run("pe", e)

        @block.scalar
        def _(e):
            S.run("act", e)

        @block.vector
        def _(e):
            S.run("dve", e)

        @block.gpsimd
        def _(e):
            S.run("pool", e)

        @block.sync
        def _(e):
            S.run("sp", e)
    return nc


def _qcols(s):
    hA = 8 * (s // 4) + (s % 4)
    return np.concatenate([np.arange(hA * 64, hA * 64 + 64), np.arange((hA + 4) * 64, (hA + 4) * 64 + 64)])


def _tileB(W):
    nk = W.shape[0] // 128
    return W.reshape(nk, 128, W.shape[1]).transpose(1, 0, 2)


def _prep(inp):
    f = np.float32
    wqkv = np.asarray(inp["attn_w_qkv"], f)[0]
    bqkv = np.asarray(inp["attn_b_qkv"], f)[0]
    wo = np.asarray(inp["attn_w_o"], f)[0]
    w_in = np.asarray(inp["sgu_w_in"], f)[0]
    w_out = np.asarray(inp["sgu_w_out"], f)[0]
    wgu = np.asarray(inp["ffn_w_gate_up"], f)
    wdn = np.asarray(inp["ffn_w_down"], f)
    slabs = np.zeros((NSLAB, 128, SLAB), f)
    idx = 0

    def put(i, arr):
        a = np.ascontiguousarray(arr).reshape(128, -1)
        slabs[i, :, :a.shape[1]] = a

    def ffn_slabs(layer, idx):
        for fi in range(NF):
            g = _tileB(wgu[layer][:, fi * 128:(fi + 1) * 128])
            u = _tileB(wgu[layer][:, 2816 + fi * 128:2816 + (fi + 1) * 128])
            put(idx, np.stack([g, u], axis=1))
            idx += 1
        for c in range(8):
            t = _tileB(wdn[layer][:, c * 128:(c + 1) * 128])
            put(idx, t[:, 0:11])
            idx += 1
            put(idx, t[:, 11:22])
            idx += 1
        return idx

    for j in range(4):
        tiles = [_tileB(wqkv[:, _qcols(2 * j + ti)]) for ti in range(2)]
        put(idx, np.stack(tiles, axis=1))
        idx += 1
    tiles = [_tileB(wqkv[:, 1024 + t * 128:1024 + (t + 1) * 128]) for t in range(2)]
    put(idx, np.stack(tiles, axis=1))
    idx += 1
    put(idx, wqkv[:, 1280:1536].reshape(8, 128, 256).transpose(1, 0, 2))
    idx += 1
    rowperm = np.concatenate([_qcols(k) for k in range(8)])
    wo_p = wo[rowperm, :]
    for i in range(4):
        tiles = [_tileB(wo_p[:, (2 * i + mi) * 128:(2 * i + mi + 1) * 128]) for mi in range(2)]
        put(idx, np.stack(tiles, axis=1))
        idx += 1
    idx = ffn_slabs(0, idx)
    for i in range(4):
        tiles = [_tileB(w_in[:, (2 * i + mi) * 128:(2 * i + mi + 1) * 128]) for mi in range(2)]
        put(idx, np.stack(tiles, axis=1))
        idx += 1
    for vh in range(2):
        for kh in range(2):
            blk = w_in[kh * 512:(kh + 1) * 512, 1024 + vh * 512:1024 + (vh + 1) * 512]
            put(idx, blk.reshape(4, 128, 512).transpose(1, 0, 2))
            idx += 1
    for i in range(4):
        tiles = [_tileB(w_out[:, (2 * i + mi) * 128:(2 * i + mi + 1) * 128]) for mi in range(2)]
        put(idx, np.stack(tiles, axis=1))
        idx += 1
    idx = ffn_slabs(1, idx)
    assert idx == NSLAB

    vecs = np.zeros((128, NV), f)

    def putv(name, vec, n):
        vecs[:, VC[name]:VC[name] + n] = np.asarray(vec, f).reshape(n, 128).T

    for i in range(2):
        putv("gmp%d" % i, inp["norm_mix_pre"][i], 8)
        putv("gmo%d" % i, inp["norm_mix_post"][i], 8)
        putv("gfp%d" % i, inp["norm_ffn_pre"][i], 8)
        putv("gfo%d" % i, inp["norm_ffn_post"][i], 8)
    putv("bo", inp["attn_b_o"][0], 8)
    swp = np.arange(128) ^ 32
    for s in range(8):
        cols = _qcols(s)
        vecs[:, VC["bq"] + s] = bqkv[cols]
        vecs[:, VC["bqs"] + s] = bqkv[cols[swp]]
    for t in range(2):
        cols = 1024 + t * 128 + np.arange(128)
        vecs[:, VC["bk"] + t] = bqkv[cols]
        vecs[:, VC["bks"] + t] = bqkv[cols[swp]]

    rows = np.zeros((1, NROW), f)
    rows[0, 0:256] = bqkv[1280:1536]
    rows[0, 256:272] = np.asarray(inp["attn_sinks"], f)[0]
    rows[0, 272:1296] = np.asarray(inp["sgu_ln_g"], f)[0]
    rows[0, 1296:2320] = np.asarray(inp["sgu_ln_b"], f)[0]
    rows[0, 2320:3344] = np.asarray(inp["sgu_b_spatial"], f)[0].reshape(-1)
    wsp = np.asarray(inp["sgu_w_spatial"], f)[0]
    wsT = np.ascontiguousarray(wsp.transpose(2, 0, 1)).reshape(128, 1024)

    half = 32
    inv_freq = 10000.0 ** (-(np.arange(half, dtype=np.float64) * 2.0) / 64.0)
    pos = np.arange(4096, dtype=np.float64)
    ang = pos[None, :] * inv_freq[:, None]
    cos = np.cos(ang).astype(f)
    sin = np.sin(ang).astype(f)
    p = np.arange(128)
    rope = np.zeros((2, 128, 4096), f)
    rope[0] = cos[p % 32]
    sgn = np.where((p % 64) < 32, -1.0, 1.0).astype(f)
    rope[1] = sin[p % 32] * sgn[:, None]
    cm = np.zeros((128, 1024 + 128), f)
    j = np.arange(128)[:, None]
    i = np.arange(128)[None, :]
    prev = (j > i).astype(f)
    cur = (j <= i).astype(f)
    cm[:, 0:512] = np.tile(prev, (1, 4))
    cm[:, 512:1024] = np.tile(cur, (1, 4))
    cm[:, 1024:1152] = cur
    return dict(wst=slabs, vecs=vecs, rows=rows, wsT=wsT, rope=rope, cmask=cm)


_CACHE = {}


def _run(inputs, n_st, stop_after=None, n_cores=8):
    x = np.asarray(inputs["x"], np.float32)
    shared = _prep(inputs)
    TOK = n_st * NT
    key = (n_st, stop_after)
    if key not in _CACHE:
        _CACHE[key] = build(n_st, stop_after)
    nc = _CACHE[key]
    in_maps = []
    for b in range(n_cores):
        xT = np.ascontiguousarray(x[b, :TOK, :].T).reshape(8, 128, TOK)
        m = dict(shared)
        m["rope"] = np.ascontiguousarray(shared["rope"][:, :, :TOK])
        m["xT"] = xT
        in_maps.append(m)
    res = run_bass_kernel_spmd(nc, in_maps, core_ids=list(range(n_cores)))
    out = np.empty((n_cores, TOK, 1024), np.float32)
    for b in range(n_cores):
        out[b] = res.results[b]["yT"].reshape(1024, TOK).T
    return out


def kernel(**inputs):
    return _run(inputs, 4)
```

```python
import numpy as np
import concourse.bass as bass
import concourse.mybir as mybir
from concourse.bass_utils import run_bass_kernel_spmd

F32 = mybir.dt.float32
BF16 = mybir.dt.bfloat16
AF = mybir.ActivationFunctionType
ALU = mybir.AluOpType

NT = 1024
NB = 8
R = 6
SLAB = 2048
NF = 22
EPS = 1e-6
NSLAB = 98
NROW = 256 + 16 + 3 * 1024

VC = {}
_c = 0
for _n, _w in [("gmp0", 8), ("gmo0", 8), ("gfp0", 8), ("gfo0", 8),
               ("gmp1", 8), ("gmo1", 8), ("gfp1", 8), ("gfo1", 8),
               ("bo", 8), ("bq", 8), ("bqs", 8), ("bk", 2), ("bks", 2), ("bog", 8)]:
    VC[_n] = _c
    _c += _w
NV = _c


class Sem:
    def __init__(self, h):
        self.h = h
        self.count = 0


class Op:
    __slots__ = ("eng", "fn", "deps", "tok", "used", "sem")


class Sched:
    def __init__(self):
        self.ops = {e: [] for e in ("pe", "act", "dve", "pool", "sp")}
        self.lastw = {}
        self.rd = {}
        self.pending = {}
        self.lastop = {}

    def add(self, eng, fn, reads=(), writes=(), dma_sem=None):
        op = Op()
        op.eng = eng
        op.fn = fn
        op.used = False
        op.sem = dma_sem
        op.tok = None
        deps = []
        for k in reads:
            w = self.lastw.get(k)
            if w is not None:
                deps.append(w)
        for k in writes:
            w = self.lastw.get(k)
            if w is not None:
                deps.append(w)
            r = self.rd.get(k)
            if r:
                deps.extend(r.values())
        p = self.pending.pop(eng, None)
        if p:
            deps.extend(p)
        op.deps = list(set(deps))
        rkey = eng if dma_sem is None else id(op)
        for k in reads:
            self.rd.setdefault(k, {})[rkey] = op
        for k in writes:
            self.lastw[k] = op
            self.rd[k] = {}
        if dma_sem is not None:
            dma_sem.count += 16
            op.tok = (dma_sem, dma_sem.count)
        self.ops[eng].append(op)
        self.lastop[eng] = op
        if eng == "pool" and dma_sem is None:
            self.lastop["poolc"] = op
        return op

    def fence(self):
        snap = [self.lastop[e] for e in ("pe", "act", "dve", "poolc") if e in self.lastop]
        self.last_fence = list(snap)
        for e in ("pe", "act", "dve", "pool"):
            self.pending[e] = list(snap)

    def fence_pe(self):
        snap = [self.lastop["pe"]]
        self.last_fence = list(snap)
        for e in ("act", "dve", "pool"):
            self.pending[e] = list(snap)

    def finalize(self, esems):
        for e, lst in self.ops.items():
            for op in lst:
                for d in op.deps:
                    if d.eng == "pe" and e == "pe" and d.sem is None:
                        continue
                    d.used = True
        for e, lst in self.ops.items():
            cnt = 0
            for op in lst:
                if op.sem is None and op.used:
                    cnt += 1
                    op.tok = (esems[e], cnt)

    def run(self, e, eng):
        waited = {}
        for op in self.ops[e]:
            need = {}
            for d in op.deps:
                if d.eng == "pe" and e == "pe" and d.sem is None:
                    continue
                s, v = d.tok
                if need.get(s, 0) < v:
                    need[s] = v
            for s, v in need.items():
                if waited.get(s, 0) < v:
                    eng.wait_ge(s.h, v)
                    waited[s] = v
            ins = op.fn(eng)
            if ins is None:
                continue
            if op.sem is not None:
                ins.then_inc(op.sem.h, 16)
            elif op.used:
                ins.then_inc(op.tok[0].h, 1)


class Rot:
    def __init__(self, items):
        self.items = list(items)
        self.i = 0

    def next(self):
        v = self.items[self.i % len(self.items)]
        self.i += 1
        return v


def build(n_st, stop_after=None):
    nc = bass.Bass("TRN2", target_bir_lowering=False)
    TOK = n_st * NT
    xT_d = nc.dram_tensor("xT", [8, 128, TOK], F32, kind="ExternalInput").ap()
    wst_d = nc.dram_tensor("wst", [NSLAB, 128, SLAB], F32, kind="ExternalInput").ap()
    vecs_d = nc.dram_tensor("vecs", [128, NV], F32, kind="ExternalInput").ap()
    rope_d = nc.dram_tensor("rope", [2, 128, TOK], F32, kind="ExternalInput").ap()
    rows_d = nc.dram_tensor("rows", [1, NROW], F32, kind="ExternalInput").ap()
    wsT_d = nc.dram_tensor("wsT", [128, 1024], F32, kind="ExternalInput").ap()
    cm_d = nc.dram_tensor("cmask", [128, 1024 + 128], F32, kind="ExternalInput").ap()
    yT_d = nc.dram_tensor("yT", [8, 128, TOK], F32, kind="ExternalOutput").ap()
    scr_d = nc.dram_tensor("wscr", [NSLAB, 128, SLAB], BF16, kind="Internal").ap()

    ARENA = 30720
    import contextlib
    with contextlib.ExitStack() as es:
        def sb(name, shape, dt):
            return es.enter_context(nc.sbuf_tensor(name, shape, dt))

        def sem(name):
            return Sem(es.enter_context(nc.semaphore(name)))

        xs = sb("xT_sb", [128, 4, 8, 512], F32)
        hm = sb("hm", [128, 8192], F32)
        sq = sb("sq", [128, 3, 512], BF16)
        rstd = sb("rstd", [128, 2, 512], F32)
        ring = sb("ring", [128, R, SLAB], BF16)
        ropeb = sb("ropeb", [128, 2, 2, 512], F32)
        vecs = sb("vecs_sb", [128, NV], F32)
        sinks = sb("sinks", [128, 16], F32)
        esink = sb("esink", [128, 16], F32)
        bvt = sb("bvt", [128, 256], F32)
        maskT = sb("maskT", [128, 2, 512], BF16)
        ones = sb("ones", [128, 128], BF16)
        sel = sb("sel", [1, 128], BF16)
        esrow = sb("esrow", [1, 2048], BF16)
        carryK = sb("carryK", [128, 2, 128], BF16)
        carryV = sb("carryV", [128, 4, 128], BF16)
        WsT = sb("WsT", [128, 8, 128], BF16)
        small = sb("small", [128, 64], F32)
        arena = sb("arena", [128, ARENA], BF16)
        ps = [es.enter_context(nc.psum_tensor("ps%d" % i, [128, 512], F32)) for i in range(8)]

        esems = {e: sem("s_" + e) for e in ("pe", "act", "dve", "pool", "sp")}
        ring_sems = [sem("ring%d" % i) for i in range(R)]
        scr_sems = [sem("scr%d" % i) for i in range(R)]
        xl_sems = [sem("xl%d" % i) for i in range(4)]
        st_sems = [sem("store%d" % i) for i in range(4)]
        rope_sems = [sem("rope%d" % i) for i in range(2)]
        c_sems = [sem("const%d" % i) for i in range(10)]
        sg_sems = [sem("sguc%d" % i) for i in range(3)]

        block = es.enter_context(nc.Block())
        S = Sched()

        hT = hm[:, 0:4096].bitcast(BF16).rearrange("p (c t) -> p c t", c=8)
        mT = hm[:].rearrange("p (g c t) -> p g c t", g=2, c=8)

        class Carver:
            def __init__(self):
                self.off = 0

            def take(self, nelem, dt):
                nb = nelem * (4 if dt == F32 else 2)
                a = self.off // 2
                self.off += nb
                assert self.off <= ARENA * 2, "arena overflow"
                v = arena[:, a:a + nb // 2]
                return v.bitcast(F32) if dt == F32 else v

        def mm(out, lhsT, rhs, start, stop, reads, writes):
            S.add("pe", lambda e, o=out, l=lhsT, r=rhs, a=start, b=stop: e.matmul(o, l, r, start=a, stop=b),
                  reads, writes)

        def act(out, in_, func, reads, writes, scale=None, bias=None):
            kw = {}
            if scale is not None:
                kw["scale"] = scale
            if bias is not None:
                kw["bias"] = bias
            S.add("act", lambda e, o=out, i=in_, f=func, kw=kw: e.activation(out=o, in_=i, func=f, **kw),
                  reads, writes)

        def tt(out, in0, in1, op, reads, writes, eng="dve"):
            S.add(eng, lambda e, o=out, a=in0, b=in1, p=op: e.tensor_tensor(out=o, in0=a, in1=b, op=p),
                  reads, writes)

        def stt(out, in0, scalar, in1, op0, op1, reads, writes):
            S.add("dve", lambda e, o=out, a=in0, s=scalar, b=in1, p0=op0, p1=op1:
                  e.scalar_tensor_tensor(out=o, in0=a, scalar=s, in1=b, op0=p0, op1=p1), reads, writes)

        def ts(out, in0, s1, s2, op0, op1, reads, writes):
            S.add("dve", lambda e, o=out, a=in0, x=s1, y=s2, p0=op0, p1=op1:
                  e.tensor_scalar(out=o, in0=a, scalar1=x, scalar2=y, op0=p0, op1=p1), reads, writes)

        def dma(q, out, in_, reads, writes, s, extra_deps=None):
            op = S.add(q, lambda e, o=out, i=in_: e.dma_start(out=o, in_=i), reads, writes, dma_sem=s)
            if extra_deps:
                op.deps = list(set(op.deps) | set(extra_deps))
            return op

        vcol = lambda name, c: vecs[:, VC[name] + c:VC[name] + c + 1]

        wstate = {"issued": 0, "used": 0}
        SEQ = (list(range(0, 10)) + list(range(10, 32)) + list(range(32, 48)) * 2 +
               list(range(48, 60)) + list(range(60, 82)) + list(range(82, 98)) * 2)
        NSEQ = len(SEQ)
        TOTAL_SLABS = NSEQ * n_st
        scr_stored = set()

        def slab_width(j):
            if (32 <= j < 48) or (82 <= j < 98):
                return 11 * 128
            return SLAB

        def w_issue():
            i = wstate["issued"]
            if i >= TOTAL_SLABS:
                return
            slot = i % R
            j = SEQ[i % NSEQ]
            wd = slab_width(j)
            if j not in scr_stored:
                dma("pool", ring[:, slot, 0:wd], wst_d[j, :, 0:wd], [], [("w", slot)], ring_sems[slot])
            else:
                dma("pool", ring[:, slot, 0:wd], scr_d[j, :, 0:wd], [("scr", j)], [("w", slot)], ring_sems[slot])
            wstate["issued"] += 1

        def w_get(count):
            n = wstate["used"]
            while wstate["issued"] < min(n + R, TOTAL_SLABS):
                w_issue()
            wstate["used"] += count
            for i in range(n, n + count):
                j = SEQ[i % NSEQ]
                if j not in scr_stored:
                    wd = slab_width(j)
                    dma("sp", scr_d[j, :, 0:wd], ring[:, i % R, 0:wd], [("w", i % R)], [("scr", j)],
                        scr_sems[i % R])
                    scr_stored.add(j)
            return [(ring[:, (n + i) % R, :], ("w", (n + i) % R)) for i in range(count)]

        def w_next():
            return w_get(1)[0]

        dma("sp", vecs[:, :], vecs_d[:, :], [], [("vecs",)], c_sems[0])
        dma("sp", bvt[:, :], rows_d[0:1, 0:256].partition_broadcast(128), [], [("bvt",)], c_sems[1])
        dma("sp", sinks[:, :], rows_d[0:1, 256:272].partition_broadcast(128), [], [("sinks",)], c_sems[2])
        _cv0 = Carver()
        wstage = _cv0.take(1024, F32)
        tril = _cv0.take(128, F32)
        dma("sp", wstage[:, :], wsT_d[:, :], [], [("wstage",)], c_sems[6])
        dma("sp", tril[:, :], cm_d[:, 1024:1152], [], [("tril",)], c_sems[7])
        dma("pool", maskT[:].rearrange("p a b -> p (a b)"), cm_d[:, 0:1024], [], [("maskT",)], c_sems[8])
        for _ in range(R):
            w_issue()
        S.add("dve", lambda e: e.memset(ones[:], 1.0), [], [("ones",)])
        act(esink[:, :], sinks[:, :], AF.Exp, [("sinks",)], [("esink",)])
        S.add("dve", lambda e: e.memset(sel[0:1, 0:64], 0.0), [], [("sel",)])
        S.add("dve", lambda e: e.memset(sel[0:1, 64:128], 1.0), [], [("sel",)])
        for h in range(16):
            ts(esrow[0:1, h * 128:(h + 1) * 128], ones[0:1, 0:128], esink[0:1, h:h + 1], None, ALU.mult, ALU.bypass,
               [("ones",), ("esink",)], [("esrow",)])
        for g in range(8):
            tt(WsT[:, g, :], wstage[:, g * 128:(g + 1) * 128], tril[:, :], ALU.mult,
               [("wstage",), ("tril",)], [("WsT",)])
        tt(vecs[:, VC["bog"]:VC["bog"] + 8], vecs[:, VC["bo"]:VC["bo"] + 8], vecs[:, VC["gmo0"]:VC["gmo0"] + 8],
           ALU.mult, [("vecs",)], [("vecs",)])
        S.fence()

        sq_rot = Rot([0, 1, 2])
        cur = {"st": 0}

        def xslot(grp):
            return (2 * cur["st"] + grp) % 4

        def X(grp, c):
            return xs[:, xslot(grp), c, :]

        def xkey(grp, c):
            return ("x", xslot(grp), c)

        pre_done = {"v": False}
        deferred = []

        def run_deferred():
            while deferred:
                deferred.pop(0)()

        def prenorm_group(gname, grp, st=None, bank=None):
            if st is None:
                st = cur["st"]
            if bank is None:
                bank = 6 + grp
            slot = (2 * st + grp) % 4
            cs = slice(grp * 512, (grp + 1) * 512)
            for c in range(8):
                r = sq_rot.next()
                act(sq[:, r, :], xs[:, slot, c, :], AF.Square, [("x", slot, c)], [("sq", r)])
                mm(ps[bank][:, :], ones[:, :], sq[:, r, :], c == 0, c == 7,
                   [("ones",), ("sq", r)], [("ps", bank)])
            act(rstd[:, grp, :], ps[bank][:, :], AF.Ln, [("ps", bank)], [("rstd", grp)],
                scale=1.0 / 1024.0, bias=EPS)
            act(rstd[:, grp, :], rstd[:, grp, :], AF.Exp, [("rstd", grp)], [("rstd", grp)], scale=-0.5)
            for c in range(8):
                stt(hT[:, c, cs], xs[:, slot, c, :], vcol(gname, c), rstd[:, grp, :], ALU.mult, ALU.mult,
                    [("x", slot, c), ("rstd", grp), ("vecs",)], [("h", grp, c), ("m", 0, c)])

        class PreTask:
            def __init__(self, gname, grp, st, bank):
                self.gname, self.grp, self.st, self.bank = gname, grp, st, bank
                self.slot = (2 * st + grp) % 4

            def square(self, c):
                r = sq_rot.next()
                act(sq[:, r, :], xs[:, self.slot, c, :], AF.Square, [("x", self.slot, c)], [("sq", r)])
                return r

            def stats(self, c, r):
                mm(ps[self.bank][:, :], ones[:, :], sq[:, r, :], c == 0, c == 7,
                   [("ones",), ("sq", r)], [("ps", self.bank)])

            def fin(self):
                grp = self.grp
                cs = slice(grp * 512, (grp + 1) * 512)
                act(rstd[:, grp, :], ps[self.bank][:, :], AF.Ln, [("ps", self.bank)], [("rstd", grp)],
                    scale=1.0 / 1024.0, bias=EPS)
                act(rstd[:, grp, :], rstd[:, grp, :], AF.Exp, [("rstd", grp)], [("rstd", grp)], scale=-0.5)
                for c in range(8):
                    stt(hT[:, c, cs], xs[:, self.slot, c, :], vcol(self.gname, c), rstd[:, grp, :],
                        ALU.mult, ALU.mult, [("x", self.slot, c), ("rstd", grp), ("vecs",)],
                        [("h", grp, c), ("m", 0, c)])

        def prenorm(gname):
            if pre_done["v"]:
                pre_done["v"] = False
                return
            for grp in range(2):
                prenorm_group(gname, grp)

        class PostNorm:
            def __init__(self, gname, bias_name=None, bg_name=None):
                self.gname = gname
                self.bias_name = bias_name
                self.bg_name = bg_name
                self.pending = None
                self.count = [0, 0]

            def tile(self, bank, grp, c):
                self.flush()
                r = sq_rot.next()
                b_sq = vcol(self.bias_name, c) if self.bias_name else None
                b_mg = vcol(self.bg_name, c) if self.bg_name else None
                act(sq[:, r, :], ps[bank][:, :], AF.Square, [("ps", bank), ("vecs",)], [("sq", r)], bias=b_sq)
                mkeys = [("m", grp, c)] + ([("h", 0, c), ("h", 1, c)] if grp == 0 else [])
                act(mT[:, grp, c, :], ps[bank][:, :], AF.Identity, [("ps", bank), ("vecs",)], mkeys,
                    scale=vcol(self.gname, c), bias=b_mg)
                self.pending = (r, grp)

            def flush(self):
                if self.pending is None:
                    return
                r, grp = self.pending
                n = self.count[grp]
                mm(ps[6 + grp][:, :], ones[:, :], sq[:, r, :], n == 0, n == 7,
                   [("ones",), ("sq", r)], [("ps", 6 + grp)])
                self.count[grp] += 1
                self.pending = None

            def post_group(self, grp, final_store=None):
                act(rstd[:, grp, :], ps[6 + grp][:, :], AF.Ln, [("ps", 6 + grp)], [("rstd", grp)],
                    scale=1.0 / 1024.0, bias=EPS)
                act(rstd[:, grp, :], rstd[:, grp, :], AF.Exp, [("rstd", grp)], [("rstd", grp)], scale=-0.5)
                for c in range(8):
                    tt(mT[:, grp, c, :], mT[:, grp, c, :], rstd[:, grp, :], ALU.mult,
                       [("m", grp, c), ("rstd", grp)], [("m", grp, c)])
                    tt(X(grp, c), X(grp, c), mT[:, grp, c, :], ALU.add,
                       [xkey(grp, c), ("m", grp, c)], [xkey(grp, c)])
                if final_store is not None:
                    final_store(grp)

        def out_phase(pn, tile_fn, nxt, final_store=None, c_pre=4):
            for c in range(8):
                pn.tile(tile_fn(0, c), 0, c)
            pn.tile(tile_fn(1, 0), 1, 0)
            pn.post_group(0, final_store)
            early = nxt is not None and nxt[1] != cur["st"]
            tasks, plan = [], {}
            if nxt is not None:
                t0 = 1 if early else 2
                tasks.append(PreTask(nxt[0], 0, nxt[1], 6))
                for i in range(4):
                    plan.setdefault(t0 + i, []).extend([(0, 2 * i), (0, 2 * i + 1)])
                if early:
                    tasks.append(PreTask(nxt[0], 1, nxt[1], 6))
                    for i in range(3):
                        plan.setdefault(5 + i, []).extend([(1, 2 * i), (1, 2 * i + 1)])
            for t in range(1, 8):
                issued = [(ti, c, tasks[ti].square(c)) for ti, c in plan.get(t, [])]
                bank = tile_fn(1, t)
                for ti, c, r in issued:
                    tasks[ti].stats(c, r)
                    if c == 7:
                        tasks[ti].fin()
                pn.tile(bank, 1, t)
            pn.flush()
            if early:
                for c in (6, 7):
                    tasks[1].stats(c, tasks[1].square(c))
                tasks[1].fin()
            pn.post_group(1, final_store)
            if nxt is not None:
                if not early:
                    deferred.append(lambda: prenorm_group(nxt[0], 1, nxt[1]))
                pre_done["v"] = True

        def attention(st):
            cv = Carver()
            qoT = cv.take(8 * NT, BF16).rearrange("p (s t) -> p s t", s=8)
            kT = cv.take(2 * NT, BF16).rearrange("p (s t) -> p s t", s=2)
            Vx = cv.take(NB * 4 * 128, BF16).rearrange("p (b g d) -> p b g d", b=NB, g=4)
            eT = cv.take(3 * 2 * 512, BF16).rearrange("p (u a t) -> p u a t", u=3, a=2)
            PT = cv.take(3 * 2 * 512, BF16).rearrange("p (u a t) -> p u a t", u=3, a=2)
            qs = cv.take(2 * 512, F32).rearrange("p (u t) -> p u t", u=2)
            t1 = cv.take(2 * 512, F32).rearrange("p (u t) -> p u t", u=2)
            t2 = cv.take(2 * 512, F32).rearrange("p (u t) -> p u t", u=2)
            lnd = cv.take(2 * 512, F32).rearrange("p (u t) -> p u t", u=2)
            Rr = cv.take(2 * 512, F32).rearrange("p (u t) -> p u t", u=2)

            prenorm("gmp0")
            S.add("dve", lambda e: e.memset(Vx[:, :, :, 64:128], 1.0), [], [("v", b) for b in range(NB)])
            if st == 0:
                S.add("dve", lambda e: e.memset(carryV[:, :, :], 0.0), [], [("vc",)])
                S.add("dve", lambda e: e.memset(carryK[:, :, :], 0.0), [], [("kc",)])

            bank_rot = Rot([0, 1, 2, 3])
            tmp_rot = Rot([0, 1])

            def qk_tile(wv, wk, ti, grp, bias_col, bias_sw_col, out_ap, out_keys):
                cs = slice(grp * 512, (grp + 1) * 512)
                bank = bank_rot.next()
                for k in range(8):
                    mm(ps[bank][:, :], wv[:, (ti * 8 + k) * 128:(ti * 8 + k + 1) * 128], hT[:, k, cs],
                       k == 0, k == 7, [wk, ("h", grp, k)], [("ps", bank)])
                u = tmp_rot.next()
                for a in range(4):
                    b = a ^ 1
                    act(qs[32 * a:32 * a + 32, u, :], ps[bank][32 * b:32 * b + 32, :], AF.Copy,
                        [("ps", bank)], [("qs", u, a)])
                stt(t1[:, u, :], ps[bank][:, :], bias_col, ropeb[:, grp, 0, :], ALU.add, ALU.mult,
                    [("ps", bank), ("rope", grp), ("vecs",)] + [("qs", u, a) for a in range(4)], [("t1", u)])
                stt(t2[:, u, :], qs[:, u, :], bias_sw_col, ropeb[:, grp, 1, :], ALU.add, ALU.mult,
                    [("qs", u, a) for a in range(4)] + [("rope", grp), ("vecs",)], [("t2", u)])
                tt(out_ap, t1[:, u, :], t2[:, u, :], ALU.add, [("t1", u), ("t2", u)], out_keys)

            def q_tile(wv, wk, j, ti, grp):
                s = 2 * j + ti
                cs = slice(grp * 512, (grp + 1) * 512)
                keys = [("q", s, b) for b in range(grp * 4, grp * 4 + 4)]
                qk_tile(wv, wk, ti, grp, vcol("bq", s), vcol("bqs", s), qoT[:, s, cs], keys)

            first = w_get(4)
            for grp in range(2):
                for j in range(4):
                    for ti in range(2):
                        q_tile(first[j][0], first[j][1], j, ti, grp)
                    if grp == 0 and j == 0:
                        run_deferred()
            wv, wk = w_next()
            for ti in range(2):
                for grp in range(2):
                    cs = slice(grp * 512, (grp + 1) * 512)
                    keys = [("k", ti, b) for b in range(grp * 4, grp * 4 + 4)]
                    qk_tile(wv, wk, ti, grp, vcol("bk", ti), vcol("bks", ti), kT[:, ti, cs], keys)
            wv, wk = w_next()
            for b0 in range(0, NB, 4):
                vb_banks = [4, 5, 6, 7]
                grp = b0 // 4
                for k in range(8):
                    for bb in range(4):
                        b = b0 + bb
                        mm(ps[vb_banks[bb]][:, 0:256], hT[:, k, b * 128:(b + 1) * 128], wv[:, k * 256:(k + 1) * 256],
                           k == 0, k == 7, [wk, ("h", grp, k)], [("ps", vb_banks[bb])])
                for bb in range(4):
                    b = b0 + bb
                    tt(Vx[:, b, :, 0:64], ps[vb_banks[bb]][:, 0:256].rearrange("p (g d) -> p g d", g=4),
                       bvt[:, :].rearrange("p (g d) -> p g d", g=4), ALU.add,
                       [("ps", vb_banks[bb]), ("bvt",)], [("v", b)])

            sc_rot = Rot([(0, 1), (2, 3)])
            pv_rot = Rot([4, 5])
            iters = [(b, g) for b in range(NB) for g in range(4)]

            def stage1(i):
                b, g = iters[i]
                has_prev = (st * NB + b) > 0
                half = g % 2
                hs = slice(half * 64, half * 64 + 64)
                s0 = (g // 2) * 4
                kt = g // 2
                bs = slice(b * 128, (b + 1) * 128)
                q_rhs = qoT[hs, s0:s0 + 4, bs]
                q_keys = [("q", s0 + jj, b) for jj in range(4)]
                bp, bc = sc_rot.next()
                u = i % 3
                if has_prev:
                    if b > 0:
                        kprev, kpk = kT[hs, kt, (b - 1) * 128:b * 128], ("k", kt, b - 1)
                    else:
                        kprev, kpk = carryK[hs, kt, :], ("kc",)
                    mm(ps[bp][:, :], kprev, q_rhs, True, True, [kpk] + q_keys, [("ps", bp)])
                mm(ps[bc][:, :], kT[hs, kt, bs], q_rhs, True, True, [("k", kt, b)] + q_keys, [("ps", bc)])
                if has_prev:
                    act(eT[:, u, 0, :], ps[bp][:, :], AF.Exp, [("ps", bp)], [("e", u, 0)], scale=0.125)
                act(eT[:, u, 1, :], ps[bc][:, :], AF.Exp, [("ps", bc)], [("e", u, 1)], scale=0.125)
                if has_prev:
                    tt(PT[:, u, :, :], eT[:, u, :, :], maskT[:, :, :], ALU.mult,
                       [("e", u, 0), ("e", u, 1), ("maskT",)], [("P", u)])
                else:
                    tt(PT[:, u, 1, :], eT[:, u, 1, :], maskT[:, 1, :], ALU.mult,
                       [("e", u, 1), ("maskT",)], [("P", u)])

            def stage2(i):
                b, g = iters[i]
                has_prev = (st * NB + b) > 0
                half = g % 2
                hs = slice(half * 64, half * 64 + 64)
                s0 = (g // 2) * 4
                bs = slice(b * 128, (b + 1) * 128)
                q_keys = [("q", s0 + jj, b) for jj in range(4)]
                u = i % 3
                w = i % 2
                pvb = pv_rot.next()
                if has_prev:
                    if b > 0:
                        vprev, vpk = Vx[:, b - 1, g, :], ("v", b - 1)
                    else:
                        vprev, vpk = carryV[:, g, :], ("vc",)
                    mm(ps[pvb][:, :], vprev, PT[:, u, 0, :], True, False, [vpk, ("P", u)], [("ps", pvb)])
                mm(ps[pvb][:, :], Vx[:, b, g, :], PT[:, u, 1, :], not has_prev, False,
                   [("v", b), ("P", u)], [("ps", pvb)])
                mm(ps[pvb][:, :], sel[0:1, :], esrow[0:1, g * 512:(g + 1) * 512], False, True,
                   [("sel",), ("esrow",)], [("ps", pvb)])
                act(lnd[64:128, w, :], ps[pvb][64:128, :], AF.Ln, [("ps", pvb)], [("lnd", w)])
                act(Rr[0:64, w, :], lnd[64:128, w, :], AF.Exp, [("lnd", w)], [("R", w)], scale=-1.0)
                tt(qoT[hs, s0:s0 + 4, bs], ps[pvb][0:64, :].rearrange("p (j t) -> p j t", j=4),
                   Rr[0:64, w, :].rearrange("p (j t) -> p j t", j=4), ALU.mult,
                   [("ps", pvb), ("R", w)], q_keys)

            for i in range(len(iters)):
                stage1(i)
                if i >= 1:
                    stage2(i - 1)
            stage2(len(iters) - 1)
            S.add("dve", lambda e: e.tensor_copy(out=carryK[:, :, :], in_=kT[:, :, (NB - 1) * 128:NB * 128]),
                  [("k", 0, NB - 1), ("k", 1, NB - 1)], [("kc",)])
            S.add("dve", lambda e: e.tensor_copy(out=carryV[:, :, :], in_=Vx[:, NB - 1, :, :]),
                  [("v", NB - 1)], [("vc",)])

            pn = PostNorm("gmo0", "bo", "bog")
            orot = Rot([0, 1, 2, 3])
            wo_slabs = w_get(4)

            def wo_tile(grp, c):
                wv, wk = wo_slabs[c // 2]
                mi = c % 2
                cs = slice(grp * 512, (grp + 1) * 512)
                bank = orot.next()
                for k in range(8):
                    mm(ps[bank][:, :], wv[:, (mi * 8 + k) * 128:(mi * 8 + k + 1) * 128], qoT[:, k, cs],
                       k == 0, k == 7, [wk] + [("q", k, b) for b in range(grp * 4, grp * 4 + 4)],
                       [("ps", bank)])
                return bank

            out_phase(pn, wo_tile, ("gfp0", st))
            S.fence_pe()

        def ffn(st, layer, final_store=None):
            cv = Carver()
            gT = cv.take(NF * NT, BF16).rearrange("p (f t) -> p f t", f=NF)
            sg = cv.take(2 * 512, F32).rearrange("p (u t) -> p u t", u=2)
            prenorm("gfp%d" % layer)
            grot = Rot([(0, 1), (2, 3)])
            urot = Rot([0, 1])
            def gu_tile(wv, wk, f, grp):
                cs = slice(grp * 512, (grp + 1) * 512)
                bg_, bu_ = grot.next()
                u = urot.next()
                for k in range(8):
                    mm(ps[bg_][:, :], wv[:, k * 128:(k + 1) * 128], hT[:, k, cs], k == 0, k == 7,
                       [wk, ("h", grp, k)], [("ps", bg_)])
                for k in range(8):
                    mm(ps[bu_][:, :], wv[:, (8 + k) * 128:(9 + k) * 128], hT[:, k, cs], k == 0, k == 7,
                       [wk, ("h", grp, k)], [("ps", bu_)])
                act(sg[:, u, :], ps[bg_][:, :], AF.Silu, [("ps", bg_)], [("sg", u)])
                tt(gT[:, f, cs], ps[bu_][:, :], sg[:, u, :], ALU.mult, [("ps", bu_), ("sg", u)],
                   [("g", f, grp)])

            NSK = 4
            first = w_get(NSK)
            for grp in range(2):
                for f in range(NSK):
                    gu_tile(first[f][0], first[f][1], f, grp)
                    if grp == 0 and f == 1:
                        run_deferred()
            for f in range(NSK, NF):
                wv, wk = w_next()
                for grp in range(2):
                    gu_tile(wv, wk, f, grp)
            pn = PostNorm("gfo%d" % layer)
            orot = Rot([0, 1, 2, 3])

            def dn_tile(grp, c):
                (wv0, wk0), (wv1, wk1) = w_get(2)
                cs = slice(grp * 512, (grp + 1) * 512)
                bank = orot.next()
                for f in range(NF):
                    wv, wk = (wv0, wk0) if f < 11 else (wv1, wk1)
                    kk = f % 11
                    mm(ps[bank][:, :], wv[:, kk * 128:(kk + 1) * 128], gT[:, f, cs], f == 0, f == NF - 1,
                       [wk, ("g", f, grp)], [("ps", bank)])
                return bank

            if layer == 0:
                nxt = ("gmp1", st)
            else:
                nxt = ("gmp0", st + 1) if st + 1 < n_st else None
            out_phase(pn, dn_tile, nxt, final_store, c_pre=3)
            S.fence_pe()

        def sgu(st):
            cv = Carver()
            uT = cv.take(8 * NT, BF16).rearrange("p (c t) -> p c t", c=8)
            vf = cv.take(3 * 1024, F32).rearrange("p (u t) -> p u t", u=3)
            vb = cv.take(3 * 1024, BF16).rearrange("p (u t) -> p u t", u=3)
            tmp = cv.take(2 * 512, F32).rearrange("p (u t) -> p u t", u=2)
            lng = cv.take(1024, F32)
            lnb = cv.take(1024, F32)
            bsp = cv.take(1024, F32)
            fdeps = list(S.last_fence)
            dma("sp", lng[:, :], rows_d[0:1, 272:1296].partition_broadcast(128), [], [("lng",)], sg_sems[0], fdeps)
            dma("sp", lnb[:, :], rows_d[0:1, 1296:2320].partition_broadcast(128), [], [("lnb",)], sg_sems[1], fdeps)
            dma("sp", bsp[:, :], rows_d[0:1, 2320:3344].partition_broadcast(128), [], [("bsp",)], sg_sems[2], fdeps)
            prenorm("gmp1")
            brot = Rot([0, 1, 2, 3])

            def u_tile(wv, wk, i, mi, grp):
                c = 2 * i + mi
                cs = slice(grp * 512, (grp + 1) * 512)
                bank = brot.next()
                for k in range(8):
                    mm(ps[bank][:, :], wv[:, (mi * 8 + k) * 128:(mi * 8 + k + 1) * 128], hT[:, k, cs],
                       k == 0, k == 7, [wk, ("h", grp, k)], [("ps", bank)])
                act(uT[:, c, cs], ps[bank][:, :], AF.Gelu_apprx_tanh, [("ps", bank)],
                    [("u", c, b) for b in range(grp * 4, grp * 4 + 4)])

            first = w_get(4)
            for grp in range(2):
                for i in range(4):
                    for mi in range(2):
                        u_tile(first[i][0], first[i][1], i, mi, grp)
                    if grp == 0 and i == 0:
                        run_deferred()
            vslabs = w_get(4)
            mrot = Rot([(4, 5), (6, 7)])
            trot = Rot([0, 1])

            def stageA(b):
                grp = b // 4
                bs = slice(b * 128, (b + 1) * 128)
                u = b % 3
                for vh in range(2):
                    bank = brot.next()
                    for k in range(8):
                        wv, wk = vslabs[vh * 2 + k // 4]
                        kk = k % 4
                        mm(ps[bank][:, :], hT[:, k, bs], wv[:, kk * 512:(kk + 1) * 512], k == 0, k == 7,
                           [wk, ("h", grp, k)], [("ps", bank)])
                    act(vf[:, u, vh * 512:(vh + 1) * 512], ps[bank][:, :], AF.Gelu_apprx_tanh,
                        [("ps", bank)], [("vf", u, vh)])
                o = 16 * u
                S.add("dve", lambda e, o=o, u=u: e.bn_stats(out=small[:, o:o + 6], in_=vf[:, u, 0:512]),
                      [("vf", u, 0)], [("bn", u, 0)])
                S.add("dve", lambda e, o=o, u=u: e.bn_stats(out=small[:, o + 6:o + 12], in_=vf[:, u, 512:1024]),
                      [("vf", u, 1)], [("bn", u, 1)])
                S.add("dve", lambda e, o=o: e.bn_aggr(out=small[:, o + 12:o + 14], in_=small[:, o:o + 12]),
                      [("bn", u, 0), ("bn", u, 1)], [("mv", u)])
                o2 = 48 + 4 * u
                act(small[:, o2:o2 + 1], small[:, o + 13:o + 14], AF.Ln, [("mv", u)], [("lnv", u)], bias=EPS)
                act(small[:, o2 + 1:o2 + 2], small[:, o2:o2 + 1], AF.Exp, [("lnv", u)], [("rs", u)], scale=-0.5)
                ts(small[:, o2 + 2:o2 + 3], small[:, o + 12:o + 13], -1.0, small[:, o2 + 1:o2 + 2],
                   ALU.mult, ALU.mult, [("mv", u), ("rs", u)], [("nmr", u)])
                vkeys = [("vf", u, 0), ("vf", u, 1)]
                act(vf[:, u, :], vf[:, u, :], AF.Identity, vkeys + [("rs", u), ("nmr", u)], vkeys,
                    scale=small[:, o2 + 1:o2 + 2], bias=small[:, o2 + 2:o2 + 3])
                tt(vf[:, u, :], vf[:, u, :], lng[:, :], ALU.mult, vkeys + [("lng",)], vkeys, eng="pool")
                tt(vb[:, u, :], vf[:, u, :], lnb[:, :], ALU.add, vkeys + [("lnb",)], [("vb", u)], eng="pool")

            def stageB(b):
                bs = slice(b * 128, (b + 1) * 128)
                u = b % 3
                banks = mrot.next()
                for gg in range(8):
                    bank = banks[gg // 4]
                    col = (gg % 4) * 128
                    mm(ps[bank][:, col:col + 128], vb[:, u, gg * 128:(gg + 1) * 128], WsT[:, gg, :], True, True,
                       [("vb", u), ("WsT",)], [("ps", bank)])
                for hb in range(2):
                    bank = banks[hb]
                    tu = trot.next()
                    tt(tmp[:, tu, :], ps[bank][:, :], bsp[:, hb * 512:(hb + 1) * 512], ALU.add,
                       [("ps", bank), ("bsp",)], [("tmp", tu)])
                    ukeys = [("u", 4 * hb + jj, b) for jj in range(4)]
                    tt(uT[:, 4 * hb:4 * hb + 4, bs], uT[:, 4 * hb:4 * hb + 4, bs],
                       tmp[:, tu, :].rearrange("p (j t) -> p j t", j=4), ALU.mult,
                       [("tmp", tu)] + ukeys, ukeys)

            for b in range(NB):
                stageA(b)
                if b >= 2:
                    stageB(b - 2)
            stageB(NB - 2)
            stageB(NB - 1)
            pn = PostNorm("gmo1")
            orot = Rot([0, 1, 2, 3])
            wo_slabs = w_get(4)

            def so_tile(grp, c):
                wv, wk = wo_slabs[c // 2]
                mi = c % 2
                cs = slice(grp * 512, (grp + 1) * 512)
                bank = orot.next()
                for k in range(8):
                    mm(ps[bank][:, :], wv[:, (mi * 8 + k) * 128:(mi * 8 + k + 1) * 128], uT[:, k, cs],
                       k == 0, k == 7, [wk] + [("u", k, b) for b in range(grp * 4, grp * 4 + 4)],
                       [("ps", bank)])
                return bank

            out_phase(pn, so_tile, ("gfp1", st))
            S.fence_pe()

        store_ops = []

        def x_load(st, grp):
            slot = (2 * st + grp) % 4
            t0 = st * NT + grp * 512
            dma("sp", xs[:, slot, :, :], xT_d[:, :, t0:t0 + 512].rearrange("c p t -> p c t"),
                [], [("x", slot, c) for c in range(8)], xl_sems[slot])

        def rope_load(st):
            for grp in range(2):
                t0 = st * NT + grp * 512
                dma("sp", ropeb[:, grp, :, :], rope_d[:, :, t0:t0 + 512].rearrange("a p t -> p a t"),
                    [], [("rope", grp)], rope_sems[grp])

        x_load(0, 0)
        x_load(0, 1)
        rope_load(0)
        for st in range(n_st):
            cur["st"] = st

            def final_store(grp, st=st):
                slot = (2 * st + grp) % 4
                t0 = st * NT + grp * 512
                dma("sp", yT_d[:, :, t0:t0 + 512].rearrange("c p t -> p c t"), xs[:, slot, :, :],
                    [("x", slot, c) for c in range(8)], [], st_sems[slot])
                store_ops.append(S.lastop["sp"])

            def prefetch_next():
                if st + 1 < n_st:
                    x_load(st + 1, 0)
                    x_load(st + 1, 1)
                    rope_load(st + 1)

            def last_phase():
                prefetch_next()
                ffn(st, 1, final_store)

            phases = [lambda: attention(st), lambda: ffn(st, 0), lambda: sgu(st), last_phase]
            nph = 4 if stop_after is None else stop_after
            for pi in range(nph):
                phases[pi]()
            if stop_after is not None and stop_after < 4:
                pre_done["v"] = False
                deferred.clear()
                while wstate["used"] < NSEQ * (st + 1):
                    w_next()
                prefetch_next()
                for grp in range(2):
                    final_store(grp)
        fin = S.add("sp", lambda e: None, [], [])
        fin.deps = list(store_ops)

        S.finalize(esems)

        @block.tensor
        def _(e):
            S.run("pe", e)

        @block.scalar
        def _(e):
            S.run("act", e)

        @block.vector
        def _(e):
            S.run("dve", e)

        @block.gpsimd
        def _(e):
            S.run("pool", e)

        @block.sync
        def _(e):
            S.run("sp", e)
    return nc


def _qcols(s):
    hA = 8 * (s // 4) + (s % 4)
    return np.concatenate([np.arange(hA * 64, hA * 64 + 64), np.arange((hA + 4) * 64, (hA + 4) * 64 + 64)])


def _tileB(W):
    nk = W.shape[0] // 128
    return W.reshape(nk, 128, W.shape[1]).transpose(1, 0, 2)


def _prep(inp):
    f = np.float32
    wqkv = np.asarray(inp["attn_w_qkv"], f)[0]
    bqkv = np.asarray(inp["attn_b_qkv"], f)[0]
    wo = np.asarray(inp["attn_w_o"], f)[0]
    w_in = np.asarray(inp["sgu_w_in"], f)[0]
    w_out = np.asarray(inp["sgu_w_out"], f)[0]
    wgu = np.asarray(inp["ffn_w_gate_up"], f)
    wdn = np.asarray(inp["ffn_w_down"], f)
    slabs = np.zeros((NSLAB, 128, SLAB), f)
    idx = 0

    def put(i, arr):
        a = np.ascontiguousarray(arr).reshape(128, -1)
        slabs[i, :, :a.shape[1]] = a

    def ffn_slabs(layer, idx):
        for fi in range(NF):
            g = _tileB(wgu[layer][:, fi * 128:(fi + 1) * 128])
            u = _tileB(wgu[layer][:, 2816 + fi * 128:2816 + (fi + 1) * 128])
            put(idx, np.stack([g, u], axis=1))
            idx += 1
        for c in range(8):
            t = _tileB(wdn[layer][:, c * 128:(c + 1) * 128])
            put(idx, t[:, 0:11])
            idx += 1
            put(idx, t[:, 11:22])
            idx += 1
        return idx

    for j in range(4):
        tiles = [_tileB(wqkv[:, _qcols(2 * j + ti)]) for ti in range(2)]
        put(idx, np.stack(tiles, axis=1))
        idx += 1
    tiles = [_tileB(wqkv[:, 1024 + t * 128:1024 + (t + 1) * 128]) for t in range(2)]
    put(idx, np.stack(tiles, axis=1))
    idx += 1
    put(idx, wqkv[:, 1280:1536].reshape(8, 128, 256).transpose(1, 0, 2))
    idx += 1
    rowperm = np.concatenate([_qcols(k) for k in range(8)])
    wo_p = wo[rowperm, :]
    for i in range(4):
        tiles = [_tileB(wo_p[:, (2 * i + mi) * 128:(2 * i + mi + 1) * 128]) for mi in range(2)]
        put(idx, np.stack(tiles, axis=1))
        idx += 1
    idx = ffn_slabs(0, idx)
    for i in range(4):
        tiles = [_tileB(w_in[:, (2 * i + mi) * 128:(2 * i + mi + 1) * 128]) for mi in range(2)]
        put(idx, np.stack(tiles, axis=1))
        idx += 1
    for vh in range(2):
        for kh in range(2):
            blk = w_in[kh * 512:(kh + 1) * 512, 1024 + vh * 512:1024 + (vh + 1) * 512]
            put(idx, blk.reshape(4, 128, 512).transpose(1, 0, 2))
            idx += 1
    for i in range(4):
        tiles = [_tileB(w_out[:, (2 * i + mi) * 128:(2 * i + mi + 1) * 128]) for mi in range(2)]
        put(idx, np.stack(tiles, axis=1))
        idx += 1
    idx = ffn_slabs(1, idx)
    assert idx == NSLAB

    vecs = np.zeros((128, NV), f)

    def putv(name, vec, n):
        vecs[:, VC[name]:VC[name] + n] = np.asarray(vec, f).reshape(n, 128).T

    for i in range(2):
        putv("gmp%d" % i, inp["norm_mix_pre"][i], 8)
        putv("gmo%d" % i, inp["norm_mix_post"][i], 8)
        putv("gfp%d" % i, inp["norm_ffn_pre"][i], 8)
        putv("gfo%d" % i, inp["norm_ffn_post"][i], 8)
    putv("bo", inp["attn_b_o"][0], 8)
    swp = np.arange(128) ^ 32
    for s in range(8):
        cols = _qcols(s)
        vecs[:, VC["bq"] + s] = bqkv[cols]
        vecs[:, VC["bqs"] + s] = bqkv[cols[swp]]
    for t in range(2):
        cols = 1024 + t * 128 + np.arange(128)
        vecs[:, VC["bk"] + t] = bqkv[cols]
        vecs[:, VC["bks"] + t] = bqkv[cols[swp]]

    rows = np.zeros((1, NROW), f)
    rows[0, 0:256] = bqkv[1280:1536]
    rows[0, 256:272] = np.asarray(inp["attn_sinks"], f)[0]
    rows[0, 272:1296] = np.asarray(inp["sgu_ln_g"], f)[0]
    rows[0, 1296:2320] = np.asarray(inp["sgu_ln_b"], f)[0]
    rows[0, 2320:3344] = np.asarray(inp["sgu_b_spatial"], f)[0].reshape(-1)
    wsp = np.asarray(inp["sgu_w_spatial"], f)[0]
    wsT = np.ascontiguousarray(wsp.transpose(2, 0, 1)).reshape(128, 1024)

    half = 32
    inv_freq = 10000.0 ** (-(np.arange(half, dtype=np.float64) * 2.0) / 64.0)
    pos = np.arange(4096, dtype=np.float64)
    ang = pos[None, :] * inv_freq[:, None]
    cos = np.cos(ang).astype(f)
    sin = np.sin(ang).astype(f)
    p = np.arange(128)
    rope = np.zeros((2, 128, 4096), f)
    rope[0] = cos[p % 32]
    sgn = np.where((p % 64) < 32, -1.0, 1.0).astype(f)
    rope[1] = sin[p % 32] * sgn[:, None]
    cm = np.zeros((128, 1024 + 128), f)
    j = np.arange(128)[:, None]
    i = np.arange(128)[None, :]
    prev = (j > i).astype(f)
    cur = (j <= i).astype(f)
    cm[:, 0:512] = np.tile(prev, (1, 4))
    cm[:, 512:1024] = np.tile(cur, (1, 4))
    cm[:, 1024:1152] = cur
    return dict(wst=slabs, vecs=vecs, rows=rows, wsT=wsT, rope=rope, cmask=cm)


_CACHE = {}


def _run(inputs, n_st, stop_after=None, n_cores=8):
    x = np.asarray(inputs["x"], np.float32)
    shared = _prep(inputs)
    TOK = n_st * NT
    key = (n_st, stop_after)
    if key not in _CACHE:
        _CACHE[key] = build(n_st, stop_after)
    nc = _CACHE[key]
    in_maps = []
    for b in range(n_cores):
        xT = np.ascontiguousarray(x[b, :TOK, :].T).reshape(8, 128, TOK)
        m = dict(shared)
        m["rope"] = np.ascontiguousarray(shared["rope"][:, :, :TOK])
        m["xT"] = xT
        in_maps.append(m)
    res = run_bass_kernel_spmd(nc, in_maps, core_ids=list(range(n_cores)))
    out = np.empty((n_cores, TOK, 1024), np.float32)
    for b in range(n_cores):
        out[b] = res.results[b]["yT"].reshape(1024, TOK).T
    return out


def kernel(**inputs):
    return _run(inputs, 4)
```

```python
import numpy as np
import concourse.bass as bass
import concourse.mybir as mybir
from concourse.bass_utils import run_bass_kernel_spmd

F32 = mybir.dt.float32
BF16 = mybir.dt.bfloat16
AF = mybir.ActivationFunctionType
ALU = mybir.AluOpType

NT = 1024
NB = 8
R = 6
SLAB = 2048
NF = 22
EPS = 1e-6
NSLAB = 98
NROW = 256 + 16 + 3 * 1024

VC = {}
_c = 0
for _n, _w in [("gmp0", 8), ("gmo0", 8), ("gfp0", 8), ("gfo0", 8),
               ("gmp1", 8), ("gmo1", 8), ("gfp1", 8), ("gfo1", 8),
               ("bo", 8), ("bq", 8), ("bqs", 8), ("bk", 2), ("bks", 2), ("bog", 8)]:
    VC[_n] = _c
    _c += _w
NV = _c


class Sem:
    def __init__(self, h):
        self.h = h
        self.count = 0


class Op:
    __slots__ = ("eng", "fn", "deps", "tok", "used", "sem")


class Sched:
    def __init__(self):
        self.ops = {e: [] for e in ("pe", "act", "dve", "pool", "sp")}
        self.lastw = {}
        self.rd = {}
        self.pending = {}
        self.lastop = {}

    def add(self, eng, fn, reads=(), writes=(), dma_sem=None):
        op = Op()
        op.eng = eng
        op.fn = fn
        op.used = False
        op.sem = dma_sem
        op.tok = None
        deps = []
        for k in reads:
            w = self.lastw.get(k)
            if w is not None:
                deps.append(w)
        for k in writes:
            w = self.lastw.get(k)
            if w is not None:
                deps.append(w)
            r = self.rd.get(k)
            if r:
                deps.extend(r.values())
        p = self.pending.pop(eng, None)
        if p:
            deps.extend(p)
        op.deps = list(set(deps))
        rkey = eng if dma_sem is None else id(op)
        for k in reads:
            self.rd.setdefault(k, {})[rkey] = op
        for k in writes:
            self.lastw[k] = op
            self.rd[k] = {}
        if dma_sem is not None:
            dma_sem.count += 16
            op.tok = (dma_sem, dma_sem.count)
        self.ops[eng].append(op)
        self.lastop[eng] = op
        if eng == "pool" and dma_sem is None:
            self.lastop["poolc"] = op
        return op

    def fence(self):
        snap = [self.lastop[e] for e in ("pe", "act", "dve", "poolc") if e in self.lastop]
        self.last_fence = list(snap)
        for e in ("pe", "act", "dve", "pool"):
            self.pending[e] = list(snap)

    def fence_pe(self):
        snap = [self.lastop["pe"]]
        self.last_fence = list(snap)
        for e in ("act", "dve", "pool"):
            self.pending[e] = list(snap)

    def finalize(self, esems):
        for e, lst in self.ops.items():
            for op in lst:
                for d in op.deps:
                    if d.eng == "pe" and e == "pe" and d.sem is None:
                        continue
                    d.used = True
        for e, lst in self.ops.items():
            cnt = 0
            for op in lst:
                if op.sem is None and op.used:
                    cnt += 1
                    op.tok = (esems[e], cnt)

    def run(self, e, eng):
        waited = {}
        for op in self.ops[e]:
            need = {}
            for d in op.deps:
                if d.eng == "pe" and e == "pe" and d.sem is None:
                    continue
                s, v = d.tok
                if need.get(s, 0) < v:
                    need[s] = v
            for s, v in need.items():
                if waited.get(s, 0) < v:
                    eng.wait_ge(s.h, v)
                    waited[s] = v
            ins = op.fn(eng)
            if ins is None:
                continue
            if op.sem is not None:
                ins.then_inc(op.sem.h, 16)
            elif op.used:
                ins.then_inc(op.tok[0].h, 1)


class Rot:
    def __init__(self, items):
        self.items = list(items)
        self.i = 0

    def next(self):
        v = self.items[self.i % len(self.items)]
        self.i += 1
        return v


def build(n_st, stop_after=None):
    nc = bass.Bass("TRN2", target_bir_lowering=False)
    TOK = n_st * NT
    xT_d = nc.dram_tensor("xT", [8, 128, TOK], F32, kind="ExternalInput").ap()
    wst_d = nc.dram_tensor("wst", [NSLAB, 128, SLAB], F32, kind="ExternalInput").ap()
    vecs_d = nc.dram_tensor("vecs", [128, NV], F32, kind="ExternalInput").ap()
    rope_d = nc.dram_tensor("rope", [2, 128, TOK], F32, kind="ExternalInput").ap()
    rows_d = nc.dram_tensor("rows", [1, NROW], F32, kind="ExternalInput").ap()
    wsT_d = nc.dram_tensor("wsT", [128, 1024], F32, kind="ExternalInput").ap()
    cm_d = nc.dram_tensor("cmask", [128, 1024 + 128], F32, kind="ExternalInput").ap()
    yT_d = nc.dram_tensor("yT", [8, 128, TOK], F32, kind="ExternalOutput").ap()
    scr_d = nc.dram_tensor("wscr", [NSLAB, 128, SLAB], BF16, kind="Internal").ap()

    ARENA = 30720
    import contextlib
    with contextlib.ExitStack() as es:
        def sb(name, shape, dt):
            return es.enter_context(nc.sbuf_tensor(name, shape, dt))

        def sem(name):
            return Sem(es.enter_context(nc.semaphore(name)))

        xs = sb("xT_sb", [128, 4, 8, 512], F32)
        hm = sb("hm", [128, 8192], F32)
        sq = sb("sq", [128, 3, 512], BF16)
        rstd = sb("rstd", [128, 2, 512], F32)
        ring = sb("ring", [128, R, SLAB], BF16)
        ropeb = sb("ropeb", [128, 2, 2, 512], F32)
        vecs = sb("vecs_sb", [128, NV], F32)
        sinks = sb("sinks", [128, 16], F32)
        esink = sb("esink", [128, 16], F32)
        bvt = sb("bvt", [128, 256], F32)
        maskT = sb("maskT", [128, 2, 512], BF16)
        ones = sb("ones", [128, 128], BF16)
        sel = sb("sel", [1, 128], BF16)
        esrow = sb("esrow", [1, 2048], BF16)
        carryK = sb("carryK", [128, 2, 128], BF16)
        carryV = sb("carryV", [128, 4, 128], BF16)
        WsT = sb("WsT", [128, 8, 128], BF16)
        small = sb("small", [128, 64], F32)
        arena = sb("arena", [128, ARENA], BF16)
        ps = [es.enter_context(nc.psum_tensor("ps%d" % i, [128, 512], F32)) for i in range(8)]

        esems = {e: sem("s_" + e) for e in ("pe", "act", "dve", "pool", "sp")}
        ring_sems = [sem("ring%d" % i) for i in range(R)]
        scr_sems = [sem("scr%d" % i) for i in range(R)]
        xl_sems = [sem("xl%d" % i) for i in range(4)]
        st_sems = [sem("store%d" % i) for i in range(4)]
        rope_sems = [sem("rope%d" % i) for i in range(2)]
        c_sems = [sem("const%d" % i) for i in range(10)]
        sg_sems = [sem("sguc%d" % i) for i in range(3)]

        block = es.enter_context(nc.Block())
        S = Sched()

        hT = hm[:, 0:4096].bitcast(BF16).rearrange("p (c t) -> p c t", c=8)
        mT = hm[:].rearrange("p (g c t) -> p g c t", g=2, c=8)

        class Carver:
            def __init__(self):
                self.off = 0

            def take(self, nelem, dt):
                nb = nelem * (4 if dt == F32 else 2)
                a = self.off // 2
                self.off += nb
                assert self.off <= ARENA * 2, "arena overflow"
                v = arena[:, a:a + nb // 2]
                return v.bitcast(F32) if dt == F32 else v

        def mm(out, lhsT, rhs, start, stop, reads, writes):
            S.add("pe", lambda e, o=out, l=lhsT, r=rhs, a=start, b=stop: e.matmul(o, l, r, start=a, stop=b),
                  reads, writes)

        def act(out, in_, func, reads, writes, scale=None, bias=None):
            kw = {}
            if scale is not None:
                kw["scale"] = scale
            if bias is not None:
                kw["bias"] = bias
            S.add("act", lambda e, o=out, i=in_, f=func, kw=kw: e.activation(out=o, in_=i, func=f, **kw),
                  reads, writes)

        def tt(out, in0, in1, op, reads, writes, eng="dve"):
            S.add(eng, lambda e, o=out, a=in0, b=in1, p=op: e.tensor_tensor(out=o, in0=a, in1=b, op=p),
                  reads, writes)

        def stt(out, in0, scalar, in1, op0, op1, reads, writes):
            S.add("dve", lambda e, o=out, a=in0, s=scalar, b=in1, p0=op0, p1=op1:
                  e.scalar_tensor_tensor(out=o, in0=a, scalar=s, in1=b, op0=p0, op1=p1), reads, writes)

        def ts(out, in0, s1, s2, op0, op1, reads, writes):
            S.add("dve", lambda e, o=out, a=in0, x=s1, y=s2, p0=op0, p1=op1:
                  e.tensor_scalar(out=o, in0=a, scalar1=x, scalar2=y, op0=p0, op1=p1), reads, writes)

        def dma(q, out, in_, reads, writes, s, extra_deps=None):
            op = S.add(q, lambda e, o=out, i=in_: e.dma_start(out=o, in_=i), reads, writes, dma_sem=s)
            if extra_deps:
                op.deps = list(set(op.deps) | set(extra_deps))
            return op

        vcol = lambda name, c: vecs[:, VC[name] + c:VC[name] + c + 1]

        wstate = {"issued": 0, "used": 0}
        SEQ = (list(range(0, 10)) + list(range(10, 32)) + list(range(32, 48)) * 2 +
               list(range(48, 60)) + list(range(60, 82)) + list(range(82, 98)) * 2)
        NSEQ = len(SEQ)
        TOTAL_SLABS = NSEQ * n_st
        scr_stored = set()

        def slab_width(j):
            if (32 <= j < 48) or (82 <= j < 98):
                return 11 * 128
            return SLAB

        def w_issue():
            i = wstate["issued"]
            if i >= TOTAL_SLABS:
                return
            slot = i % R
            j = SEQ[i % NSEQ]
            wd = slab_width(j)
            if j not in scr_stored:
                dma("pool", ring[:, slot, 0:wd], wst_d[j, :, 0:wd], [], [("w", slot)], ring_sems[slot])
            else:
                dma("pool", ring[:, slot, 0:wd], scr_d[j, :, 0:wd], [("scr", j)], [("w", slot)], ring_sems[slot])
            wstate["issued"] += 1

        def w_get(count):
            n = wstate["used"]
            while wstate["issued"] < min(n + R, TOTAL_SLABS):
                w_issue()
            wstate["used"] += count
            for i in range(n, n + count):
                j = SEQ[i % NSEQ]
                if j not in scr_stored:
                    wd = slab_width(j)
                    dma("sp", scr_d[j, :, 0:wd], ring[:, i % R, 0:wd], [("w", i % R)], [("scr", j)],
                        scr_sems[i % R])
                    scr_stored.add(j)
            return [(ring[:, (n + i) % R, :], ("w", (n + i) % R)) for i in range(count)]

        def w_next():
            return w_get(1)[0]

        dma("sp", vecs[:, :], vecs_d[:, :], [], [("vecs",)], c_sems[0])
        dma("sp", bvt[:, :], rows_d[0:1, 0:256].partition_broadcast(128), [], [("bvt",)], c_sems[1])
        dma("sp", sinks[:, :], rows_d[0:1, 256:272].partition_broadcast(128), [], [("sinks",)], c_sems[2])
        _cv0 = Carver()
        wstage = _cv0.take(1024, F32)
        tril = _cv0.take(128, F32)
        dma("sp", wstage[:, :], wsT_d[:, :], [], [("wstage",)], c_sems[6])
        dma("sp", tril[:, :], cm_d[:, 1024:1152], [], [("tril",)], c_sems[7])
        dma("pool", maskT[:].rearrange("p a b -> p (a b)"), cm_d[:, 0:1024], [], [("maskT",)], c_sems[8])
        for _ in range(R):
            w_issue()
        S.add("dve", lambda e: e.memset(ones[:], 1.0), [], [("ones",)])
        act(esink[:, :], sinks[:, :], AF.Exp, [("sinks",)], [("esink",)])
        S.add("dve", lambda e: e.memset(sel[0:1, 0:64], 0.0), [], [("sel",)])
        S.add("dve", lambda e: e.memset(sel[0:1, 64:128], 1.0), [], [("sel",)])
        for h in range(16):
            ts(esrow[0:1, h * 128:(h + 1) * 128], ones[0:1, 0:128], esink[0:1, h:h + 1], None, ALU.mult, ALU.bypass,
               [("ones",), ("esink",)], [("esrow",)])
        for g in range(8):
            tt(WsT[:, g, :], wstage[:, g * 128:(g + 1) * 128], tril[:, :], ALU.mult,
               [("wstage",), ("tril",)], [("WsT",)])
        tt(vecs[:, VC["bog"]:VC["bog"] + 8], vecs[:, VC["bo"]:VC["bo"] + 8], vecs[:, VC["gmo0"]:VC["gmo0"] + 8],
           ALU.mult, [("vecs",)], [("vecs",)])
        S.fence()

        sq_rot = Rot([0, 1, 2])
        cur = {"st": 0}

        def xslot(grp):
            return (2 * cur["st"] + grp) % 4

        def X(grp, c):
            return xs[:, xslot(grp), c, :]

        def xkey(grp, c):
            return ("x", xslot(grp), c)

        pre_done = {"v": False}
        deferred = []

        post_q = []

        def run_post(n):
            for _ in range(min(n, len(post_q))):
                post_q.pop(0)()

        def run_deferred():
            run_post(99)
            while deferred:
                deferred.pop(0)()

        def prenorm_group(gname, grp, st=None, bank=None):
            if st is None:
                st = cur["st"]
            if bank is None:
                bank = 6 + grp
            slot = (2 * st + grp) % 4
            cs = slice(grp * 512, (grp + 1) * 512)
            for c in range(8):
                r = sq_rot.next()
                act(sq[:, r, :], xs[:, slot, c, :], AF.Square, [("x", slot, c)], [("sq", r)])
                mm(ps[bank][:, :], ones[:, :], sq[:, r, :], c == 0, c == 7,
                   [("ones",), ("sq", r)], [("ps", bank)])
            act(rstd[:, grp, :], ps[bank][:, :], AF.Ln, [("ps", bank)], [("rstd", grp)],
                scale=1.0 / 1024.0, bias=EPS)
            act(rstd[:, grp, :], rstd[:, grp, :], AF.Exp, [("rstd", grp)], [("rstd", grp)], scale=-0.5)
            for c in range(8):
                stt(hT[:, c, cs], xs[:, slot, c, :], vcol(gname, c), rstd[:, grp, :], ALU.mult, ALU.mult,
                    [("x", slot, c), ("rstd", grp), ("vecs",)], [("h", grp, c), ("m", 0, c)])

        class PreTask:
            def __init__(self, gname, grp, st, bank):
                self.gname, self.grp, self.st, self.bank = gname, grp, st, bank
                self.slot = (2 * st + grp) % 4

            def square(self, c):
                r = sq_rot.next()
                act(sq[:, r, :], xs[:, self.slot, c, :], AF.Square, [("x", self.slot, c)], [("sq", r)])
                return r

            def stats(self, c, r):
                mm(ps[self.bank][:, :], ones[:, :], sq[:, r, :], c == 0, c == 7,
                   [("ones",), ("sq", r)], [("ps", self.bank)])

            def fin(self):
                grp = self.grp
                cs = slice(grp * 512, (grp + 1) * 512)
                act(rstd[:, grp, :], ps[self.bank][:, :], AF.Ln, [("ps", self.bank)], [("rstd", grp)],
                    scale=1.0 / 1024.0, bias=EPS)
                act(rstd[:, grp, :], rstd[:, grp, :], AF.Exp, [("rstd", grp)], [("rstd", grp)], scale=-0.5)
                for c in range(8):
                    stt(hT[:, c, cs], xs[:, self.slot, c, :], vcol(self.gname, c), rstd[:, grp, :],
                        ALU.mult, ALU.mult, [("x", self.slot, c), ("rstd", grp), ("vecs",)],
                        [("h", grp, c), ("m", 0, c)])

        def prenorm(gname):
            if pre_done["v"]:
                pre_done["v"] = False
                return
            for grp in range(2):
                prenorm_group(gname, grp)

        class PostNorm:
            def __init__(self, gname, bias_name=None, bg_name=None):
                self.gname = gname
                self.bias_name = bias_name
                self.bg_name = bg_name
                self.pending = None
                self.count = [0, 0]

            def tile(self, bank, grp, c):
                self.flush()
                r = sq_rot.next()
                b_sq = vcol(self.bias_name, c) if self.bias_name else None
                b_mg = vcol(self.bg_name, c) if self.bg_name else None
                act(sq[:, r, :], ps[bank][:, :], AF.Square, [("ps", bank), ("vecs",)], [("sq", r)], bias=b_sq)
                mkeys = [("m", grp, c)] + ([("h", 0, c), ("h", 1, c)] if grp == 0 else [])
                act(mT[:, grp, c, :], ps[bank][:, :], AF.Identity, [("ps", bank), ("vecs",)], mkeys,
                    scale=vcol(self.gname, c), bias=b_mg)
                self.pending = (r, grp)

            def flush(self):
                if self.pending is None:
                    return
                r, grp = self.pending
                n = self.count[grp]
                mm(ps[6 + grp][:, :], ones[:, :], sq[:, r, :], n == 0, n == 7,
                   [("ones",), ("sq", r)], [("ps", 6 + grp)])
                self.count[grp] += 1
                self.pending = None

            def post_group(self, grp, final_store=None, defer=False):
                act(rstd[:, grp, :], ps[6 + grp][:, :], AF.Ln, [("ps", 6 + grp)], [("rstd", grp)],
                    scale=1.0 / 1024.0, bias=EPS)
                act(rstd[:, grp, :], rstd[:, grp, :], AF.Exp, [("rstd", grp)], [("rstd", grp)], scale=-0.5)
                st_now = cur["st"]

                def chunk(c):
                    slot = (2 * st_now + grp) % 4
                    xa = xs[:, slot, c, :]
                    tt(mT[:, grp, c, :], mT[:, grp, c, :], rstd[:, grp, :], ALU.mult,
                       [("m", grp, c), ("rstd", grp)], [("m", grp, c)])
                    tt(xa, xa, mT[:, grp, c, :], ALU.add, [("x", slot, c), ("m", grp, c)], [("x", slot, c)])
                    if c == 7 and final_store is not None:
                        final_store(grp)

                for c in range(8):
                    if defer:
                        post_q.append(lambda c=c: chunk(c))
                    else:
                        chunk(c)

        def out_phase(pn, tile_fn, nxt, final_store=None, c_pre=4):
            for c in range(8):
                pn.tile(tile_fn(0, c), 0, c)
            pn.tile(tile_fn(1, 0), 1, 0)
            pn.post_group(0, final_store)
            early = nxt is not None and nxt[1] != cur["st"]
            tasks, plan = [], {}
            if nxt is not None:
                t0 = 1 if early else 2
                tasks.append(PreTask(nxt[0], 0, nxt[1], 6))
                for i in range(4):
                    plan.setdefault(t0 + i, []).extend([(0, 2 * i), (0, 2 * i + 1)])
                if early:
                    tasks.append(PreTask(nxt[0], 1, nxt[1], 6))
                    for i in range(3):
                        plan.setdefault(5 + i, []).extend([(1, 2 * i), (1, 2 * i + 1)])
            for t in range(1, 8):
                issued = [(ti, c, tasks[ti].square(c)) for ti, c in plan.get(t, [])]
                bank = tile_fn(1, t)
                for ti, c, r in issued:
                    tasks[ti].stats(c, r)
                    if c == 7:
                        tasks[ti].fin()
                pn.tile(bank, 1, t)
            pn.flush()
            if early:
                for c in (6, 7):
                    tasks[1].stats(c, tasks[1].square(c))
                tasks[1].fin()
            pn.post_group(1, final_store, defer=early)
            if nxt is not None:
                if not early:
                    deferred.append(lambda: prenorm_group(nxt[0], 1, nxt[1]))
                pre_done["v"] = True

        def attention(st):
            cv = Carver()
            qoT = cv.take(8 * NT, BF16).rearrange("p (s t) -> p s t", s=8)
            kT = cv.take(2 * NT, BF16).rearrange("p (s t) -> p s t", s=2)
            Vx = cv.take(NB * 4 * 128, BF16).rearrange("p (b g d) -> p b g d", b=NB, g=4)
            eT = cv.take(3 * 2 * 512, BF16).rearrange("p (u a t) -> p u a t", u=3, a=2)
            PT = cv.take(3 * 2 * 512, BF16).rearrange("p (u a t) -> p u a t", u=3, a=2)
            qs = cv.take(2 * 512, F32).rearrange("p (u t) -> p u t", u=2)
            t1 = cv.take(2 * 512, F32).rearrange("p (u t) -> p u t", u=2)
            t2 = cv.take(2 * 512, F32).rearrange("p (u t) -> p u t", u=2)
            lnd = cv.take(2 * 512, F32).rearrange("p (u t) -> p u t", u=2)
            Rr = cv.take(2 * 512, F32).rearrange("p (u t) -> p u t", u=2)

            prenorm("gmp0")
            S.add("dve", lambda e: e.memset(Vx[:, :, :, 64:128], 1.0), [], [("v", b) for b in range(NB)])
            if st == 0:
                S.add("dve", lambda e: e.memset(carryV[:, :, :], 0.0), [], [("vc",)])
                S.add("dve", lambda e: e.memset(carryK[:, :, :], 0.0), [], [("kc",)])

            bank_rot = Rot([0, 1, 2, 3])
            tmp_rot = Rot([0, 1])

            def qk_tile(wv, wk, ti, grp, bias_col, bias_sw_col, out_ap, out_keys):
                cs = slice(grp * 512, (grp + 1) * 512)
                bank = bank_rot.next()
                for k in range(8):
                    mm(ps[bank][:, :], wv[:, (ti * 8 + k) * 128:(ti * 8 + k + 1) * 128], hT[:, k, cs],
                       k == 0, k == 7, [wk, ("h", grp, k)], [("ps", bank)])
                u = tmp_rot.next()
                for a in range(4):
                    b = a ^ 1
                    act(qs[32 * a:32 * a + 32, u, :], ps[bank][32 * b:32 * b + 32, :], AF.Copy,
                        [("ps", bank)], [("qs", u, a)])
                stt(t1[:, u, :], ps[bank][:, :], bias_col, ropeb[:, grp, 0, :], ALU.add, ALU.mult,
                    [("ps", bank), ("rope", grp), ("vecs",)] + [("qs", u, a) for a in range(4)], [("t1", u)])
                stt(t2[:, u, :], qs[:, u, :], bias_sw_col, ropeb[:, grp, 1, :], ALU.add, ALU.mult,
                    [("qs", u, a) for a in range(4)] + [("rope", grp), ("vecs",)], [("t2", u)])
                tt(out_ap, t1[:, u, :], t2[:, u, :], ALU.add, [("t1", u), ("t2", u)], out_keys)

            def q_tile(wv, wk, j, ti, grp):
                s = 2 * j + ti
                cs = slice(grp * 512, (grp + 1) * 512)
                keys = [("q", s, b) for b in range(grp * 4, grp * 4 + 4)]
                qk_tile(wv, wk, ti, grp, vcol("bq", s), vcol("bqs", s), qoT[:, s, cs], keys)

            first = w_get(4)
            for grp in range(2):
                for j in range(4):
                    for ti in range(2):
                        q_tile(first[j][0], first[j][1], j, ti, grp)
                        run_post(2)
                    if grp == 0 and j == 0:
                        run_deferred()
            wv, wk = w_next()
            for ti in range(2):
                for grp in range(2):
                    cs = slice(grp * 512, (grp + 1) * 512)
                    keys = [("k", ti, b) for b in range(grp * 4, grp * 4 + 4)]
                    qk_tile(wv, wk, ti, grp, vcol("bk", ti), vcol("bks", ti), kT[:, ti, cs], keys)
            wv, wk = w_next()
            for b0 in range(0, NB, 4):
                vb_banks = [4, 5, 6, 7]
                grp = b0 // 4
                for k in range(8):
                    for bb in range(4):
                        b = b0 + bb
                        mm(ps[vb_banks[bb]][:, 0:256], hT[:, k, b * 128:(b + 1) * 128], wv[:, k * 256:(k + 1) * 256],
                           k == 0, k == 7, [wk, ("h", grp, k)], [("ps", vb_banks[bb])])
                for bb in range(4):
                    b = b0 + bb
                    tt(Vx[:, b, :, 0:64], ps[vb_banks[bb]][:, 0:256].rearrange("p (g d) -> p g d", g=4),
                       bvt[:, :].rearrange("p (g d) -> p g d", g=4), ALU.add,
                       [("ps", vb_banks[bb]), ("bvt",)], [("v", b)])

            sc_rot = Rot([(0, 1), (2, 3)])
            pv_rot = Rot([4, 5])
            iters = [(b, g) for b in range(NB) for g in range(4)]

            def stage1(i):
                b, g = iters[i]
                has_prev = (st * NB + b) > 0
                half = g % 2
                hs = slice(half * 64, half * 64 + 64)
                s0 = (g // 2) * 4
                kt = g // 2
                bs = slice(b * 128, (b + 1) * 128)
                q_rhs = qoT[hs, s0:s0 + 4, bs]
                q_keys = [("q", s0 + jj, b) for jj in range(4)]
                bp, bc = sc_rot.next()
                u = i % 3
                if has_prev:
                    if b > 0:
                        kprev, kpk = kT[hs, kt, (b - 1) * 128:b * 128], ("k", kt, b - 1)
                    else:
                        kprev, kpk = carryK[hs, kt, :], ("kc",)
                    mm(ps[bp][:, :], kprev, q_rhs, True, True, [kpk] + q_keys, [("ps", bp)])
                mm(ps[bc][:, :], kT[hs, kt, bs], q_rhs, True, True, [("k", kt, b)] + q_keys, [("ps", bc)])
                if has_prev:
                    act(eT[:, u, 0, :], ps[bp][:, :], AF.Exp, [("ps", bp)], [("e", u, 0)], scale=0.125)
                act(eT[:, u, 1, :], ps[bc][:, :], AF.Exp, [("ps", bc)], [("e", u, 1)], scale=0.125)
                if has_prev:
                    tt(PT[:, u, :, :], eT[:, u, :, :], maskT[:, :, :], ALU.mult,
                       [("e", u, 0), ("e", u, 1), ("maskT",)], [("P", u)])
                else:
                    tt(PT[:, u, 1, :], eT[:, u, 1, :], maskT[:, 1, :], ALU.mult,
                       [("e", u, 1), ("maskT",)], [("P", u)])

            def stage2(i):
                b, g = iters[i]
                has_prev = (st * NB + b) > 0
                half = g % 2
                hs = slice(half * 64, half * 64 + 64)
                s0 = (g // 2) * 4
                bs = slice(b * 128, (b + 1) * 128)
                q_keys = [("q", s0 + jj, b) for jj in range(4)]
                u = i % 3
                w = i % 2
                pvb = pv_rot.next()
                if has_prev:
                    if b > 0:
                        vprev, vpk = Vx[:, b - 1, g, :], ("v", b - 1)
                    else:
                        vprev, vpk = carryV[:, g, :], ("vc",)
                    mm(ps[pvb][:, :], vprev, PT[:, u, 0, :], True, False, [vpk, ("P", u)], [("ps", pvb)])
                mm(ps[pvb][:, :], Vx[:, b, g, :], PT[:, u, 1, :], not has_prev, False,
                   [("v", b), ("P", u)], [("ps", pvb)])
                mm(ps[pvb][:, :], sel[0:1, :], esrow[0:1, g * 512:(g + 1) * 512], False, True,
                   [("sel",), ("esrow",)], [("ps", pvb)])
                act(lnd[64:128, w, :], ps[pvb][64:128, :], AF.Ln, [("ps", pvb)], [("lnd", w)])
                act(Rr[0:64, w, :], lnd[64:128, w, :], AF.Exp, [("lnd", w)], [("R", w)], scale=-1.0)
                tt(qoT[hs, s0:s0 + 4, bs], ps[pvb][0:64, :].rearrange("p (j t) -> p j t", j=4),
                   Rr[0:64, w, :].rearrange("p (j t) -> p j t", j=4), ALU.mult,
                   [("ps", pvb), ("R", w)], q_keys)

            for i in range(len(iters)):
                stage1(i)
                if i >= 1:
                    stage2(i - 1)
            stage2(len(iters) - 1)
            S.add("dve", lambda e: e.tensor_copy(out=carryK[:, :, :], in_=kT[:, :, (NB - 1) * 128:NB * 128]),
                  [("k", 0, NB - 1), ("k", 1, NB - 1)], [("kc",)])
            S.add("dve", lambda e: e.tensor_copy(out=carryV[:, :, :], in_=Vx[:, NB - 1, :, :]),
                  [("v", NB - 1)], [("vc",)])

            pn = PostNorm("gmo0", "bo", "bog")
            orot = Rot([0, 1, 2, 3])
            wo_slabs = w_get(4)

            def wo_tile(grp, c):
                wv, wk = wo_slabs[c // 2]
                mi = c % 2
                cs = slice(grp * 512, (grp + 1) * 512)
                bank = orot.next()
                for k in range(8):
                    mm(ps[bank][:, :], wv[:, (mi * 8 + k) * 128:(mi * 8 + k + 1) * 128], qoT[:, k, cs],
                       k == 0, k == 7, [wk] + [("q", k, b) for b in range(grp * 4, grp * 4 + 4)],
                       [("ps", bank)])
                return bank

            out_phase(pn, wo_tile, ("gfp0", st))
            S.fence_pe()

        def ffn(st, layer, final_store=None):
            cv = Carver()
            gT = cv.take(NF * NT, BF16).rearrange("p (f t) -> p f t", f=NF)
            sg = cv.take(2 * 512, F32).rearrange("p (u t) -> p u t", u=2)
            prenorm("gfp%d" % layer)
            grot = Rot([(0, 1), (2, 3)])
            urot = Rot([0, 1])
            def gu_tile(wv, wk, f, grp):
                cs = slice(grp * 512, (grp + 1) * 512)
                bg_, bu_ = grot.next()
                u = urot.next()
                for k in range(8):
                    mm(ps[bg_][:, :], wv[:, k * 128:(k + 1) * 128], hT[:, k, cs], k == 0, k == 7,
                       [wk, ("h", grp, k)], [("ps", bg_)])
                for k in range(8):
                    mm(ps[bu_][:, :], wv[:, (8 + k) * 128:(9 + k) * 128], hT[:, k, cs], k == 0, k == 7,
                       [wk, ("h", grp, k)], [("ps", bu_)])
                act(sg[:, u, :], ps[bg_][:, :], AF.Silu, [("ps", bg_)], [("sg", u)])
                tt(gT[:, f, cs], ps[bu_][:, :], sg[:, u, :], ALU.mult, [("ps", bu_), ("sg", u)],
                   [("g", f, grp)])

            NSK = 4
            first = w_get(NSK)
            for grp in range(2):
                for f in range(NSK):
                    gu_tile(first[f][0], first[f][1], f, grp)
                    if grp == 0 and f == 1:
                        run_deferred()
            for f in range(NSK, NF):
                wv, wk = w_next()
                for grp in range(2):
                    gu_tile(wv, wk, f, grp)
            pn = PostNorm("gfo%d" % layer)
            orot = Rot([0, 1, 2, 3])

            def dn_tile(grp, c):
                (wv0, wk0), (wv1, wk1) = w_get(2)
                cs = slice(grp * 512, (grp + 1) * 512)
                bank = orot.next()
                for f in range(NF):
                    wv, wk = (wv0, wk0) if f < 11 else (wv1, wk1)
                    kk = f % 11
                    mm(ps[bank][:, :], wv[:, kk * 128:(kk + 1) * 128], gT[:, f, cs], f == 0, f == NF - 1,
                       [wk, ("g", f, grp)], [("ps", bank)])
                return bank

            if layer == 0:
                nxt = ("gmp1", st)
            else:
                nxt = ("gmp0", st + 1) if st + 1 < n_st else None
            out_phase(pn, dn_tile, nxt, final_store, c_pre=3)
            S.fence_pe()

        def sgu(st):
            cv = Carver()
            uT = cv.take(8 * NT, BF16).rearrange("p (c t) -> p c t", c=8)
            vf = cv.take(3 * 1024, F32).rearrange("p (u t) -> p u t", u=3)
            vb = cv.take(3 * 1024, BF16).rearrange("p (u t) -> p u t", u=3)
            tmp = cv.take(2 * 512, F32).rearrange("p (u t) -> p u t", u=2)
            lng = cv.take(1024, F32)
            lnb = cv.take(1024, F32)
            bsp = cv.take(1024, F32)
            fdeps = list(S.last_fence)
            dma("sp", lng[:, :], rows_d[0:1, 272:1296].partition_broadcast(128), [], [("lng",)], sg_sems[0], fdeps)
            dma("sp", lnb[:, :], rows_d[0:1, 1296:2320].partition_broadcast(128), [], [("lnb",)], sg_sems[1], fdeps)
            dma("sp", bsp[:, :], rows_d[0:1, 2320:3344].partition_broadcast(128), [], [("bsp",)], sg_sems[2], fdeps)
            prenorm("gmp1")
            brot = Rot([0, 1, 2, 3])

            def u_tile(wv, wk, i, mi, grp):
                c = 2 * i + mi
                cs = slice(grp * 512, (grp + 1) * 512)
                bank = brot.next()
                for k in range(8):
                    mm(ps[bank][:, :], wv[:, (mi * 8 + k) * 128:(mi * 8 + k + 1) * 128], hT[:, k, cs],
                       k == 0, k == 7, [wk, ("h", grp, k)], [("ps", bank)])
                act(uT[:, c, cs], ps[bank][:, :], AF.Gelu_apprx_tanh, [("ps", bank)],
                    [("u", c, b) for b in range(grp * 4, grp * 4 + 4)])

            first = w_get(4)
            for grp in range(2):
                for i in range(4):
                    for mi in range(2):
                        u_tile(first[i][0], first[i][1], i, mi, grp)
                    if grp == 0 and i == 0:
                        run_deferred()
            vslabs = w_get(4)
            mrot = Rot([(4, 5), (6, 7)])
            trot = Rot([0, 1])

            def stageA(b):
                grp = b // 4
                bs = slice(b * 128, (b + 1) * 128)
                u = b % 3
                for vh in range(2):
                    bank = brot.next()
                    for k in range(8):
                        wv, wk = vslabs[vh * 2 + k // 4]
                        kk = k % 4
                        mm(ps[bank][:, :], hT[:, k, bs], wv[:, kk * 512:(kk + 1) * 512], k == 0, k == 7,
                           [wk, ("h", grp, k)], [("ps", bank)])
                    act(vf[:, u, vh * 512:(vh + 1) * 512], ps[bank][:, :], AF.Gelu_apprx_tanh,
                        [("ps", bank)], [("vf", u, vh)])
                o = 16 * u
                S.add("dve", lambda e, o=o, u=u: e.bn_stats(out=small[:, o:o + 6], in_=vf[:, u, 0:512]),
                      [("vf", u, 0)], [("bn", u, 0)])
                S.add("dve", lambda e, o=o, u=u: e.bn_stats(out=small[:, o + 6:o + 12], in_=vf[:, u, 512:1024]),
                      [("vf", u, 1)], [("bn", u, 1)])
                S.add("dve", lambda e, o=o: e.bn_aggr(out=small[:, o + 12:o + 14], in_=small[:, o:o + 12]),
                      [("bn", u, 0), ("bn", u, 1)], [("mv", u)])
                o2 = 48 + 4 * u
                act(small[:, o2:o2 + 1], small[:, o + 13:o + 14], AF.Ln, [("mv", u)], [("lnv", u)], bias=EPS)
                act(small[:, o2 + 1:o2 + 2], small[:, o2:o2 + 1], AF.Exp, [("lnv", u)], [("rs", u)], scale=-0.5)
                ts(small[:, o2 + 2:o2 + 3], small[:, o + 12:o + 13], -1.0, small[:, o2 + 1:o2 + 2],
                   ALU.mult, ALU.mult, [("mv", u), ("rs", u)], [("nmr", u)])
                vkeys = [("vf", u, 0), ("vf", u, 1)]
                act(vf[:, u, :], vf[:, u, :], AF.Identity, vkeys + [("rs", u), ("nmr", u)], vkeys,
                    scale=small[:, o2 + 1:o2 + 2], bias=small[:, o2 + 2:o2 + 3])
                tt(vf[:, u, :], vf[:, u, :], lng[:, :], ALU.mult, vkeys + [("lng",)], vkeys, eng="pool")
                tt(vb[:, u, :], vf[:, u, :], lnb[:, :], ALU.add, vkeys + [("lnb",)], [("vb", u)], eng="pool")

            def stageB(b):
                bs = slice(b * 128, (b + 1) * 128)
                u = b % 3
                banks = mrot.next()
                for gg in range(8):
                    bank = banks[gg // 4]
                    col = (gg % 4) * 128
                    mm(ps[bank][:, col:col + 128], vb[:, u, gg * 128:(gg + 1) * 128], WsT[:, gg, :], True, True,
                       [("vb", u), ("WsT",)], [("ps", bank)])
                for hb in range(2):
                    bank = banks[hb]
                    tu = trot.next()
                    tt(tmp[:, tu, :], ps[bank][:, :], bsp[:, hb * 512:(hb + 1) * 512], ALU.add,
                       [("ps", bank), ("bsp",)], [("tmp", tu)])
                    ukeys = [("u", 4 * hb + jj, b) for jj in range(4)]
                    tt(uT[:, 4 * hb:4 * hb + 4, bs], uT[:, 4 * hb:4 * hb + 4, bs],
                       tmp[:, tu, :].rearrange("p (j t) -> p j t", j=4), ALU.mult,
                       [("tmp", tu)] + ukeys, ukeys)

            for b in range(NB):
                stageA(b)
                if b >= 2:
                    stageB(b - 2)
            stageB(NB - 2)
            stageB(NB - 1)
            pn = PostNorm("gmo1")
            orot = Rot([0, 1, 2, 3])
            wo_slabs = w_get(4)

            def so_tile(grp, c):
                wv, wk = wo_slabs[c // 2]
                mi = c % 2
                cs = slice(grp * 512, (grp + 1) * 512)
                bank = orot.next()
                for k in range(8):
                    mm(ps[bank][:, :], wv[:, (mi * 8 + k) * 128:(mi * 8 + k + 1) * 128], uT[:, k, cs],
                       k == 0, k == 7, [wk] + [("u", k, b) for b in range(grp * 4, grp * 4 + 4)],
                       [("ps", bank)])
                return bank

            out_phase(pn, so_tile, ("gfp1", st))
            S.fence_pe()

        store_ops = []

        def x_load(st, grp):
            slot = (2 * st + grp) % 4
            t0 = st * NT + grp * 512
            dma("sp", xs[:, slot, :, :], xT_d[:, :, t0:t0 + 512].rearrange("c p t -> p c t"),
                [], [("x", slot, c) for c in range(8)], xl_sems[slot])

        def rope_load(st):
            for grp in range(2):
                t0 = st * NT + grp * 512
                dma("sp", ropeb[:, grp, :, :], rope_d[:, :, t0:t0 + 512].rearrange("a p t -> p a t"),
                    [], [("rope", grp)], rope_sems[grp])

        x_load(0, 0)
        x_load(0, 1)
        rope_load(0)
        for st in range(n_st):
            cur["st"] = st

            def final_store(grp, st=st):
                slot = (2 * st + grp) % 4
                t0 = st * NT + grp * 512
                dma("sp", yT_d[:, :, t0:t0 + 512].rearrange("c p t -> p c t"), xs[:, slot, :, :],
                    [("x", slot, c) for c in range(8)], [], st_sems[slot])
                store_ops.append(S.lastop["sp"])

            def prefetch_next():
                if st + 1 < n_st:
                    x_load(st + 1, 0)
                    x_load(st + 1, 1)
                    rope_load(st + 1)

            def last_phase():
                prefetch_next()
                ffn(st, 1, final_store)

            phases = [lambda: attention(st), lambda: ffn(st, 0), lambda: sgu(st), last_phase]
            nph = 4 if stop_after is None else stop_after
            for pi in range(nph):
                phases[pi]()
            if stop_after is not None and stop_after < 4:
                pre_done["v"] = False
                deferred.clear()
                run_post(99)
                while wstate["used"] < NSEQ * (st + 1):
                    w_next()
                prefetch_next()
                for grp in range(2):
                    final_store(grp)
        fin = S.add("sp", lambda e: None, [], [])
        fin.deps = list(store_ops)

        S.finalize(esems)

        @block.tensor
        def _(e):
            S.run("pe", e)

        @block.scalar
        def _(e):
            S.run("act", e)

        @block.vector
        def _(e):
            S.run("dve", e)

        @block.gpsimd
        def _(e):
            S.run("pool", e)

        @block.sync
        def _(e):
            S.run("sp", e)
    return nc


def _qcols(s):
    hA = 8 * (s // 4) + (s % 4)
    return np.concatenate([np.arange(hA * 64, hA * 64 + 64), np.arange((hA + 4) * 64, (hA + 4) * 64 + 64)])


def _tileB(W):
    nk = W.shape[0] // 128
    return W.reshape(nk, 128, W.shape[1]).transpose(1, 0, 2)


def _prep(inp):
    f = np.float32
    wqkv = np.asarray(inp["attn_w_qkv"], f)[0]
    bqkv = np.asarray(inp["attn_b_qkv"], f)[0]
    wo = np.asarray(inp["attn_w_o"], f)[0]
    w_in = np.asarray(inp["sgu_w_in"], f)[0]
    w_out = np.asarray(inp["sgu_w_out"], f)[0]
    wgu = np.asarray(inp["ffn_w_gate_up"], f)
    wdn = np.asarray(inp["ffn_w_down"], f)
    slabs = np.zeros((NSLAB, 128, SLAB), f)
    idx = 0

    def put(i, arr):
        a = np.ascontiguousarray(arr).reshape(128, -1)
        slabs[i, :, :a.shape[1]] = a

    def ffn_slabs(layer, idx):
        for fi in range(NF):
            g = _tileB(wgu[layer][:, fi * 128:(fi + 1) * 128])
            u = _tileB(wgu[layer][:, 2816 + fi * 128:2816 + (fi + 1) * 128])
            put(idx, np.stack([g, u], axis=1))
            idx += 1
        for c in range(8):
            t = _tileB(wdn[layer][:, c * 128:(c + 1) * 128])
            put(idx, t[:, 0:11])
            idx += 1
            put(idx, t[:, 11:22])
            idx += 1
        return idx

    for j in range(4):
        tiles = [_tileB(wqkv[:, _qcols(2 * j + ti)]) for ti in range(2)]
        put(idx, np.stack(tiles, axis=1))
        idx += 1
    tiles = [_tileB(wqkv[:, 1024 + t * 128:1024 + (t + 1) * 128]) for t in range(2)]
    put(idx, np.stack(tiles, axis=1))
    idx += 1
    put(idx, wqkv[:, 1280:1536].reshape(8, 128, 256).transpose(1, 0, 2))
    idx += 1
    rowperm = np.concatenate([_qcols(k) for k in range(8)])
    wo_p = wo[rowperm, :]
    for i in range(4):
        tiles = [_tileB(wo_p[:, (2 * i + mi) * 128:(2 * i + mi + 1) * 128]) for mi in range(2)]
        put(idx, np.stack(tiles, axis=1))
        idx += 1
    idx = ffn_slabs(0, idx)
    for i in range(4):
        tiles = [_tileB(w_in[:, (2 * i + mi) * 128:(2 * i + mi + 1) * 128]) for mi in range(2)]
        put(idx, np.stack(tiles, axis=1))
        idx += 1
    for vh in range(2):
        for kh in range(2):
            blk = w_in[kh * 512:(kh + 1) * 512, 1024 + vh * 512:1024 + (vh + 1) * 512]
            put(idx, blk.reshape(4, 128, 512).transpose(1, 0, 2))
            idx += 1
    for i in range(4):
        tiles = [_tileB(w_out[:, (2 * i + mi) * 128:(2 * i + mi + 1) * 128]) for mi in range(2)]
        put(idx, np.stack(tiles, axis=1))
        idx += 1
    idx = ffn_slabs(1, idx)
    assert idx == NSLAB

    vecs = np.zeros((128, NV), f)

    def putv(name, vec, n):
        vecs[:, VC[name]:VC[name] + n] = np.asarray(vec, f).reshape(n, 128).T

    for i in range(2):
        putv("gmp%d" % i, inp["norm_mix_pre"][i], 8)
        putv("gmo%d" % i, inp["norm_mix_post"][i], 8)
        putv("gfp%d" % i, inp["norm_ffn_pre"][i], 8)
        putv("gfo%d" % i, inp["norm_ffn_post"][i], 8)
    putv("bo", inp["attn_b_o"][0], 8)
    swp = np.arange(128) ^ 32
    for s in range(8):
        cols = _qcols(s)
        vecs[:, VC["bq"] + s] = bqkv[cols]
        vecs[:, VC["bqs"] + s] = bqkv[cols[swp]]
    for t in range(2):
        cols = 1024 + t * 128 + np.arange(128)
        vecs[:, VC["bk"] + t] = bqkv[cols]
        vecs[:, VC["bks"] + t] = bqkv[cols[swp]]

    rows = np.zeros((1, NROW), f)
    rows[0, 0:256] = bqkv[1280:1536]
    rows[0, 256:272] = np.asarray(inp["attn_sinks"], f)[0]
    rows[0, 272:1296] = np.asarray(inp["sgu_ln_g"], f)[0]
    rows[0, 1296:2320] = np.asarray(inp["sgu_ln_b"], f)[0]
    rows[0, 2320:3344] = np.asarray(inp["sgu_b_spatial"], f)[0].reshape(-1)
    wsp = np.asarray(inp["sgu_w_spatial"], f)[0]
    wsT = np.ascontiguousarray(wsp.transpose(2, 0, 1)).reshape(128, 1024)

    half = 32
    inv_freq = 10000.0 ** (-(np.arange(half, dtype=np.float64) * 2.0) / 64.0)
    pos = np.arange(4096, dtype=np.float64)
    ang = pos[None, :] * inv_freq[:, None]
    cos = np.cos(ang).astype(f)
    sin = np.sin(ang).astype(f)
    p = np.arange(128)
    rope = np.zeros((2, 128, 4096), f)
    rope[0] = cos[p % 32]
    sgn = np.where((p % 64) < 32, -1.0, 1.0).astype(f)
    rope[1] = sin[p % 32] * sgn[:, None]
    cm = np.zeros((128, 1024 + 128), f)
    j = np.arange(128)[:, None]
    i = np.arange(128)[None, :]
    prev = (j > i).astype(f)
    cur = (j <= i).astype(f)
    cm[:, 0:512] = np.tile(prev, (1, 4))
    cm[:, 512:1024] = np.tile(cur, (1, 4))
    cm[:, 1024:1152] = cur
    return dict(wst=slabs, vecs=vecs, rows=rows, wsT=wsT, rope=rope, cmask=cm)


_CACHE = {}


def _run(inputs, n_st, stop_after=None, n_cores=8):
    x = np.asarray(inputs["x"], np.float32)
    shared = _prep(inputs)
    TOK = n_st * NT
    key = (n_st, stop_after)
    if key not in _CACHE:
        _CACHE[key] = build(n_st, stop_after)
    nc = _CACHE[key]
    in_maps = []
    for b in range(n_cores):
        xT = np.ascontiguousarray(x[b, :TOK, :].T).reshape(8, 128, TOK)
        m = dict(shared)
        m["rope"] = np.ascontiguousarray(shared["rope"][:, :, :TOK])
        m["xT"] = xT
        in_maps.append(m)
    res = run_bass_kernel_spmd(nc, in_maps, core_ids=list(range(n_cores)))
    out = np.empty((n_cores, TOK, 1024), np.float32)
    for b in range(n_cores):
        out[b] = res.results[b]["yT"].reshape(1024, TOK).T
    return out


def kernel(**inputs):
    return _run(inputs, 4)
```

```python
import numpy as np
import concourse.bass as bass
import concourse.mybir as mybir
from concourse.bass_utils import run_bass_kernel_spmd

F32 = mybir.dt.float32
BF16 = mybir.dt.bfloat16
AF = mybir.ActivationFunctionType
ALU = mybir.AluOpType

NT = 1024
NB = 8
R = 6
SLAB = 2048
NF = 22
EPS = 1e-6
NSLAB = 98
NROW = 256 + 16 + 3 * 1024

VC = {}
_c = 0
for _n, _w in [("gmp0", 8), ("gmo0", 8), ("gfp0", 8), ("gfo0", 8),
               ("gmp1", 8), ("gmo1", 8), ("gfp1", 8), ("gfo1", 8),
               ("bo", 8), ("bq", 8), ("bqs", 8), ("bk", 2), ("bks", 2), ("bog", 8)]:
    VC[_n] = _c
    _c += _w
NV = _c


class Sem:
    def __init__(self, h):
        self.h = h
        self.count = 0


class Op:
    __slots__ = ("eng", "fn", "deps", "tok", "used", "sem")


class Sched:
    def __init__(self):
        self.ops = {e: [] for e in ("pe", "act", "dve", "pool", "sp")}
        self.lastw = {}
        self.rd = {}
        self.pending = {}
        self.lastop = {}

    def add(self, eng, fn, reads=(), writes=(), dma_sem=None):
        op = Op()
        op.eng = eng
        op.fn = fn
        op.used = False
        op.sem = dma_sem
        op.tok = None
        deps = []
        for k in reads:
            w = self.lastw.get(k)
            if w is not None:
                deps.append(w)
        for k in writes:
            w = self.lastw.get(k)
            if w is not None:
                deps.append(w)
            r = self.rd.get(k)
            if r:
                deps.extend(r.values())
        p = self.pending.pop(eng, None)
        if p:
            deps.extend(p)
        op.deps = list(set(deps))
        rkey = eng if dma_sem is None else id(op)
        for k in reads:
            self.rd.setdefault(k, {})[rkey] = op
        for k in writes:
            self.lastw[k] = op
            self.rd[k] = {}
        if dma_sem is not None:
            dma_sem.count += 16
            op.tok = (dma_sem, dma_sem.count)
        self.ops[eng].append(op)
        self.lastop[eng] = op
        if eng == "pool" and dma_sem is None:
            self.lastop["poolc"] = op
        return op

    def fence(self):
        snap = [self.lastop[e] for e in ("pe", "act", "dve", "poolc") if e in self.lastop]
        self.last_fence = list(snap)
        for e in ("pe", "act", "dve", "pool"):
            self.pending[e] = list(snap)

    def fence_pe(self):
        snap = [self.lastop["pe"]]
        self.last_fence = list(snap)
        for e in ("act", "dve", "pool"):
            self.pending[e] = list(snap)

    def finalize(self, esems):
        for e, lst in self.ops.items():
            for op in lst:
                for d in op.deps:
                    if d.eng == "pe" and e == "pe" and d.sem is None:
                        continue
                    d.used = True
        for e, lst in self.ops.items():
            cnt = 0
            for op in lst:
                if op.sem is None and op.used:
                    cnt += 1
                    op.tok = (esems[e], cnt)

    def run(self, e, eng):
        waited = {}
        for op in self.ops[e]:
            need = {}
            for d in op.deps:
                if d.eng == "pe" and e == "pe" and d.sem is None:
                    continue
                s, v = d.tok
                if need.get(s, 0) < v:
                    need[s] = v
            for s, v in need.items():
                if waited.get(s, 0) < v:
                    eng.wait_ge(s.h, v)
                    waited[s] = v
            ins = op.fn(eng)
            if ins is None:
                continue
            if op.sem is not None:
                ins.then_inc(op.sem.h, 16)
            elif op.used:
                ins.then_inc(op.tok[0].h, 1)


class Rot:
    def __init__(self, items):
        self.items = list(items)
        self.i = 0

    def next(self):
        v = self.items[self.i % len(self.items)]
        self.i += 1
        return v


def build(n_st, stop_after=None):
    nc = bass.Bass("TRN2", target_bir_lowering=False)
    TOK = n_st * NT
    xT_d = nc.dram_tensor("xT", [8, 128, TOK], F32, kind="ExternalInput").ap()
    wst_d = nc.dram_tensor("wst", [NSLAB, 128, SLAB], F32, kind="ExternalInput").ap()
    vecs_d = nc.dram_tensor("vecs", [128, NV], F32, kind="ExternalInput").ap()
    rope_d = nc.dram_tensor("rope", [2, 128, TOK], F32, kind="ExternalInput").ap()
    rows_d = nc.dram_tensor("rows", [1, NROW], F32, kind="ExternalInput").ap()
    wsT_d = nc.dram_tensor("wsT", [128, 1024], F32, kind="ExternalInput").ap()
    cm_d = nc.dram_tensor("cmask", [128, 1024 + 128], F32, kind="ExternalInput").ap()
    yT_d = nc.dram_tensor("yT", [8, 128, TOK], F32, kind="ExternalOutput").ap()
    scr_d = nc.dram_tensor("wscr", [NSLAB, 128, SLAB], BF16, kind="Internal").ap()

    ARENA = 30720
    import contextlib
    with contextlib.ExitStack() as es:
        def sb(name, shape, dt):
            return es.enter_context(nc.sbuf_tensor(name, shape, dt))

        def sem(name):
            return Sem(es.enter_context(nc.semaphore(name)))

        xs = sb("xT_sb", [128, 4, 8, 512], F32)
        hm = sb("hm", [128, 8192], F32)
        sq = sb("sq", [128, 3, 512], BF16)
        rstd = sb("rstd", [128, 2, 512], F32)
        ring = sb("ring", [128, R, SLAB], BF16)
        ropeb = sb("ropeb", [128, 2, 2, 512], F32)
        vecs = sb("vecs_sb", [128, NV], F32)
        sinks = sb("sinks", [128, 16], F32)
        esink = sb("esink", [128, 16], F32)
        bvt = sb("bvt", [128, 256], F32)
        maskT = sb("maskT", [128, 2, 512], BF16)
        ones = sb("ones", [128, 128], BF16)
        sel = sb("sel", [1, 128], BF16)
        esrow = sb("esrow", [1, 2048], BF16)
        carryK = sb("carryK", [128, 2, 128], BF16)
        carryV = sb("carryV", [128, 4, 128], BF16)
        WsT = sb("WsT", [128, 8, 128], BF16)
        small = sb("small", [128, 96], F32)
        arena = sb("arena", [128, ARENA], BF16)
        ps = [es.enter_context(nc.psum_tensor("ps%d" % i, [128, 512], F32)) for i in range(8)]

        esems = {e: sem("s_" + e) for e in ("pe", "act", "dve", "pool", "sp")}
        ring_sems = [sem("ring%d" % i) for i in range(R)]
        scr_sems = [sem("scr%d" % i) for i in range(R)]
        xl_sems = [sem("xl%d" % i) for i in range(4)]
        st_sems = [sem("store%d" % i) for i in range(4)]
        rope_sems = [sem("rope%d" % i) for i in range(2)]
        c_sems = [sem("const%d" % i) for i in range(10)]
        sg_sems = [sem("sguc%d" % i) for i in range(3)]

        block = es.enter_context(nc.Block())
        S = Sched()

        hT = hm[:, 0:4096].bitcast(BF16).rearrange("p (c t) -> p c t", c=8)
        mT = hm[:].rearrange("p (g c t) -> p g c t", g=2, c=8)

        class Carver:
            def __init__(self):
                self.off = 0

            def take(self, nelem, dt):
                nb = nelem * (4 if dt == F32 else 2)
                a = self.off // 2
                self.off += nb
                assert self.off <= ARENA * 2, "arena overflow"
                v = arena[:, a:a + nb // 2]
                return v.bitcast(F32) if dt == F32 else v

        def mm(out, lhsT, rhs, start, stop, reads, writes):
            S.add("pe", lambda e, o=out, l=lhsT, r=rhs, a=start, b=stop: e.matmul(o, l, r, start=a, stop=b),
                  reads, writes)

        def act(out, in_, func, reads, writes, scale=None, bias=None):
            kw = {}
            if scale is not None:
                kw["scale"] = scale
            if bias is not None:
                kw["bias"] = bias
            S.add("act", lambda e, o=out, i=in_, f=func, kw=kw: e.activation(out=o, in_=i, func=f, **kw),
                  reads, writes)

        def tt(out, in0, in1, op, reads, writes, eng="dve"):
            S.add(eng, lambda e, o=out, a=in0, b=in1, p=op: e.tensor_tensor(out=o, in0=a, in1=b, op=p),
                  reads, writes)

        def stt(out, in0, scalar, in1, op0, op1, reads, writes):
            S.add("dve", lambda e, o=out, a=in0, s=scalar, b=in1, p0=op0, p1=op1:
                  e.scalar_tensor_tensor(out=o, in0=a, scalar=s, in1=b, op0=p0, op1=p1), reads, writes)

        def ts(out, in0, s1, s2, op0, op1, reads, writes):
            S.add("dve", lambda e, o=out, a=in0, x=s1, y=s2, p0=op0, p1=op1:
                  e.tensor_scalar(out=o, in0=a, scalar1=x, scalar2=y, op0=p0, op1=p1), reads, writes)

        def dma(q, out, in_, reads, writes, s, extra_deps=None):
            op = S.add(q, lambda e, o=out, i=in_: e.dma_start(out=o, in_=i), reads, writes, dma_sem=s)
            if extra_deps:
                op.deps = list(set(op.deps) | set(extra_deps))
            return op

        vcol = lambda name, c: vecs[:, VC[name] + c:VC[name] + c + 1]

        wstate = {"issued": 0, "used": 0}
        SEQ = (list(range(0, 10)) + list(range(10, 32)) + list(range(32, 48)) * 2 +
               list(range(48, 60)) + list(range(60, 82)) + list(range(82, 98)) * 2)
        NSEQ = len(SEQ)
        TOTAL_SLABS = NSEQ * n_st
        scr_stored = set()

        def slab_width(j):
            if (32 <= j < 48) or (82 <= j < 98):
                return 11 * 128
            return SLAB

        def w_issue():
            i = wstate["issued"]
            if i >= TOTAL_SLABS:
                return
            slot = i % R
            j = SEQ[i % NSEQ]
            wd = slab_width(j)
            if j not in scr_stored:
                dma("pool", ring[:, slot, 0:wd], wst_d[j, :, 0:wd], [], [("w", slot)], ring_sems[slot])
            else:
                dma("pool", ring[:, slot, 0:wd], scr_d[j, :, 0:wd], [("scr", j)], [("w", slot)], ring_sems[slot])
            wstate["issued"] += 1

        def w_get(count):
            n = wstate["used"]
            while wstate["issued"] < min(n + R, TOTAL_SLABS):
                w_issue()
            wstate["used"] += count
            for i in range(n, n + count):
                j = SEQ[i % NSEQ]
                if j not in scr_stored:
                    wd = slab_width(j)
                    dma("sp", scr_d[j, :, 0:wd], ring[:, i % R, 0:wd], [("w", i % R)], [("scr", j)],
                        scr_sems[i % R])
                    scr_stored.add(j)
            return [(ring[:, (n + i) % R, :], ("w", (n + i) % R)) for i in range(count)]

        def w_next():
            return w_get(1)[0]

        dma("sp", vecs[:, :], vecs_d[:, :], [], [("vecs",)], c_sems[0])
        dma("sp", bvt[:, :], rows_d[0:1, 0:256].partition_broadcast(128), [], [("bvt",)], c_sems[1])
        dma("sp", sinks[:, :], rows_d[0:1, 256:272].partition_broadcast(128), [], [("sinks",)], c_sems[2])
        _cv0 = Carver()
        wstage = _cv0.take(1024, F32)
        tril = _cv0.take(128, F32)
        dma("sp", wstage[:, :], wsT_d[:, :], [], [("wstage",)], c_sems[6])
        dma("sp", tril[:, :], cm_d[:, 1024:1152], [], [("tril",)], c_sems[7])
        dma("pool", maskT[:].rearrange("p a b -> p (a b)"), cm_d[:, 0:1024], [], [("maskT",)], c_sems[8])
        for _ in range(R):
            w_issue()
        S.add("dve", lambda e: e.memset(ones[:], 1.0), [], [("ones",)])
        act(esink[:, :], sinks[:, :], AF.Exp, [("sinks",)], [("esink",)])
        S.add("dve", lambda e: e.memset(sel[0:1, 0:64], 0.0), [], [("sel",)])
        S.add("dve", lambda e: e.memset(sel[0:1, 64:128], 1.0), [], [("sel",)])
        for h in range(16):
            ts(esrow[0:1, h * 128:(h + 1) * 128], ones[0:1, 0:128], esink[0:1, h:h + 1], None, ALU.mult, ALU.bypass,
               [("ones",), ("esink",)], [("esrow",)])
        for g in range(8):
            tt(WsT[:, g, :], wstage[:, g * 128:(g + 1) * 128], tril[:, :], ALU.mult,
               [("wstage",), ("tril",)], [("WsT",)])
        tt(vecs[:, VC["bog"]:VC["bog"] + 8], vecs[:, VC["bo"]:VC["bo"] + 8], vecs[:, VC["gmo0"]:VC["gmo0"] + 8],
           ALU.mult, [("vecs",)], [("vecs",)])
        S.fence()

        sq_rot = Rot([0, 1, 2])
        cur = {"st": 0}

        def xslot(grp):
            return (2 * cur["st"] + grp) % 4

        def X(grp, c):
            return xs[:, xslot(grp), c, :]

        def xkey(grp, c):
            return ("x", xslot(grp), c)

        pre_done = {"v": False}
        deferred = []

        post_q = []

        def run_post(n):
            for _ in range(min(n, len(post_q))):
                post_q.pop(0)()

        def run_deferred():
            run_post(99)
            while deferred:
                deferred.pop(0)()

        def prenorm_group(gname, grp, st=None, bank=None):
            if st is None:
                st = cur["st"]
            if bank is None:
                bank = 6 + grp
            slot = (2 * st + grp) % 4
            cs = slice(grp * 512, (grp + 1) * 512)
            for c in range(8):
                r = sq_rot.next()
                act(sq[:, r, :], xs[:, slot, c, :], AF.Square, [("x", slot, c)], [("sq", r)])
                mm(ps[bank][:, :], ones[:, :], sq[:, r, :], c == 0, c == 7,
                   [("ones",), ("sq", r)], [("ps", bank)])
            act(rstd[:, grp, :], ps[bank][:, :], AF.Ln, [("ps", bank)], [("rstd", grp)],
                scale=1.0 / 1024.0, bias=EPS)
            act(rstd[:, grp, :], rstd[:, grp, :], AF.Exp, [("rstd", grp)], [("rstd", grp)], scale=-0.5)
            for c in range(8):
                stt(hT[:, c, cs], xs[:, slot, c, :], vcol(gname, c), rstd[:, grp, :], ALU.mult, ALU.mult,
                    [("x", slot, c), ("rstd", grp), ("vecs",)], [("h", grp, c), ("m", 0, c)])

        class PreTask:
            def __init__(self, gname, grp, st, bank):
                self.gname, self.grp, self.st, self.bank = gname, grp, st, bank
                self.slot = (2 * st + grp) % 4

            def square(self, c):
                r = sq_rot.next()
                act(sq[:, r, :], xs[:, self.slot, c, :], AF.Square, [("x", self.slot, c)], [("sq", r)])
                return r

            def stats(self, c, r):
                mm(ps[self.bank][:, :], ones[:, :], sq[:, r, :], c == 0, c == 7,
                   [("ones",), ("sq", r)], [("ps", self.bank)])

            def fin(self):
                grp = self.grp
                cs = slice(grp * 512, (grp + 1) * 512)
                act(rstd[:, grp, :], ps[self.bank][:, :], AF.Ln, [("ps", self.bank)], [("rstd", grp)],
                    scale=1.0 / 1024.0, bias=EPS)
                act(rstd[:, grp, :], rstd[:, grp, :], AF.Exp, [("rstd", grp)], [("rstd", grp)], scale=-0.5)
                for c in range(8):
                    stt(hT[:, c, cs], xs[:, self.slot, c, :], vcol(self.gname, c), rstd[:, grp, :],
                        ALU.mult, ALU.mult, [("x", self.slot, c), ("rstd", grp), ("vecs",)],
                        [("h", grp, c), ("m", 0, c)])

        def prenorm(gname):
            if pre_done["v"]:
                pre_done["v"] = False
                return
            for grp in range(2):
                prenorm_group(gname, grp)

        class PostNorm:
            def __init__(self, gname, bias_name=None, bg_name=None):
                self.gname = gname
                self.bias_name = bias_name
                self.bg_name = bg_name
                self.pending = None
                self.count = [0, 0]

            def tile(self, bank, grp, c):
                self.flush()
                r = sq_rot.next()
                b_sq = vcol(self.bias_name, c) if self.bias_name else None
                b_mg = vcol(self.bg_name, c) if self.bg_name else None
                act(sq[:, r, :], ps[bank][:, :], AF.Square, [("ps", bank), ("vecs",)], [("sq", r)], bias=b_sq)
                mkeys = [("m", grp, c)] + ([("h", 0, c), ("h", 1, c)] if grp == 0 else [])
                act(mT[:, grp, c, :], ps[bank][:, :], AF.Identity, [("ps", bank), ("vecs",)], mkeys,
                    scale=vcol(self.gname, c), bias=b_mg)
                self.pending = (r, grp)

            def flush(self):
                if self.pending is None:
                    return
                r, grp = self.pending
                n = self.count[grp]
                mm(ps[6 + grp][:, :], ones[:, :], sq[:, r, :], n == 0, n == 7,
                   [("ones",), ("sq", r)], [("ps", 6 + grp)])
                self.count[grp] += 1
                self.pending = None

            def post_group(self, grp, final_store=None, defer=False):
                act(rstd[:, grp, :], ps[6 + grp][:, :], AF.Ln, [("ps", 6 + grp)], [("rstd", grp)],
                    scale=1.0 / 1024.0, bias=EPS)
                act(rstd[:, grp, :], rstd[:, grp, :], AF.Exp, [("rstd", grp)], [("rstd", grp)], scale=-0.5)
                st_now = cur["st"]

                def chunk(c):
                    slot = (2 * st_now + grp) % 4
                    xa = xs[:, slot, c, :]
                    tt(mT[:, grp, c, :], mT[:, grp, c, :], rstd[:, grp, :], ALU.mult,
                       [("m", grp, c), ("rstd", grp)], [("m", grp, c)])
                    tt(xa, xa, mT[:, grp, c, :], ALU.add, [("x", slot, c), ("m", grp, c)], [("x", slot, c)])
                    if c == 7 and final_store is not None:
                        final_store(grp)

                for c in range(8):
                    if defer:
                        post_q.append(lambda c=c: chunk(c))
                    else:
                        chunk(c)

        def out_phase(pn, tile_fn, nxt, final_store=None, c_pre=4):
            for c in range(8):
                pn.tile(tile_fn(0, c), 0, c)
            pn.tile(tile_fn(1, 0), 1, 0)
            pn.post_group(0, final_store)
            early = nxt is not None and nxt[1] != cur["st"]
            tasks, plan = [], {}
            if nxt is not None:
                t0 = 1 if early else 2
                tasks.append(PreTask(nxt[0], 0, nxt[1], 6))
                for i in range(4):
                    plan.setdefault(t0 + i, []).extend([(0, 2 * i), (0, 2 * i + 1)])
                if early:
                    tasks.append(PreTask(nxt[0], 1, nxt[1], 6))
                    for i in range(3):
                        plan.setdefault(5 + i, []).extend([(1, 2 * i), (1, 2 * i + 1)])
            for t in range(1, 8):
                issued = [(ti, c, tasks[ti].square(c)) for ti, c in plan.get(t, [])]
                bank = tile_fn(1, t)
                for ti, c, r in issued:
                    tasks[ti].stats(c, r)
                    if c == 7:
                        tasks[ti].fin()
                pn.tile(bank, 1, t)
            pn.flush()
            if early:
                for c in (6, 7):
                    tasks[1].stats(c, tasks[1].square(c))
                tasks[1].fin()
            pn.post_group(1, final_store, defer=early)
            if nxt is not None:
                if not early:
                    deferred.append(lambda: prenorm_group(nxt[0], 1, nxt[1]))
                pre_done["v"] = True

        def attention(st):
            cv = Carver()
            qoT = cv.take(8 * NT, BF16).rearrange("p (s t) -> p s t", s=8)
            kT = cv.take(2 * NT, BF16).rearrange("p (s t) -> p s t", s=2)
            Vx = cv.take(NB * 4 * 128, BF16).rearrange("p (b g d) -> p b g d", b=NB, g=4)
            eT = cv.take(3 * 2 * 512, BF16).rearrange("p (u a t) -> p u a t", u=3, a=2)
            PT = cv.take(3 * 2 * 512, BF16).rearrange("p (u a t) -> p u a t", u=3, a=2)
            qs = cv.take(2 * 512, F32).rearrange("p (u t) -> p u t", u=2)
            t1 = cv.take(2 * 512, F32).rearrange("p (u t) -> p u t", u=2)
            t2 = cv.take(2 * 512, F32).rearrange("p (u t) -> p u t", u=2)
            lnd = cv.take(2 * 512, F32).rearrange("p (u t) -> p u t", u=2)
            Rr = cv.take(2 * 512, F32).rearrange("p (u t) -> p u t", u=2)

            prenorm("gmp0")
            S.add("dve", lambda e: e.memset(Vx[:, :, :, 64:128], 1.0), [], [("v", b) for b in range(NB)])
            if st == 0:
                S.add("dve", lambda e: e.memset(carryV[:, :, :], 0.0), [], [("vc",)])
                S.add("dve", lambda e: e.memset(carryK[:, :, :], 0.0), [], [("kc",)])

            bank_rot = Rot([0, 1, 2, 3])
            tmp_rot = Rot([0, 1])

            def qk_tile(wv, wk, ti, grp, bias_col, bias_sw_col, out_ap, out_keys):
                cs = slice(grp * 512, (grp + 1) * 512)
                bank = bank_rot.next()
                for k in range(8):
                    mm(ps[bank][:, :], wv[:, (ti * 8 + k) * 128:(ti * 8 + k + 1) * 128], hT[:, k, cs],
                       k == 0, k == 7, [wk, ("h", grp, k)], [("ps", bank)])
                u = tmp_rot.next()
                for a in range(4):
                    b = a ^ 1
                    act(qs[32 * a:32 * a + 32, u, :], ps[bank][32 * b:32 * b + 32, :], AF.Copy,
                        [("ps", bank)], [("qs", u, a)])
                stt(t1[:, u, :], ps[bank][:, :], bias_col, ropeb[:, grp, 0, :], ALU.add, ALU.mult,
                    [("ps", bank), ("rope", grp), ("vecs",)] + [("qs", u, a) for a in range(4)], [("t1", u)])
                stt(t2[:, u, :], qs[:, u, :], bias_sw_col, ropeb[:, grp, 1, :], ALU.add, ALU.mult,
                    [("qs", u, a) for a in range(4)] + [("rope", grp), ("vecs",)], [("t2", u)])
                tt(out_ap, t1[:, u, :], t2[:, u, :], ALU.add, [("t1", u), ("t2", u)], out_keys)

            def q_tile(wv, wk, j, ti, grp):
                s = 2 * j + ti
                cs = slice(grp * 512, (grp + 1) * 512)
                keys = [("q", s, b) for b in range(grp * 4, grp * 4 + 4)]
                qk_tile(wv, wk, ti, grp, vcol("bq", s), vcol("bqs", s), qoT[:, s, cs], keys)

            first = w_get(4)
            for grp in range(2):
                for j in range(4):
                    for ti in range(2):
                        q_tile(first[j][0], first[j][1], j, ti, grp)
                        run_post(2)
                    if grp == 0 and j == 0:
                        run_deferred()
            wv, wk = w_next()
            for ti in range(2):
                for grp in range(2):
                    cs = slice(grp * 512, (grp + 1) * 512)
                    keys = [("k", ti, b) for b in range(grp * 4, grp * 4 + 4)]
                    qk_tile(wv, wk, ti, grp, vcol("bk", ti), vcol("bks", ti), kT[:, ti, cs], keys)
            wv, wk = w_next()
            for b0 in range(0, NB, 4):
                vb_banks = [4, 5, 6, 7]
                grp = b0 // 4
                for k in range(8):
                    for bb in range(4):
                        b = b0 + bb
                        mm(ps[vb_banks[bb]][:, 0:256], hT[:, k, b * 128:(b + 1) * 128], wv[:, k * 256:(k + 1) * 256],
                           k == 0, k == 7, [wk, ("h", grp, k)], [("ps", vb_banks[bb])])
                for bb in range(4):
                    b = b0 + bb
                    tt(Vx[:, b, :, 0:64], ps[vb_banks[bb]][:, 0:256].rearrange("p (g d) -> p g d", g=4),
                       bvt[:, :].rearrange("p (g d) -> p g d", g=4), ALU.add,
                       [("ps", vb_banks[bb]), ("bvt",)], [("v", b)])

            sc_rot = Rot([(0, 1), (2, 3)])
            pv_rot = Rot([4, 5])
            iters = [(b, g) for b in range(NB) for g in range(4)]

            def stage1(i):
                b, g = iters[i]
                has_prev = (st * NB + b) > 0
                half = g % 2
                hs = slice(half * 64, half * 64 + 64)
                s0 = (g // 2) * 4
                kt = g // 2
                bs = slice(b * 128, (b + 1) * 128)
                q_rhs = qoT[hs, s0:s0 + 4, bs]
                q_keys = [("q", s0 + jj, b) for jj in range(4)]
                bp, bc = sc_rot.next()
                u = i % 3
                if has_prev:
                    if b > 0:
                        kprev, kpk = kT[hs, kt, (b - 1) * 128:b * 128], ("k", kt, b - 1)
                    else:
                        kprev, kpk = carryK[hs, kt, :], ("kc",)
                    mm(ps[bp][:, :], kprev, q_rhs, True, True, [kpk] + q_keys, [("ps", bp)])
                mm(ps[bc][:, :], kT[hs, kt, bs], q_rhs, True, True, [("k", kt, b)] + q_keys, [("ps", bc)])
                if has_prev:
                    act(eT[:, u, 0, :], ps[bp][:, :], AF.Exp, [("ps", bp)], [("e", u, 0)], scale=0.125)
                act(eT[:, u, 1, :], ps[bc][:, :], AF.Exp, [("ps", bc)], [("e", u, 1)], scale=0.125)
                if has_prev:
                    tt(PT[:, u, :, :], eT[:, u, :, :], maskT[:, :, :], ALU.mult,
                       [("e", u, 0), ("e", u, 1), ("maskT",)], [("P", u)])
                else:
                    tt(PT[:, u, 1, :], eT[:, u, 1, :], maskT[:, 1, :], ALU.mult,
                       [("e", u, 1), ("maskT",)], [("P", u)])

            def stage2(i):
                b, g = iters[i]
                has_prev = (st * NB + b) > 0
                half = g % 2
                hs = slice(half * 64, half * 64 + 64)
                s0 = (g // 2) * 4
                bs = slice(b * 128, (b + 1) * 128)
                q_keys = [("q", s0 + jj, b) for jj in range(4)]
                u = i % 3
                w = i % 2
                pvb = pv_rot.next()
                if has_prev:
                    if b > 0:
                        vprev, vpk = Vx[:, b - 1, g, :], ("v", b - 1)
                    else:
                        vprev, vpk = carryV[:, g, :], ("vc",)
                    mm(ps[pvb][:, :], vprev, PT[:, u, 0, :], True, False, [vpk, ("P", u)], [("ps", pvb)])
                mm(ps[pvb][:, :], Vx[:, b, g, :], PT[:, u, 1, :], not has_prev, False,
                   [("v", b), ("P", u)], [("ps", pvb)])
                mm(ps[pvb][:, :], sel[0:1, :], esrow[0:1, g * 512:(g + 1) * 512], False, True,
                   [("sel",), ("esrow",)], [("ps", pvb)])
                act(lnd[64:128, w, :], ps[pvb][64:128, :], AF.Ln, [("ps", pvb)], [("lnd", w)])
                act(Rr[0:64, w, :], lnd[64:128, w, :], AF.Exp, [("lnd", w)], [("R", w)], scale=-1.0)
                tt(qoT[hs, s0:s0 + 4, bs], ps[pvb][0:64, :].rearrange("p (j t) -> p j t", j=4),
                   Rr[0:64, w, :].rearrange("p (j t) -> p j t", j=4), ALU.mult,
                   [("ps", pvb), ("R", w)], q_keys)

            for i in range(len(iters)):
                stage1(i)
                if i >= 1:
                    stage2(i - 1)
            stage2(len(iters) - 1)
            S.add("dve", lambda e: e.tensor_copy(out=carryK[:, :, :], in_=kT[:, :, (NB - 1) * 128:NB * 128]),
                  [("k", 0, NB - 1), ("k", 1, NB - 1)], [("kc",)])
            S.add("dve", lambda e: e.tensor_copy(out=carryV[:, :, :], in_=Vx[:, NB - 1, :, :]),
                  [("v", NB - 1)], [("vc",)])

            pn = PostNorm("gmo0", "bo", "bog")
            orot = Rot([0, 1, 2, 3])
            wo_slabs = w_get(4)

            def wo_tile(grp, c):
                wv, wk = wo_slabs[c // 2]
                mi = c % 2
                cs = slice(grp * 512, (grp + 1) * 512)
                bank = orot.next()
                for k in range(8):
                    mm(ps[bank][:, :], wv[:, (mi * 8 + k) * 128:(mi * 8 + k + 1) * 128], qoT[:, k, cs],
                       k == 0, k == 7, [wk] + [("q", k, b) for b in range(grp * 4, grp * 4 + 4)],
                       [("ps", bank)])
                return bank

            out_phase(pn, wo_tile, ("gfp0", st))
            S.fence_pe()

        def ffn(st, layer, final_store=None):
            cv = Carver()
            gT = cv.take(NF * NT, BF16).rearrange("p (f t) -> p f t", f=NF)
            sg = cv.take(2 * 512, F32).rearrange("p (u t) -> p u t", u=2)
            prenorm("gfp%d" % layer)
            grot = Rot([(0, 1), (2, 3)])
            urot = Rot([0, 1])
            def gu_tile(wv, wk, f, grp):
                cs = slice(grp * 512, (grp + 1) * 512)
                bg_, bu_ = grot.next()
                u = urot.next()
                for k in range(8):
                    mm(ps[bg_][:, :], wv[:, k * 128:(k + 1) * 128], hT[:, k, cs], k == 0, k == 7,
                       [wk, ("h", grp, k)], [("ps", bg_)])
                for k in range(8):
                    mm(ps[bu_][:, :], wv[:, (8 + k) * 128:(9 + k) * 128], hT[:, k, cs], k == 0, k == 7,
                       [wk, ("h", grp, k)], [("ps", bu_)])
                act(sg[:, u, :], ps[bg_][:, :], AF.Silu, [("ps", bg_)], [("sg", u)])
                tt(gT[:, f, cs], ps[bu_][:, :], sg[:, u, :], ALU.mult, [("ps", bu_), ("sg", u)],
                   [("g", f, grp)])

            NSK = 4
            first = w_get(NSK)
            for grp in range(2):
                for f in range(NSK):
                    gu_tile(first[f][0], first[f][1], f, grp)
                    if grp == 0 and f == 1:
                        run_deferred()
            for f in range(NSK, NF):
                wv, wk = w_next()
                for grp in range(2):
                    gu_tile(wv, wk, f, grp)
            pn = PostNorm("gfo%d" % layer)
            orot = Rot([0, 1, 2, 3])

            def dn_tile(grp, c):
                (wv0, wk0), (wv1, wk1) = w_get(2)
                cs = slice(grp * 512, (grp + 1) * 512)
                bank = orot.next()
                for f in range(NF):
                    wv, wk = (wv0, wk0) if f < 11 else (wv1, wk1)
                    kk = f % 11
                    mm(ps[bank][:, :], wv[:, kk * 128:(kk + 1) * 128], gT[:, f, cs], f == 0, f == NF - 1,
                       [wk, ("g", f, grp)], [("ps", bank)])
                return bank

            if layer == 0:
                nxt = ("gmp1", st)
            else:
                nxt = ("gmp0", st + 1) if st + 1 < n_st else None
            out_phase(pn, dn_tile, nxt, final_store, c_pre=3)
            S.fence_pe()

        def sgu(st):
            cv = Carver()
            uT = cv.take(8 * NT, BF16).rearrange("p (c t) -> p c t", c=8)
            vf = cv.take(4 * 1024, F32).rearrange("p (u t) -> p u t", u=4)
            vb = cv.take(4 * 1024, BF16).rearrange("p (u t) -> p u t", u=4)
            tmp = cv.take(2 * 512, F32).rearrange("p (u t) -> p u t", u=2)
            lng = cv.take(1024, F32)
            lnb = cv.take(1024, F32)
            bsp = cv.take(1024, F32)
            fdeps = list(S.last_fence)
            dma("sp", lng[:, :], rows_d[0:1, 272:1296].partition_broadcast(128), [], [("lng",)], sg_sems[0], fdeps)
            dma("sp", lnb[:, :], rows_d[0:1, 1296:2320].partition_broadcast(128), [], [("lnb",)], sg_sems[1], fdeps)
            dma("sp", bsp[:, :], rows_d[0:1, 2320:3344].partition_broadcast(128), [], [("bsp",)], sg_sems[2], fdeps)
            prenorm("gmp1")
            brot = Rot([0, 1, 2, 3])

            def u_tile(wv, wk, i, mi, grp):
                c = 2 * i + mi
                cs = slice(grp * 512, (grp + 1) * 512)
                bank = brot.next()
                for k in range(8):
                    mm(ps[bank][:, :], wv[:, (mi * 8 + k) * 128:(mi * 8 + k + 1) * 128], hT[:, k, cs],
                       k == 0, k == 7, [wk, ("h", grp, k)], [("ps", bank)])
                act(uT[:, c, cs], ps[bank][:, :], AF.Gelu_apprx_tanh, [("ps", bank)],
                    [("u", c, b) for b in range(grp * 4, grp * 4 + 4)])

            first = w_get(4)
            for grp in range(2):
                for i in range(4):
                    for mi in range(2):
                        u_tile(first[i][0], first[i][1], i, mi, grp)
                    if grp == 0 and i == 0:
                        run_deferred()
            vslabs = w_get(4)
            mrot = Rot([(4, 5), (6, 7)])
            trot = Rot([0, 1])

            def stageA(b):
                grp = b // 4
                bs = slice(b * 128, (b + 1) * 128)
                u = b % 4
                for vh in range(2):
                    bank = brot.next()
                    for k in range(8):
                        wv, wk = vslabs[vh * 2 + k // 4]
                        kk = k % 4
                        mm(ps[bank][:, :], hT[:, k, bs], wv[:, kk * 512:(kk + 1) * 512], k == 0, k == 7,
                           [wk, ("h", grp, k)], [("ps", bank)])
                    act(vf[:, u, vh * 512:(vh + 1) * 512], ps[bank][:, :], AF.Gelu_apprx_tanh,
                        [("ps", bank)], [("vf", u, vh)])
                o = 16 * u
                S.add("dve", lambda e, o=o, u=u: e.bn_stats(out=small[:, o:o + 6], in_=vf[:, u, 0:512]),
                      [("vf", u, 0)], [("bn", u, 0)])
                S.add("dve", lambda e, o=o, u=u: e.bn_stats(out=small[:, o + 6:o + 12], in_=vf[:, u, 512:1024]),
                      [("vf", u, 1)], [("bn", u, 1)])
                S.add("dve", lambda e, o=o: e.bn_aggr(out=small[:, o + 12:o + 14], in_=small[:, o:o + 12]),
                      [("bn", u, 0), ("bn", u, 1)], [("mv", u)])
                o2 = 64 + 4 * u
                act(small[:, o2:o2 + 1], small[:, o + 13:o + 14], AF.Ln, [("mv", u)], [("lnv", u)], bias=EPS)
                act(small[:, o2 + 1:o2 + 2], small[:, o2:o2 + 1], AF.Exp, [("lnv", u)], [("rs", u)], scale=-0.5)
                ts(small[:, o2 + 2:o2 + 3], small[:, o + 12:o + 13], -1.0, small[:, o2 + 1:o2 + 2],
                   ALU.mult, ALU.mult, [("mv", u), ("rs", u)], [("nmr", u)])
                vkeys = [("vf", u, 0), ("vf", u, 1)]
                act(vf[:, u, :], vf[:, u, :], AF.Identity, vkeys + [("rs", u), ("nmr", u)], vkeys,
                    scale=small[:, o2 + 1:o2 + 2], bias=small[:, o2 + 2:o2 + 3])
                tt(vf[:, u, :], vf[:, u, :], lng[:, :], ALU.mult, vkeys + [("lng",)], vkeys, eng="pool")
                tt(vb[:, u, :], vf[:, u, :], lnb[:, :], ALU.add, vkeys + [("lnb",)], [("vb", u)], eng="pool")

            def stageB(b):
                bs = slice(b * 128, (b + 1) * 128)
                u = b % 4
                banks = mrot.next()
                for gg in range(8):
                    bank = banks[gg // 4]
                    col = (gg % 4) * 128
                    mm(ps[bank][:, col:col + 128], vb[:, u, gg * 128:(gg + 1) * 128], WsT[:, gg, :], True, True,
                       [("vb", u), ("WsT",)], [("ps", bank)])
                for hb in range(2):
                    bank = banks[hb]
                    tu = trot.next()
                    tt(tmp[:, tu, :], ps[bank][:, :], bsp[:, hb * 512:(hb + 1) * 512], ALU.add,
                       [("ps", bank), ("bsp",)], [("tmp", tu)])
                    ukeys = [("u", 4 * hb + jj, b) for jj in range(4)]
                    tt(uT[:, 4 * hb:4 * hb + 4, bs], uT[:, 4 * hb:4 * hb + 4, bs],
                       tmp[:, tu, :].rearrange("p (j t) -> p j t", j=4), ALU.mult,
                       [("tmp", tu)] + ukeys, ukeys)

            for b in range(NB):
                stageA(b)
                if b >= 3:
                    stageB(b - 3)
            stageB(NB - 3)
            stageB(NB - 2)
            stageB(NB - 1)
            pn = PostNorm("gmo1")
            orot = Rot([0, 1, 2, 3])
            wo_slabs = w_get(4)

            def so_tile(grp, c):
                wv, wk = wo_slabs[c // 2]
                mi = c % 2
                cs = slice(grp * 512, (grp + 1) * 512)
                bank = orot.next()
                for k in range(8):
                    mm(ps[bank][:, :], wv[:, (mi * 8 + k) * 128:(mi * 8 + k + 1) * 128], uT[:, k, cs],
                       k == 0, k == 7, [wk] + [("u", k, b) for b in range(grp * 4, grp * 4 + 4)],
                       [("ps", bank)])
                return bank

            out_phase(pn, so_tile, ("gfp1", st))
            S.fence_pe()

        store_ops = []

        def x_load(st, grp):
            slot = (2 * st + grp) % 4
            t0 = st * NT + grp * 512
            dma("sp", xs[:, slot, :, :], xT_d[:, :, t0:t0 + 512].rearrange("c p t -> p c t"),
                [], [("x", slot, c) for c in range(8)], xl_sems[slot])

        def rope_load(st):
            for grp in range(2):
                t0 = st * NT + grp * 512
                dma("sp", ropeb[:, grp, :, :], rope_d[:, :, t0:t0 + 512].rearrange("a p t -> p a t"),
                    [], [("rope", grp)], rope_sems[grp])

        x_load(0, 0)
        x_load(0, 1)
        rope_load(0)
        for st in range(n_st):
            cur["st"] = st

            def final_store(grp, st=st):
                slot = (2 * st + grp) % 4
                t0 = st * NT + grp * 512
                dma("sp", yT_d[:, :, t0:t0 + 512].rearrange("c p t -> p c t"), xs[:, slot, :, :],
                    [("x", slot, c) for c in range(8)], [], st_sems[slot])
                store_ops.append(S.lastop["sp"])

            def prefetch_next():
                if st + 1 < n_st:
                    x_load(st + 1, 0)
                    x_load(st + 1, 1)
                    rope_load(st + 1)

            def last_phase():
                prefetch_next()
                ffn(st, 1, final_store)

            phases = [lambda: attention(st), lambda: ffn(st, 0), lambda: sgu(st), last_phase]
            nph = 4 if stop_after is None else stop_after
            for pi in range(nph):
                phases[pi]()
            if stop_after is not None and stop_after < 4:
                pre_done["v"] = False
                deferred.clear()
                run_post(99)
                while wstate["used"] < NSEQ * (st + 1):
                    w_next()
                prefetch_next()
                for grp in range(2):
                    final_store(grp)
        fin = S.add("sp", lambda e: None, [], [])
        fin.deps = list(store_ops)

        S.finalize(esems)

        @block.tensor
        def _(e):
            S.run("pe", e)

        @block.scalar
        def _(e):
            S.run("act", e)

        @block.vector
        def _(e):
            S.run("dve", e)

        @block.gpsimd
        def _(e):
            S.run("pool", e)

        @block.sync
        def _(e):
            S.run("sp", e)
    return nc


def _qcols(s):
    hA = 8 * (s // 4) + (s % 4)
    return np.concatenate([np.arange(hA * 64, hA * 64 + 64), np.arange((hA + 4) * 64, (hA + 4) * 64 + 64)])


def _tileB(W):
    nk = W.shape[0] // 128
    return W.reshape(nk, 128, W.shape[1]).transpose(1, 0, 2)


def _prep(inp):
    f = np.float32
    wqkv = np.asarray(inp["attn_w_qkv"], f)[0]
    bqkv = np.asarray(inp["attn_b_qkv"], f)[0]
    wo = np.asarray(inp["attn_w_o"], f)[0]
    w_in = np.asarray(inp["sgu_w_in"], f)[0]
    w_out = np.asarray(inp["sgu_w_out"], f)[0]
    wgu = np.asarray(inp["ffn_w_gate_up"], f)
    wdn = np.asarray(inp["ffn_w_down"], f)
    slabs = np.zeros((NSLAB, 128, SLAB), f)
    idx = 0

    def put(i, arr):
        a = np.ascontiguousarray(arr).reshape(128, -1)
        slabs[i, :, :a.shape[1]] = a

    def ffn_slabs(layer, idx):
        for fi in range(NF):
            g = _tileB(wgu[layer][:, fi * 128:(fi + 1) * 128])
            u = _tileB(wgu[layer][:, 2816 + fi * 128:2816 + (fi + 1) * 128])
            put(idx, np.stack([g, u], axis=1))
            idx += 1
        for c in range(8):
            t = _tileB(wdn[layer][:, c * 128:(c + 1) * 128])
            put(idx, t[:, 0:11])
            idx += 1
            put(idx, t[:, 11:22])
            idx += 1
        return idx

    for j in range(4):
        tiles = [_tileB(wqkv[:, _qcols(2 * j + ti)]) for ti in range(2)]
        put(idx, np.stack(tiles, axis=1))
        idx += 1
    tiles = [_tileB(wqkv[:, 1024 + t * 128:1024 + (t + 1) * 128]) for t in range(2)]
    put(idx, np.stack(tiles, axis=1))
    idx += 1
    put(idx, wqkv[:, 1280:1536].reshape(8, 128, 256).transpose(1, 0, 2))
    idx += 1
    rowperm = np.concatenate([_qcols(k) for k in range(8)])
    wo_p = wo[rowperm, :]
    for i in range(4):
        tiles = [_tileB(wo_p[:, (2 * i + mi) * 128:(2 * i + mi + 1) * 128]) for mi in range(2)]
        put(idx, np.stack(tiles, axis=1))
        idx += 1
    idx = ffn_slabs(0, idx)
    for i in range(4):
        tiles = [_tileB(w_in[:, (2 * i + mi) * 128:(2 * i + mi + 1) * 128]) for mi in range(2)]
        put(idx, np.stack(tiles, axis=1))
        idx += 1
    for vh in range(2):
        for kh in range(2):
            blk = w_in[kh * 512:(kh + 1) * 512, 1024 + vh * 512:1024 + (vh + 1) * 512]
            put(idx, blk.reshape(4, 128, 512).transpose(1, 0, 2))
            idx += 1
    for i in range(4):
        tiles = [_tileB(w_out[:, (2 * i + mi) * 128:(2 * i + mi + 1) * 128]) for mi in range(2)]
        put(idx, np.stack(tiles, axis=1))
        idx += 1
    idx = ffn_slabs(1, idx)
    assert idx == NSLAB

    vecs = np.zeros((128, NV), f)

    def putv(name, vec, n):
        vecs[:, VC[name]:VC[name] + n] = np.asarray(vec, f).reshape(n, 128).T

    for i in range(2):
        putv("gmp%d" % i, inp["norm_mix_pre"][i], 8)
        putv("gmo%d" % i, inp["norm_mix_post"][i], 8)
        putv("gfp%d" % i, inp["norm_ffn_pre"][i], 8)
        putv("gfo%d" % i, inp["norm_ffn_post"][i], 8)
    putv("bo", inp["attn_b_o"][0], 8)
    swp = np.arange(128) ^ 32
    for s in range(8):
        cols = _qcols(s)
        vecs[:, VC["bq"] + s] = bqkv[cols]
        vecs[:, VC["bqs"] + s] = bqkv[cols[swp]]
    for t in range(2):
        cols = 1024 + t * 128 + np.arange(128)
        vecs[:, VC["bk"] + t] = bqkv[cols]
        vecs[:, VC["bks"] + t] = bqkv[cols[swp]]

    rows = np.zeros((1, NROW), f)
    rows[0, 0:256] = bqkv[1280:1536]
    rows[0, 256:272] = np.asarray(inp["attn_sinks"], f)[0]
    rows[0, 272:1296] = np.asarray(inp["sgu_ln_g"], f)[0]
    rows[0, 1296:2320] = np.asarray(inp["sgu_ln_b"], f)[0]
    rows[0, 2320:3344] = np.asarray(inp["sgu_b_spatial"], f)[0].reshape(-1)
    wsp = np.asarray(inp["sgu_w_spatial"], f)[0]
    wsT = np.ascontiguousarray(wsp.transpose(2, 0, 1)).reshape(128, 1024)

    half = 32
    inv_freq = 10000.0 ** (-(np.arange(half, dtype=np.float64) * 2.0) / 64.0)
    pos = np.arange(4096, dtype=np.float64)
    ang = pos[None, :] * inv_freq[:, None]
    cos = np.cos(ang).astype(f)
    sin = np.sin(ang).astype(f)
    p = np.arange(128)
    rope = np.zeros((2, 128, 4096), f)
    rope[0] = cos[p % 32]
    sgn = np.where((p % 64) < 32, -1.0, 1.0).astype(f)
    rope[1] = sin[p % 32] * sgn[:, None]
    cm = np.zeros((128, 1024 + 128), f)
    j = np.arange(128)[:, None]
    i = np.arange(128)[None, :]
    prev = (j > i).astype(f)
    cur = (j <= i).astype(f)
    cm[:, 0:512] = np.tile(prev, (1, 4))
    cm[:, 512:1024] = np.tile(cur, (1, 4))
    cm[:, 1024:1152] = cur
    return dict(wst=slabs, vecs=vecs, rows=rows, wsT=wsT, rope=rope, cmask=cm)


_CACHE = {}


def _run(inputs, n_st, stop_after=None, n_cores=8):
    x = np.asarray(inputs["x"], np.float32)
    shared = _prep(inputs)
    TOK = n_st * NT
    key = (n_st, stop_after)
    if key not in _CACHE:
        _CACHE[key] = build(n_st, stop_after)
    nc = _CACHE[key]
    in_maps = []
    for b in range(n_cores):
        xT = np.ascontiguousarray(x[b, :TOK, :].T).reshape(8, 128, TOK)
        m = dict(shared)
        m["rope"] = np.ascontiguousarray(shared["rope"][:, :, :TOK])
        m["xT"] = xT
        in_maps.append(m)
    res = run_bass_kernel_spmd(nc, in_maps, core_ids=list(range(n_cores)))
    out = np.empty((n_cores, TOK, 1024), np.float32)
    for b in range(n_cores):
        out[b] = res.results[b]["yT"].reshape(1024, TOK).T
    return out


def kernel(**inputs):
    return _run(inputs, 4)
```

```python
import numpy as np
import concourse.bass as bass
import concourse.mybir as mybir
from concourse.bass_utils import run_bass_kernel_spmd

F32 = mybir.dt.float32
BF16 = mybir.dt.bfloat16
AF = mybir.ActivationFunctionType
ALU = mybir.AluOpType

NT = 1024
NB = 8
R = 6
SLAB = 2048
NF = 22
EPS = 1e-6
NSLAB = 98
NROW = 256 + 16 + 3 * 1024

VC = {}
_c = 0
for _n, _w in [("gmp0", 8), ("gmo0", 8), ("gfp0", 8), ("gfo0", 8),
               ("gmp1", 8), ("gmo1", 8), ("gfp1", 8), ("gfo1", 8),
               ("bo", 8), ("bq", 8), ("bqs", 8), ("bk", 2), ("bks", 2), ("bog", 8)]:
    VC[_n] = _c
    _c += _w
NV = _c


class Sem:
    def __init__(self, h):
        self.h = h
        self.count = 0


class Op:
    __slots__ = ("eng", "fn", "deps", "tok", "used", "sem")


class Sched:
    def __init__(self):
        self.ops = {e: [] for e in ("pe", "act", "dve", "pool", "sp")}
        self.lastw = {}
        self.rd = {}
        self.pending = {}
        self.lastop = {}

    def add(self, eng, fn, reads=(), writes=(), dma_sem=None):
        op = Op()
        op.eng = eng
        op.fn = fn
        op.used = False
        op.sem = dma_sem
        op.tok = None
        deps = []
        for k in reads:
            w = self.lastw.get(k)
            if w is not None:
                deps.append(w)
        for k in writes:
            w = self.lastw.get(k)
            if w is not None:
                deps.append(w)
            r = self.rd.get(k)
            if r:
                deps.extend(r.values())
        p = self.pending.pop(eng, None)
        if p:
            deps.extend(p)
        op.deps = list(set(deps))
        rkey = eng if dma_sem is None else id(op)
        for k in reads:
            self.rd.setdefault(k, {})[rkey] = op
        for k in writes:
            self.lastw[k] = op
            self.rd[k] = {}
        if dma_sem is not None:
            dma_sem.count += 16
            op.tok = (dma_sem, dma_sem.count)
        self.ops[eng].append(op)
        self.lastop[eng] = op
        if eng == "pool" and dma_sem is None:
            self.lastop["poolc"] = op
        return op

    def fence(self):
        snap = [self.lastop[e] for e in ("pe", "act", "dve", "poolc") if e in self.lastop]
        self.last_fence = list(snap)
        for e in ("pe", "act", "dve", "pool"):
            self.pending[e] = list(snap)

    def fence_pe(self):
        snap = [self.lastop["pe"]]
        self.last_fence = list(snap)
        for e in ("act", "dve", "pool"):
            self.pending[e] = list(snap)

    def finalize(self, esems):
        for e, lst in self.ops.items():
            for op in lst:
                for d in op.deps:
                    if d.eng == "pe" and e == "pe" and d.sem is None:
                        continue
                    d.used = True
        for e, lst in self.ops.items():
            cnt = 0
            for op in lst:
                if op.sem is None and op.used:
                    cnt += 1
                    op.tok = (esems[e], cnt)

    def run(self, e, eng):
        waited = {}
        for op in self.ops[e]:
            need = {}
            for d in op.deps:
                if d.eng == "pe" and e == "pe" and d.sem is None:
                    continue
                s, v = d.tok
                if need.get(s, 0) < v:
                    need[s] = v
            for s, v in need.items():
                if waited.get(s, 0) < v:
                    eng.wait_ge(s.h, v)
                    waited[s] = v
            ins = op.fn(eng)
            if ins is None:
                continue
            if op.sem is not None:
                ins.then_inc(op.sem.h, 16)
            elif op.used:
                ins.then_inc(op.tok[0].h, 1)


class Rot:
    def __init__(self, items):
        self.items = list(items)
        self.i = 0

    def next(self):
        v = self.items[self.i % len(self.items)]
        self.i += 1
        return v


def build(n_st, stop_after=None):
    nc = bass.Bass("TRN2", target_bir_lowering=False)
    TOK = n_st * NT
    xT_d = nc.dram_tensor("xT", [8, 128, TOK], F32, kind="ExternalInput").ap()
    wst_d = nc.dram_tensor("wst", [NSLAB, 128, SLAB], F32, kind="ExternalInput").ap()
    vecs_d = nc.dram_tensor("vecs", [128, NV], F32, kind="ExternalInput").ap()
    rope_d = nc.dram_tensor("rope", [2, 128, TOK], F32, kind="ExternalInput").ap()
    rows_d = nc.dram_tensor("rows", [1, NROW], F32, kind="ExternalInput").ap()
    wsT_d = nc.dram_tensor("wsT", [128, 1024], F32, kind="ExternalInput").ap()
    cm_d = nc.dram_tensor("cmask", [128, 1024 + 128], F32, kind="ExternalInput").ap()
    yT_d = nc.dram_tensor("yT", [8, 128, TOK], F32, kind="ExternalOutput").ap()
    scr_d = nc.dram_tensor("wscr", [NSLAB, 128, SLAB], BF16, kind="Internal").ap()

    ARENA = 30720
    import contextlib
    with contextlib.ExitStack() as es:
        def sb(name, shape, dt):
            return es.enter_context(nc.sbuf_tensor(name, shape, dt))

        def sem(name):
            return Sem(es.enter_context(nc.semaphore(name)))

        xs = sb("xT_sb", [128, 4, 8, 512], F32)
        hm = sb("hm", [128, 8192], F32)
        sq = sb("sq", [128, 3, 512], BF16)
        rstd = sb("rstd", [128, 2, 512], F32)
        ring = sb("ring", [128, R, SLAB], BF16)
        ropeb = sb("ropeb", [128, 2, 2, 512], F32)
        vecs = sb("vecs_sb", [128, NV], F32)
        sinks = sb("sinks", [128, 16], F32)
        esink = sb("esink", [128, 16], F32)
        bvt = sb("bvt", [128, 256], F32)
        maskT = sb("maskT", [128, 2, 512], BF16)
        ones = sb("ones", [128, 128], BF16)
        sel = sb("sel", [1, 128], BF16)
        esrow = sb("esrow", [1, 2048], BF16)
        carryK = sb("carryK", [128, 2, 128], BF16)
        carryV = sb("carryV", [128, 4, 128], BF16)
        WsT = sb("WsT", [128, 8, 128], BF16)
        small = sb("small", [128, 96], F32)
        arena = sb("arena", [128, ARENA], BF16)
        ps = [es.enter_context(nc.psum_tensor("ps%d" % i, [128, 512], F32)) for i in range(8)]

        esems = {e: sem("s_" + e) for e in ("pe", "act", "dve", "pool", "sp")}
        ring_sems = [sem("ring%d" % i) for i in range(R)]
        scr_sems = [sem("scr%d" % i) for i in range(R)]
        xl_sems = [sem("xl%d" % i) for i in range(4)]
        st_sems = [sem("store%d" % i) for i in range(4)]
        rope_sems = [sem("rope%d" % i) for i in range(2)]
        c_sems = [sem("const%d" % i) for i in range(10)]
        sg_sems = [sem("sguc%d" % i) for i in range(3)]

        block = es.enter_context(nc.Block())
        S = Sched()

        hT = hm[:, 0:4096].bitcast(BF16).rearrange("p (c t) -> p c t", c=8)
        mT = hm[:].rearrange("p (g c t) -> p g c t", g=2, c=8)

        class Carver:
            def __init__(self):
                self.off = 0

            def take(self, nelem, dt):
                nb = nelem * (4 if dt == F32 else 2)
                a = self.off // 2
                self.off += nb
                assert self.off <= ARENA * 2, "arena overflow"
                v = arena[:, a:a + nb // 2]
                return v.bitcast(F32) if dt == F32 else v

        def mm(out, lhsT, rhs, start, stop, reads, writes):
            S.add("pe", lambda e, o=out, l=lhsT, r=rhs, a=start, b=stop: e.matmul(o, l, r, start=a, stop=b),
                  reads, writes)

        def act(out, in_, func, reads, writes, scale=None, bias=None):
            kw = {}
            if scale is not None:
                kw["scale"] = scale
            if bias is not None:
                kw["bias"] = bias
            S.add("act", lambda e, o=out, i=in_, f=func, kw=kw: e.activation(out=o, in_=i, func=f, **kw),
                  reads, writes)

        def tt(out, in0, in1, op, reads, writes, eng="dve"):
            S.add(eng, lambda e, o=out, a=in0, b=in1, p=op: e.tensor_tensor(out=o, in0=a, in1=b, op=p),
                  reads, writes)

        def stt(out, in0, scalar, in1, op0, op1, reads, writes):
            S.add("dve", lambda e, o=out, a=in0, s=scalar, b=in1, p0=op0, p1=op1:
                  e.scalar_tensor_tensor(out=o, in0=a, scalar=s, in1=b, op0=p0, op1=p1), reads, writes)

        def ts(out, in0, s1, s2, op0, op1, reads, writes):
            S.add("dve", lambda e, o=out, a=in0, x=s1, y=s2, p0=op0, p1=op1:
                  e.tensor_scalar(out=o, in0=a, scalar1=x, scalar2=y, op0=p0, op1=p1), reads, writes)

        def dma(q, out, in_, reads, writes, s, extra_deps=None):
            op = S.add(q, lambda e, o=out, i=in_: e.dma_start(out=o, in_=i), reads, writes, dma_sem=s)
            if extra_deps:
                op.deps = list(set(op.deps) | set(extra_deps))
            return op

        vcol = lambda name, c: vecs[:, VC[name] + c:VC[name] + c + 1]

        wstate = {"issued": 0, "used": 0}
        SEQ = (list(range(0, 10)) + list(range(10, 32)) + list(range(32, 48)) * 2 +
               list(range(48, 60)) + list(range(60, 82)) + list(range(82, 98)) * 2)
        NSEQ = len(SEQ)
        TOTAL_SLABS = NSEQ * n_st
        scr_stored = set()

        def slab_width(j):
            if (32 <= j < 48) or (82 <= j < 98):
                return 11 * 128
            return SLAB

        def w_issue():
            i = wstate["issued"]
            if i >= TOTAL_SLABS:
                return
            slot = i % R
            j = SEQ[i % NSEQ]
            wd = slab_width(j)
            if j not in scr_stored:
                dma("pool", ring[:, slot, 0:wd], wst_d[j, :, 0:wd], [], [("w", slot)], ring_sems[slot])
            else:
                dma("pool", ring[:, slot, 0:wd], scr_d[j, :, 0:wd], [("scr", j)], [("w", slot)], ring_sems[slot])
            wstate["issued"] += 1

        def w_get(count):
            n = wstate["used"]
            while wstate["issued"] < min(n + R, TOTAL_SLABS):
                w_issue()
            wstate["used"] += count
            for i in range(n, n + count):
                j = SEQ[i % NSEQ]
                if j not in scr_stored:
                    wd = slab_width(j)
                    dma("sp", scr_d[j, :, 0:wd], ring[:, i % R, 0:wd], [("w", i % R)], [("scr", j)],
                        scr_sems[i % R])
                    scr_stored.add(j)
            return [(ring[:, (n + i) % R, :], ("w", (n + i) % R)) for i in range(count)]

        def w_next():
            return w_get(1)[0]

        dma("sp", vecs[:, :], vecs_d[:, :], [], [("vecs",)], c_sems[0])
        dma("sp", bvt[:, :], rows_d[0:1, 0:256].partition_broadcast(128), [], [("bvt",)], c_sems[1])
        dma("sp", sinks[:, :], rows_d[0:1, 256:272].partition_broadcast(128), [], [("sinks",)], c_sems[2])
        _cv0 = Carver()
        wstage = _cv0.take(1024, F32)
        tril = _cv0.take(128, F32)
        dma("sp", wstage[:, :], wsT_d[:, :], [], [("wstage",)], c_sems[6])
        dma("sp", tril[:, :], cm_d[:, 1024:1152], [], [("tril",)], c_sems[7])
        dma("pool", maskT[:].rearrange("p a b -> p (a b)"), cm_d[:, 0:1024], [], [("maskT",)], c_sems[8])
        for _ in range(R):
            w_issue()
        S.add("dve", lambda e: e.memset(ones[:], 1.0), [], [("ones",)])
        act(esink[:, :], sinks[:, :], AF.Exp, [("sinks",)], [("esink",)])
        S.add("dve", lambda e: e.memset(sel[0:1, 0:64], 0.0), [], [("sel",)])
        S.add("dve", lambda e: e.memset(sel[0:1, 64:128], 1.0), [], [("sel",)])
        for h in range(16):
            ts(esrow[0:1, h * 128:(h + 1) * 128], ones[0:1, 0:128], esink[0:1, h:h + 1], None, ALU.mult, ALU.bypass,
               [("ones",), ("esink",)], [("esrow",)])
        for g in range(8):
            tt(WsT[:, g, :], wstage[:, g * 128:(g + 1) * 128], tril[:, :], ALU.mult,
               [("wstage",), ("tril",)], [("WsT",)])
        tt(vecs[:, VC["bog"]:VC["bog"] + 8], vecs[:, VC["bo"]:VC["bo"] + 8], vecs[:, VC["gmo0"]:VC["gmo0"] + 8],
           ALU.mult, [("vecs",)], [("vecs",)])
        S.fence()

        sq_rot = Rot([0, 1, 2])
        cur = {"st": 0}

        def xslot(grp):
            return (2 * cur["st"] + grp) % 4

        def X(grp, c):
            return xs[:, xslot(grp), c, :]

        def xkey(grp, c):
            return ("x", xslot(grp), c)

        pre_done = {"v": False}
        deferred = []

        post_q = []

        def run_post(n):
            for _ in range(min(n, len(post_q))):
                post_q.pop(0)()

        def run_deferred():
            run_post(99)
            while deferred:
                deferred.pop(0)()

        def prenorm_group(gname, grp, st=None, bank=None):
            if st is None:
                st = cur["st"]
            if bank is None:
                bank = 6 + grp
            slot = (2 * st + grp) % 4
            cs = slice(grp * 512, (grp + 1) * 512)
            for c in range(8):
                r = sq_rot.next()
                act(sq[:, r, :], xs[:, slot, c, :], AF.Square, [("x", slot, c)], [("sq", r)])
                mm(ps[bank][:, :], ones[:, :], sq[:, r, :], c == 0, c == 7,
                   [("ones",), ("sq", r)], [("ps", bank)])
            act(rstd[:, grp, :], ps[bank][:, :], AF.Ln, [("ps", bank)], [("rstd", grp)],
                scale=1.0 / 1024.0, bias=EPS)
            act(rstd[:, grp, :], rstd[:, grp, :], AF.Exp, [("rstd", grp)], [("rstd", grp)], scale=-0.5)
            for c in range(8):
                stt(hT[:, c, cs], xs[:, slot, c, :], vcol(gname, c), rstd[:, grp, :], ALU.mult, ALU.mult,
                    [("x", slot, c), ("rstd", grp), ("vecs",)], [("h", grp, c), ("m", 0, c)])

        class PreTask:
            def __init__(self, gname, grp, st, bank):
                self.gname, self.grp, self.st, self.bank = gname, grp, st, bank
                self.slot = (2 * st + grp) % 4

            def square(self, c):
                r = sq_rot.next()
                act(sq[:, r, :], xs[:, self.slot, c, :], AF.Square, [("x", self.slot, c)], [("sq", r)])
                return r

            def stats(self, c, r):
                mm(ps[self.bank][:, :], ones[:, :], sq[:, r, :], c == 0, c == 7,
                   [("ones",), ("sq", r)], [("ps", self.bank)])

            def fin(self):
                grp = self.grp
                cs = slice(grp * 512, (grp + 1) * 512)
                act(rstd[:, grp, :], ps[self.bank][:, :], AF.Ln, [("ps", self.bank)], [("rstd", grp)],
                    scale=1.0 / 1024.0, bias=EPS)
                act(rstd[:, grp, :], rstd[:, grp, :], AF.Exp, [("rstd", grp)], [("rstd", grp)], scale=-0.5)
                for c in range(8):
                    stt(hT[:, c, cs], xs[:, self.slot, c, :], vcol(self.gname, c), rstd[:, grp, :],
                        ALU.mult, ALU.mult, [("x", self.slot, c), ("rstd", grp), ("vecs",)],
                        [("h", grp, c), ("m", 0, c)])

        def prenorm(gname):
            if pre_done["v"]:
                pre_done["v"] = False
                return
            for grp in range(2):
                prenorm_group(gname, grp)

        class PostNorm:
            def __init__(self, gname, bias_name=None, bg_name=None):
                self.gname = gname
                self.bias_name = bias_name
                self.bg_name = bg_name
                self.pending = None
                self.count = [0, 0]

            def tile(self, bank, grp, c):
                self.flush()
                r = sq_rot.next()
                b_sq = vcol(self.bias_name, c) if self.bias_name else None
                b_mg = vcol(self.bg_name, c) if self.bg_name else None
                act(sq[:, r, :], ps[bank][:, :], AF.Square, [("ps", bank), ("vecs",)], [("sq", r)], bias=b_sq)
                mkeys = [("m", grp, c)] + ([("h", 0, c), ("h", 1, c)] if grp == 0 else [])
                act(mT[:, grp, c, :], ps[bank][:, :], AF.Identity, [("ps", bank), ("vecs",)], mkeys,
                    scale=vcol(self.gname, c), bias=b_mg)
                self.pending = (r, grp)

            def flush(self):
                if self.pending is None:
                    return
                r, grp = self.pending
                n = self.count[grp]
                mm(ps[6 + grp][:, :], ones[:, :], sq[:, r, :], n == 0, n == 7,
                   [("ones",), ("sq", r)], [("ps", 6 + grp)])
                self.count[grp] += 1
                self.pending = None

            def post_group(self, grp, final_store=None, defer=False):
                act(rstd[:, grp, :], ps[6 + grp][:, :], AF.Ln, [("ps", 6 + grp)], [("rstd", grp)],
                    scale=1.0 / 1024.0, bias=EPS)
                act(rstd[:, grp, :], rstd[:, grp, :], AF.Exp, [("rstd", grp)], [("rstd", grp)], scale=-0.5)
                st_now = cur["st"]

                def chunk(c):
                    slot = (2 * st_now + grp) % 4
                    xa = xs[:, slot, c, :]
                    tt(mT[:, grp, c, :], mT[:, grp, c, :], rstd[:, grp, :], ALU.mult,
                       [("m", grp, c), ("rstd", grp)], [("m", grp, c)])
                    tt(xa, xa, mT[:, grp, c, :], ALU.add, [("x", slot, c), ("m", grp, c)], [("x", slot, c)])
                    if c == 7 and final_store is not None:
                        final_store(grp)

                for c in range(8):
                    if defer:
                        post_q.append(lambda c=c: chunk(c))
                    else:
                        chunk(c)

        def out_phase(pn, tile_fn, nxt, final_store=None, c_pre=4):
            for c in range(8):
                pn.tile(tile_fn(0, c), 0, c)
            pn.tile(tile_fn(1, 0), 1, 0)
            pn.post_group(0, final_store)
            early = nxt is not None and nxt[1] != cur["st"]
            tasks, plan = [], {}
            if nxt is not None:
                t0 = 1 if early else 2
                tasks.append(PreTask(nxt[0], 0, nxt[1], 6))
                for i in range(4):
                    plan.setdefault(t0 + i, []).extend([(0, 2 * i), (0, 2 * i + 1)])
                if early:
                    tasks.append(PreTask(nxt[0], 1, nxt[1], 6))
                    for i in range(3):
                        plan.setdefault(5 + i, []).extend([(1, 2 * i), (1, 2 * i + 1)])
            for t in range(1, 8):
                issued = [(ti, c, tasks[ti].square(c)) for ti, c in plan.get(t, [])]
                bank = tile_fn(1, t)
                for ti, c, r in issued:
                    tasks[ti].stats(c, r)
                    if c == 7:
                        tasks[ti].fin()
                pn.tile(bank, 1, t)
            pn.flush()
            if early:
                for c in (6, 7):
                    tasks[1].stats(c, tasks[1].square(c))
                tasks[1].fin()
            pn.post_group(1, final_store, defer=early)
            if nxt is not None:
                if not early:
                    deferred.append(lambda: prenorm_group(nxt[0], 1, nxt[1]))
                pre_done["v"] = True

        def attention(st):
            cv = Carver()
            qoT = cv.take(8 * NT, BF16).rearrange("p (s t) -> p s t", s=8)
            kT = cv.take(2 * NT, BF16).rearrange("p (s t) -> p s t", s=2)
            Vx = cv.take(NB * 4 * 128, BF16).rearrange("p (b g d) -> p b g d", b=NB, g=4)
            eT = cv.take(3 * 2 * 512, BF16).rearrange("p (u a t) -> p u a t", u=3, a=2)
            PT = cv.take(3 * 2 * 512, BF16).rearrange("p (u a t) -> p u a t", u=3, a=2)
            qs = cv.take(2 * 512, F32).rearrange("p (u t) -> p u t", u=2)
            t1 = cv.take(2 * 512, F32).rearrange("p (u t) -> p u t", u=2)
            t2 = cv.take(2 * 512, F32).rearrange("p (u t) -> p u t", u=2)
            lnd = cv.take(2 * 512, F32).rearrange("p (u t) -> p u t", u=2)
            Rr = cv.take(2 * 512, F32).rearrange("p (u t) -> p u t", u=2)

            prenorm("gmp0")
            S.add("dve", lambda e: e.memset(Vx[:, :, :, 64:128], 1.0), [], [("v", b) for b in range(NB)])
            if st == 0:
                S.add("dve", lambda e: e.memset(carryV[:, :, :], 0.0), [], [("vc",)])
                S.add("dve", lambda e: e.memset(carryK[:, :, :], 0.0), [], [("kc",)])

            bank_rot = Rot([0, 1, 2, 3])
            tmp_rot = Rot([0, 1])

            def qk_tile(wv, wk, ti, grp, bias_col, bias_sw_col, out_ap, out_keys):
                cs = slice(grp * 512, (grp + 1) * 512)
                bank = bank_rot.next()
                for k in range(8):
                    mm(ps[bank][:, :], wv[:, (ti * 8 + k) * 128:(ti * 8 + k + 1) * 128], hT[:, k, cs],
                       k == 0, k == 7, [wk, ("h", grp, k)], [("ps", bank)])
                u = tmp_rot.next()
                for a in range(4):
                    b = a ^ 1
                    act(qs[32 * a:32 * a + 32, u, :], ps[bank][32 * b:32 * b + 32, :], AF.Copy,
                        [("ps", bank)], [("qs", u, a)])
                stt(t1[:, u, :], ps[bank][:, :], bias_col, ropeb[:, grp, 0, :], ALU.add, ALU.mult,
                    [("ps", bank), ("rope", grp), ("vecs",)] + [("qs", u, a) for a in range(4)], [("t1", u)])
                stt(t2[:, u, :], qs[:, u, :], bias_sw_col, ropeb[:, grp, 1, :], ALU.add, ALU.mult,
                    [("qs", u, a) for a in range(4)] + [("rope", grp), ("vecs",)], [("t2", u)])
                tt(out_ap, t1[:, u, :], t2[:, u, :], ALU.add, [("t1", u), ("t2", u)], out_keys)

            def q_tile(wv, wk, j, ti, grp):
                s = 2 * j + ti
                cs = slice(grp * 512, (grp + 1) * 512)
                keys = [("q", s, b) for b in range(grp * 4, grp * 4 + 4)]
                qk_tile(wv, wk, ti, grp, vcol("bq", s), vcol("bqs", s), qoT[:, s, cs], keys)

            first = w_get(4)
            for grp in range(2):
                for j in range(4):
                    for ti in range(2):
                        q_tile(first[j][0], first[j][1], j, ti, grp)
                        run_post(2)
                    if grp == 0 and j == 0:
                        run_deferred()
            (kwv, kwk), (vwv, vwk) = w_get(2)

            def v_round(b0):
                vb_banks = [4, 5, 6, 7]
                grp = b0 // 4
                for k in range(8):
                    for bb in range(4):
                        b = b0 + bb
                        mm(ps[vb_banks[bb]][:, 0:256], hT[:, k, b * 128:(b + 1) * 128], vwv[:, k * 256:(k + 1) * 256],
                           k == 0, k == 7, [vwk, ("h", grp, k)], [("ps", vb_banks[bb])])
                for bb in range(4):
                    b = b0 + bb
                    tt(Vx[:, b, :, 0:64], ps[vb_banks[bb]][:, 0:256].rearrange("p (g d) -> p g d", g=4),
                       bvt[:, :].rearrange("p (g d) -> p g d", g=4), ALU.add,
                       [("ps", vb_banks[bb]), ("bvt",)], [("v", b)])

            v_round(0)
            for ti in range(2):
                for grp in range(2):
                    cs = slice(grp * 512, (grp + 1) * 512)
                    keys = [("k", ti, b) for b in range(grp * 4, grp * 4 + 4)]
                    qk_tile(kwv, kwk, ti, grp, vcol("bk", ti), vcol("bks", ti), kT[:, ti, cs], keys)
            v_round(4)

            sc_rot = Rot([(0, 1), (2, 3)])
            pv_rot = Rot([4, 5])
            iters = [(b, g) for b in range(NB) for g in range(4)]

            def stage1(i):
                b, g = iters[i]
                has_prev = (st * NB + b) > 0
                half = g % 2
                hs = slice(half * 64, half * 64 + 64)
                s0 = (g // 2) * 4
                kt = g // 2
                bs = slice(b * 128, (b + 1) * 128)
                q_rhs = qoT[hs, s0:s0 + 4, bs]
                q_keys = [("q", s0 + jj, b) for jj in range(4)]
                bp, bc = sc_rot.next()
                u = i % 3
                if has_prev:
                    if b > 0:
                        kprev, kpk = kT[hs, kt, (b - 1) * 128:b * 128], ("k", kt, b - 1)
                    else:
                        kprev, kpk = carryK[hs, kt, :], ("kc",)
                    mm(ps[bp][:, :], kprev, q_rhs, True, True, [kpk] + q_keys, [("ps", bp)])
                mm(ps[bc][:, :], kT[hs, kt, bs], q_rhs, True, True, [("k", kt, b)] + q_keys, [("ps", bc)])
                if has_prev:
                    act(eT[:, u, 0, :], ps[bp][:, :], AF.Exp, [("ps", bp)], [("e", u, 0)], scale=0.125)
                act(eT[:, u, 1, :], ps[bc][:, :], AF.Exp, [("ps", bc)], [("e", u, 1)], scale=0.125)
                if has_prev:
                    tt(PT[:, u, :, :], eT[:, u, :, :], maskT[:, :, :], ALU.mult,
                       [("e", u, 0), ("e", u, 1), ("maskT",)], [("P", u)])
                else:
                    tt(PT[:, u, 1, :], eT[:, u, 1, :], maskT[:, 1, :], ALU.mult,
                       [("e", u, 1), ("maskT",)], [("P", u)])

            def stage2(i):
                b, g = iters[i]
                has_prev = (st * NB + b) > 0
                half = g % 2
                hs = slice(half * 64, half * 64 + 64)
                s0 = (g // 2) * 4
                bs = slice(b * 128, (b + 1) * 128)
                q_keys = [("q", s0 + jj, b) for jj in range(4)]
                u = i % 3
                w = i % 2
                pvb = pv_rot.next()
                if has_prev:
                    if b > 0:
                        vprev, vpk = Vx[:, b - 1, g, :], ("v", b - 1)
                    else:
                        vprev, vpk = carryV[:, g, :], ("vc",)
                    mm(ps[pvb][:, :], vprev, PT[:, u, 0, :], True, False, [vpk, ("P", u)], [("ps", pvb)])
                mm(ps[pvb][:, :], Vx[:, b, g, :], PT[:, u, 1, :], not has_prev, False,
                   [("v", b), ("P", u)], [("ps", pvb)])
                mm(ps[pvb][:, :], sel[0:1, :], esrow[0:1, g * 512:(g + 1) * 512], False, True,
                   [("sel",), ("esrow",)], [("ps", pvb)])
                act(lnd[64:128, w, :], ps[pvb][64:128, :], AF.Ln, [("ps", pvb)], [("lnd", w)])
                act(Rr[0:64, w, :], lnd[64:128, w, :], AF.Exp, [("lnd", w)], [("R", w)], scale=-1.0)
                tt(qoT[hs, s0:s0 + 4, bs], ps[pvb][0:64, :].rearrange("p (j t) -> p j t", j=4),
                   Rr[0:64, w, :].rearrange("p (j t) -> p j t", j=4), ALU.mult,
                   [("ps", pvb), ("R", w)], q_keys)

            for i in range(len(iters)):
                stage1(i)
                if i >= 1:
                    stage2(i - 1)
            stage2(len(iters) - 1)
            S.add("dve", lambda e: e.tensor_copy(out=carryK[:, :, :], in_=kT[:, :, (NB - 1) * 128:NB * 128]),
                  [("k", 0, NB - 1), ("k", 1, NB - 1)], [("kc",)])
            S.add("dve", lambda e: e.tensor_copy(out=carryV[:, :, :], in_=Vx[:, NB - 1, :, :]),
                  [("v", NB - 1)], [("vc",)])

            pn = PostNorm("gmo0", "bo", "bog")
            orot = Rot([0, 1, 2, 3])
            wo_slabs = w_get(4)

            def wo_tile(grp, c):
                wv, wk = wo_slabs[c // 2]
                mi = c % 2
                cs = slice(grp * 512, (grp + 1) * 512)
                bank = orot.next()
                for k in range(8):
                    mm(ps[bank][:, :], wv[:, (mi * 8 + k) * 128:(mi * 8 + k + 1) * 128], qoT[:, k, cs],
                       k == 0, k == 7, [wk] + [("q", k, b) for b in range(grp * 4, grp * 4 + 4)],
                       [("ps", bank)])
                return bank

            out_phase(pn, wo_tile, ("gfp0", st))
            S.fence_pe()

        def ffn(st, layer, final_store=None):
            cv = Carver()
            gT = cv.take(NF * NT, BF16).rearrange("p (f t) -> p f t", f=NF)
            sg = cv.take(2 * 512, F32).rearrange("p (u t) -> p u t", u=2)
            prenorm("gfp%d" % layer)
            grot = Rot([(0, 1), (2, 3)])
            urot = Rot([0, 1])
            def gu_tile(wv, wk, f, grp):
                cs = slice(grp * 512, (grp + 1) * 512)
                bg_, bu_ = grot.next()
                u = urot.next()
                for k in range(8):
                    mm(ps[bg_][:, :], wv[:, k * 128:(k + 1) * 128], hT[:, k, cs], k == 0, k == 7,
                       [wk, ("h", grp, k)], [("ps", bg_)])
                for k in range(8):
                    mm(ps[bu_][:, :], wv[:, (8 + k) * 128:(9 + k) * 128], hT[:, k, cs], k == 0, k == 7,
                       [wk, ("h", grp, k)], [("ps", bu_)])
                act(sg[:, u, :], ps[bg_][:, :], AF.Silu, [("ps", bg_)], [("sg", u)])
                tt(gT[:, f, cs], ps[bu_][:, :], sg[:, u, :], ALU.mult, [("ps", bu_), ("sg", u)],
                   [("g", f, grp)])

            NSK = 4
            first = w_get(NSK)
            for grp in range(2):
                for f in range(NSK):
                    gu_tile(first[f][0], first[f][1], f, grp)
                    if grp == 0 and f == 1:
                        run_deferred()
            for f in range(NSK, NF):
                wv, wk = w_next()
                for grp in range(2):
                    gu_tile(wv, wk, f, grp)
            pn = PostNorm("gfo%d" % layer)
            orot = Rot([0, 1, 2, 3])

            def dn_tile(grp, c):
                (wv0, wk0), (wv1, wk1) = w_get(2)
                cs = slice(grp * 512, (grp + 1) * 512)
                bank = orot.next()
                for f in range(NF):
                    wv, wk = (wv0, wk0) if f < 11 else (wv1, wk1)
                    kk = f % 11
                    mm(ps[bank][:, :], wv[:, kk * 128:(kk + 1) * 128], gT[:, f, cs], f == 0, f == NF - 1,
                       [wk, ("g", f, grp)], [("ps", bank)])
                return bank

            if layer == 0:
                nxt = ("gmp1", st)
            else:
                nxt = ("gmp0", st + 1) if st + 1 < n_st else None
            out_phase(pn, dn_tile, nxt, final_store, c_pre=3)
            S.fence_pe()

        def sgu(st):
            cv = Carver()
            uT = cv.take(8 * NT, BF16).rearrange("p (c t) -> p c t", c=8)
            vf = cv.take(4 * 1024, F32).rearrange("p (u t) -> p u t", u=4)
            vb = cv.take(4 * 1024, BF16).rearrange("p (u t) -> p u t", u=4)
            tmp = cv.take(2 * 512, F32).rearrange("p (u t) -> p u t", u=2)
            lng = cv.take(1024, F32)
            lnb = cv.take(1024, F32)
            bsp = cv.take(1024, F32)
            fdeps = list(S.last_fence)
            dma("sp", lng[:, :], rows_d[0:1, 272:1296].partition_broadcast(128), [], [("lng",)], sg_sems[0], fdeps)
            dma("sp", lnb[:, :], rows_d[0:1, 1296:2320].partition_broadcast(128), [], [("lnb",)], sg_sems[1], fdeps)
            dma("sp", bsp[:, :], rows_d[0:1, 2320:3344].partition_broadcast(128), [], [("bsp",)], sg_sems[2], fdeps)
            prenorm("gmp1")
            brot = Rot([0, 1, 2, 3])

            def u_tile(wv, wk, i, mi, grp):
                c = 2 * i + mi
                cs = slice(grp * 512, (grp + 1) * 512)
                bank = brot.next()
                for k in range(8):
                    mm(ps[bank][:, :], wv[:, (mi * 8 + k) * 128:(mi * 8 + k + 1) * 128], hT[:, k, cs],
                       k == 0, k == 7, [wk, ("h", grp, k)], [("ps", bank)])
                act(uT[:, c, cs], ps[bank][:, :], AF.Gelu_apprx_tanh, [("ps", bank)],
                    [("u", c, b) for b in range(grp * 4, grp * 4 + 4)])

            first = w_get(4)
            for grp in range(2):
                for i in range(4):
                    for mi in range(2):
                        u_tile(first[i][0], first[i][1], i, mi, grp)
                    if grp == 0 and i == 0:
                        run_deferred()
            vslabs = w_get(4)
            mrot = Rot([(4, 5), (6, 7)])
            trot = Rot([0, 1])

            def stageA(b):
                grp = b // 4
                bs = slice(b * 128, (b + 1) * 128)
                u = b % 4
                for vh in range(2):
                    bank = brot.next()
                    for k in range(8):
                        wv, wk = vslabs[vh * 2 + k // 4]
                        kk = k % 4
                        mm(ps[bank][:, :], hT[:, k, bs], wv[:, kk * 512:(kk + 1) * 512], k == 0, k == 7,
                           [wk, ("h", grp, k)], [("ps", bank)])
                    act(vf[:, u, vh * 512:(vh + 1) * 512], ps[bank][:, :], AF.Gelu_apprx_tanh,
                        [("ps", bank)], [("vf", u, vh)])
                o = 16 * u
                S.add("dve", lambda e, o=o, u=u: e.bn_stats(out=small[:, o:o + 6], in_=vf[:, u, 0:512]),
                      [("vf", u, 0)], [("bn", u, 0)])
                S.add("dve", lambda e, o=o, u=u: e.bn_stats(out=small[:, o + 6:o + 12], in_=vf[:, u, 512:1024]),
                      [("vf", u, 1)], [("bn", u, 1)])
                S.add("dve", lambda e, o=o: e.bn_aggr(out=small[:, o + 12:o + 14], in_=small[:, o:o + 12]),
                      [("bn", u, 0), ("bn", u, 1)], [("mv", u)])
                o2 = 64 + 4 * u
                act(small[:, o2:o2 + 1], small[:, o + 13:o + 14], AF.Ln, [("mv", u)], [("lnv", u)], bias=EPS)
                act(small[:, o2 + 1:o2 + 2], small[:, o2:o2 + 1], AF.Exp, [("lnv", u)], [("rs", u)], scale=-0.5)
                ts(small[:, o2 + 2:o2 + 3], small[:, o + 12:o + 13], -1.0, small[:, o2 + 1:o2 + 2],
                   ALU.mult, ALU.mult, [("mv", u), ("rs", u)], [("nmr", u)])
                vkeys = [("vf", u, 0), ("vf", u, 1)]
                act(vf[:, u, :], vf[:, u, :], AF.Identity, vkeys + [("rs", u), ("nmr", u)], vkeys,
                    scale=small[:, o2 + 1:o2 + 2], bias=small[:, o2 + 2:o2 + 3])
                tt(vf[:, u, :], vf[:, u, :], lng[:, :], ALU.mult, vkeys + [("lng",)], vkeys, eng="pool")
                tt(vb[:, u, :], vf[:, u, :], lnb[:, :], ALU.add, vkeys + [("lnb",)], [("vb", u)], eng="pool")

            def stageB(b):
                bs = slice(b * 128, (b + 1) * 128)
                u = b % 4
                banks = mrot.next()
                for gg in range(8):
                    bank = banks[gg // 4]
                    col = (gg % 4) * 128
                    mm(ps[bank][:, col:col + 128], vb[:, u, gg * 128:(gg + 1) * 128], WsT[:, gg, :], True, True,
                       [("vb", u), ("WsT",)], [("ps", bank)])
                for hb in range(2):
                    bank = banks[hb]
                    tu = trot.next()
                    tt(tmp[:, tu, :], ps[bank][:, :], bsp[:, hb * 512:(hb + 1) * 512], ALU.add,
                       [("ps", bank), ("bsp",)], [("tmp", tu)])
                    ukeys = [("u", 4 * hb + jj, b) for jj in range(4)]
                    tt(uT[:, 4 * hb:4 * hb + 4, bs], uT[:, 4 * hb:4 * hb + 4, bs],
                       tmp[:, tu, :].rearrange("p (j t) -> p j t", j=4), ALU.mult,
                       [("tmp", tu)] + ukeys, ukeys)

            for b in range(NB):
                stageA(b)
                if b >= 3:
                    stageB(b - 3)
            stageB(NB - 3)
            stageB(NB - 2)
            stageB(NB - 1)
            pn = PostNorm("gmo1")
            orot = Rot([0, 1, 2, 3])
            wo_slabs = w_get(4)

            def so_tile(grp, c):
                wv, wk = wo_slabs[c // 2]
                mi = c % 2
                cs = slice(grp * 512, (grp + 1) * 512)
                bank = orot.next()
                for k in range(8):
                    mm(ps[bank][:, :], wv[:, (mi * 8 + k) * 128:(mi * 8 + k + 1) * 128], uT[:, k, cs],
                       k == 0, k == 7, [wk] + [("u", k, b) for b in range(grp * 4, grp * 4 + 4)],
                       [("ps", bank)])
                return bank

            out_phase(pn, so_tile, ("gfp1", st))
            S.fence_pe()

        store_ops = []

        def x_load(st, grp):
            slot = (2 * st + grp) % 4
            t0 = st * NT + grp * 512
            dma("sp", xs[:, slot, :, :], xT_d[:, :, t0:t0 + 512].rearrange("c p t -> p c t"),
                [], [("x", slot, c) for c in range(8)], xl_sems[slot])

        def rope_load(st):
            for grp in range(2):
                t0 = st * NT + grp * 512
                dma("sp", ropeb[:, grp, :, :], rope_d[:, :, t0:t0 + 512].rearrange("a p t -> p a t"),
                    [], [("rope", grp)], rope_sems[grp])

        x_load(0, 0)
        x_load(0, 1)
        rope_load(0)
        for st in range(n_st):
            cur["st"] = st

            def final_store(grp, st=st):
                slot = (2 * st + grp) % 4
                t0 = st * NT + grp * 512
                dma("sp", yT_d[:, :, t0:t0 + 512].rearrange("c p t -> p c t"), xs[:, slot, :, :],
                    [("x", slot, c) for c in range(8)], [], st_sems[slot])
                store_ops.append(S.lastop["sp"])

            def prefetch_next():
                if st + 1 < n_st:
                    x_load(st + 1, 0)
                    x_load(st + 1, 1)
                    rope_load(st + 1)

            def last_phase():
                prefetch_next()
                ffn(st, 1, final_store)

            phases = [lambda: attention(st), lambda: ffn(st, 0), lambda: sgu(st), last_phase]
            nph = 4 if stop_after is None else stop_after
            for pi in range(nph):
                phases[pi]()
            if stop_after is not None and stop_after < 4:
                pre_done["v"] = False
                deferred.clear()
                run_post(99)
                while wstate["used"] < NSEQ * (st + 1):
                    w_next()
                prefetch_next()
                for grp in range(2):
                    final_store(grp)
        fin = S.add("sp", lambda e: None, [], [])
        fin.deps = list(store_ops)

        S.finalize(esems)

        @block.tensor
        def _(e):
            S.run("pe", e)

        @block.scalar
        def _(e):
            S.run("act", e)

        @block.vector
        def _(e):
            S.run("dve", e)

        @block.gpsimd
        def _(e):
            S.run("pool", e)

        @block.sync
        def _(e):
            S.run("sp", e)
    return nc


def _qcols(s):
    hA = 8 * (s // 4) + (s % 4)
    return np.concatenate([np.arange(hA * 64, hA * 64 + 64), np.arange((hA + 4) * 64, (hA + 4) * 64 + 64)])


def _tileB(W):
    nk = W.shape[0] // 128
    return W.reshape(nk, 128, W.shape[1]).transpose(1, 0, 2)


def _prep(inp):
    f = np.float32
    wqkv = np.asarray(inp["attn_w_qkv"], f)[0]
    bqkv = np.asarray(inp["attn_b_qkv"], f)[0]
    wo = np.asarray(inp["attn_w_o"], f)[0]
    w_in = np.asarray(inp["sgu_w_in"], f)[0]
    w_out = np.asarray(inp["sgu_w_out"], f)[0]
    wgu = np.asarray(inp["ffn_w_gate_up"], f)
    wdn = np.asarray(inp["ffn_w_down"], f)
    slabs = np.zeros((NSLAB, 128, SLAB), f)
    idx = 0

    def put(i, arr):
        a = np.ascontiguousarray(arr).reshape(128, -1)
        slabs[i, :, :a.shape[1]] = a

    def ffn_slabs(layer, idx):
        for fi in range(NF):
            g = _tileB(wgu[layer][:, fi * 128:(fi + 1) * 128])
            u = _tileB(wgu[layer][:, 2816 + fi * 128:2816 + (fi + 1) * 128])
            put(idx, np.stack([g, u], axis=1))
            idx += 1
        for c in range(8):
            t = _tileB(wdn[layer][:, c * 128:(c + 1) * 128])
            put(idx, t[:, 0:11])
            idx += 1
            put(idx, t[:, 11:22])
            idx += 1
        return idx

    for j in range(4):
        tiles = [_tileB(wqkv[:, _qcols(2 * j + ti)]) for ti in range(2)]
        put(idx, np.stack(tiles, axis=1))
        idx += 1
    tiles = [_tileB(wqkv[:, 1024 + t * 128:1024 + (t + 1) * 128]) for t in range(2)]
    put(idx, np.stack(tiles, axis=1))
    idx += 1
    put(idx, wqkv[:, 1280:1536].reshape(8, 128, 256).transpose(1, 0, 2))
    idx += 1
    rowperm = np.concatenate([_qcols(k) for k in range(8)])
    wo_p = wo[rowperm, :]
    for i in range(4):
        tiles = [_tileB(wo_p[:, (2 * i + mi) * 128:(2 * i + mi + 1) * 128]) for mi in range(2)]
        put(idx, np.stack(tiles, axis=1))
        idx += 1
    idx = ffn_slabs(0, idx)
    for i in range(4):
        tiles = [_tileB(w_in[:, (2 * i + mi) * 128:(2 * i + mi + 1) * 128]) for mi in range(2)]
        put(idx, np.stack(tiles, axis=1))
        idx += 1
    for vh in range(2):
        for kh in range(2):
            blk = w_in[kh * 512:(kh + 1) * 512, 1024 + vh * 512:1024 + (vh + 1) * 512]
            put(idx, blk.reshape(4, 128, 512).transpose(1, 0, 2))
            idx += 1
    for i in range(4):
        tiles = [_tileB(w_out[:, (2 * i + mi) * 128:(2 * i + mi + 1) * 128]) for mi in range(2)]
        put(idx, np.stack(tiles, axis=1))
        idx += 1
    idx = ffn_slabs(1, idx)
    assert idx == NSLAB

    vecs = np.zeros((128, NV), f)

    def putv(name, vec, n):
        vecs[:, VC[name]:VC[name] + n] = np.asarray(vec, f).reshape(n, 128).T

    for i in range(2):
        putv("gmp%d" % i, inp["norm_mix_pre"][i], 8)
        putv("gmo%d" % i, inp["norm_mix_post"][i], 8)
        putv("gfp%d" % i, inp["norm_ffn_pre"][i], 8)
        putv("gfo%d" % i, inp["norm_ffn_post"][i], 8)
    putv("bo", inp["attn_b_o"][0], 8)
    swp = np.arange(128) ^ 32
    for s in range(8):
        cols = _qcols(s)
        vecs[:, VC["bq"] + s] = bqkv[cols]
        vecs[:, VC["bqs"] + s] = bqkv[cols[swp]]
    for t in range(2):
        cols = 1024 + t * 128 + np.arange(128)
        vecs[:, VC["bk"] + t] = bqkv[cols]
        vecs[:, VC["bks"] + t] = bqkv[cols[swp]]

    rows = np.zeros((1, NROW), f)
    rows[0, 0:256] = bqkv[1280:1536]
    rows[0, 256:272] = np.asarray(inp["attn_sinks"], f)[0]
    rows[0, 272:1296] = np.asarray(inp["sgu_ln_g"], f)[0]
    rows[0, 1296:2320] = np.asarray(inp["sgu_ln_b"], f)[0]
    rows[0, 2320:3344] = np.asarray(inp["sgu_b_spatial"], f)[0].reshape(-1)
    wsp = np.asarray(inp["sgu_w_spatial"], f)[0]
    wsT = np.ascontiguousarray(wsp.transpose(2, 0, 1)).reshape(128, 1024)

    half = 32
    inv_freq = 10000.0 ** (-(np.arange(half, dtype=np.float64) * 2.0) / 64.0)
    pos = np.arange(4096, dtype=np.float64)
    ang = pos[None, :] * inv_freq[:, None]
    cos = np.cos(ang).astype(f)
    sin = np.sin(ang).astype(f)
    p = np.arange(128)
    rope = np.zeros((2, 128, 4096), f)
    rope[0] = cos[p % 32]
    sgn = np.where((p % 64) < 32, -1.0, 1.0).astype(f)
    rope[1] = sin[p % 32] * sgn[:, None]
    cm = np.zeros((128, 1024 + 128), f)
    j = np.arange(128)[:, None]
    i = np.arange(128)[None, :]
    prev = (j > i).astype(f)
    cur = (j <= i).astype(f)
    cm[:, 0:512] = np.tile(prev, (1, 4))
    cm[:, 512:1024] = np.tile(cur, (1, 4))
    cm[:, 1024:1152] = cur
    return dict(wst=slabs, vecs=vecs, rows=rows, wsT=wsT, rope=rope, cmask=cm)


_CACHE = {}


def _run(inputs, n_st, stop_after=None, n_cores=8):
    x = np.asarray(inputs["x"], np.float32)
    shared = _prep(inputs)
    TOK = n_st * NT
    key = (n_st, stop_after)
    if key not in _CACHE:
        _CACHE[key] = build(n_st, stop_after)
    nc = _CACHE[key]
    in_maps = []
    for b in range(n_cores):
        xT = np.ascontiguousarray(x[b, :TOK, :].T).reshape(8, 128, TOK)
        m = dict(shared)
        m["rope"] = np.ascontiguousarray(shared["rope"][:, :, :TOK])
        m["xT"] = xT
        in_maps.append(m)
    res = run_bass_kernel_spmd(nc, in_maps, core_ids=list(range(n_cores)))
    out = np.empty((n_cores, TOK, 1024), np.float32)
    for b in range(n_cores):
        out[b] = res.results[b]["yT"].reshape(1024, TOK).T
    return out


def kernel(**inputs):
    return _run(inputs, 4)
```

```python
import numpy as np
import concourse.bass as bass
import concourse.mybir as mybir
from concourse.bass_utils import run_bass_kernel_spmd

F32 = mybir.dt.float32
BF16 = mybir.dt.bfloat16
AF = mybir.ActivationFunctionType
ALU = mybir.AluOpType

NT = 1024
NB = 8
R = 6
SLAB = 2048
NF = 22
EPS = 1e-6
NSLAB = 98
NROW = 256 + 16 + 3 * 1024

VC = {}
_c = 0
for _n, _w in [("gmp0", 8), ("gmo0", 8), ("gfp0", 8), ("gfo0", 8),
               ("gmp1", 8), ("gmo1", 8), ("gfp1", 8), ("gfo1", 8),
               ("bo", 8), ("bq", 8), ("bqs", 8), ("bk", 2), ("bks", 2), ("bog", 8)]:
    VC[_n] = _c
    _c += _w
NV = _c


class Sem:
    def __init__(self, h):
        self.h = h
        self.count = 0


class Op:
    __slots__ = ("eng", "fn", "deps", "tok", "used", "sem")


class Sched:
    def __init__(self):
        self.ops = {e: [] for e in ("pe", "act", "dve", "pool", "sp")}
        self.lastw = {}
        self.rd = {}
        self.pending = {}
        self.lastop = {}

    def add(self, eng, fn, reads=(), writes=(), dma_sem=None):
        op = Op()
        op.eng = eng
        op.fn = fn
        op.used = False
        op.sem = dma_sem
        op.tok = None
        deps = []
        for k in reads:
            w = self.lastw.get(k)
            if w is not None:
                deps.append(w)
        for k in writes:
            w = self.lastw.get(k)
            if w is not None:
                deps.append(w)
            r = self.rd.get(k)
            if r:
                deps.extend(r.values())
        p = self.pending.pop(eng, None)
        if p:
            deps.extend(p)
        op.deps = list(set(deps))
        rkey = eng if dma_sem is None else id(op)
        for k in reads:
            self.rd.setdefault(k, {})[rkey] = op
        for k in writes:
            self.lastw[k] = op
            self.rd[k] = {}
        if dma_sem is not None:
            dma_sem.count += 16
            op.tok = (dma_sem, dma_sem.count)
        self.ops[eng].append(op)
        self.lastop[eng] = op
        if eng == "pool" and dma_sem is None:
            self.lastop["poolc"] = op
        return op

    def fence(self):
        snap = [self.lastop[e] for e in ("pe", "act", "dve", "poolc") if e in self.lastop]
        self.last_fence = list(snap)
        for e in ("pe", "act", "dve", "pool"):
            self.pending[e] = list(snap)

    def fence_pe(self):
        snap = [self.lastop["pe"]]
        self.last_fence = list(snap)
        for e in ("act", "dve", "pool"):
            self.pending[e] = list(snap)

    def finalize(self, esems):
        for e, lst in self.ops.items():
            for op in lst:
                for d in op.deps:
                    if d.eng == "pe" and e == "pe" and d.sem is None:
                        continue
                    d.used = True
        for e, lst in self.ops.items():
            cnt = 0
            for op in lst:
                if op.sem is None and op.used:
                    cnt += 1
                    op.tok = (esems[e], cnt)

    def run(self, e, eng):
        waited = {}
        for op in self.ops[e]:
            need = {}
            for d in op.deps:
                if d.eng == "pe" and e == "pe" and d.sem is None:
                    continue
                s, v = d.tok
                if need.get(s, 0) < v:
                    need[s] = v
            for s, v in need.items():
                if waited.get(s, 0) < v:
                    eng.wait_ge(s.h, v)
                    waited[s] = v
            ins = op.fn(eng)
            if ins is None:
                continue
            if op.sem is not None:
                ins.then_inc(op.sem.h, 16)
            elif op.used:
                ins.then_inc(op.tok[0].h, 1)


class Rot:
    def __init__(self, items):
        self.items = list(items)
        self.i = 0

    def next(self):
        v = self.items[self.i % len(self.items)]
        self.i += 1
        return v


def build(n_st, stop_after=None):
    nc = bass.Bass("TRN2", target_bir_lowering=False)
    TOK = n_st * NT
    xT_d = nc.dram_tensor("xT", [8, 128, TOK], F32, kind="ExternalInput").ap()
    wst_d = nc.dram_tensor("wst", [NSLAB, 128, SLAB], F32, kind="ExternalInput").ap()
    vecs_d = nc.dram_tensor("vecs", [128, NV], F32, kind="ExternalInput").ap()
    rope_d = nc.dram_tensor("rope", [2, 128, TOK], F32, kind="ExternalInput").ap()
    rows_d = nc.dram_tensor("rows", [1, NROW], F32, kind="ExternalInput").ap()
    wsT_d = nc.dram_tensor("wsT", [128, 1024], F32, kind="ExternalInput").ap()
    cm_d = nc.dram_tensor("cmask", [128, 1024 + 128], F32, kind="ExternalInput").ap()
    yT_d = nc.dram_tensor("yT", [8, 128, TOK], F32, kind="ExternalOutput").ap()
    scr_d = nc.dram_tensor("wscr", [NSLAB, 128, SLAB], BF16, kind="Internal").ap()

    ARENA = 30720
    import contextlib
    with contextlib.ExitStack() as es:
        def sb(name, shape, dt):
            return es.enter_context(nc.sbuf_tensor(name, shape, dt))

        def sem(name):
            return Sem(es.enter_context(nc.semaphore(name)))

        xs = sb("xT_sb", [128, 4, 8, 512], F32)
        hm = sb("hm", [128, 8192], F32)
        sq = sb("sq", [128, 3, 512], BF16)
        rstd = sb("rstd", [128, 2, 512], F32)
        ring = sb("ring", [128, R, SLAB], BF16)
        ropeb = sb("ropeb", [128, 2, 2, 512], F32)
        vecs = sb("vecs_sb", [128, NV], F32)
        sinks = sb("sinks", [128, 16], F32)
        esink = sb("esink", [128, 16], F32)
        bvt = sb("bvt", [128, 256], F32)
        maskT = sb("maskT", [128, 2, 512], BF16)
        ones = sb("ones", [128, 128], BF16)
        sel = sb("sel", [1, 128], BF16)
        esrow = sb("esrow", [1, 2048], BF16)
        carryK = sb("carryK", [128, 2, 128], BF16)
        carryV = sb("carryV", [128, 4, 128], BF16)
        WsT = sb("WsT", [128, 8, 128], BF16)
        small = sb("small", [128, 96], F32)
        arena = sb("arena", [128, ARENA], BF16)
        ps_all = es.enter_context(nc.psum_tensor("psall", [128, 8, 512], F32))
        ps = [ps_all[:, i, :] for i in range(8)]

        esems = {e: sem("s_" + e) for e in ("pe", "act", "dve", "pool", "sp")}
        ring_sems = [sem("ring%d" % i) for i in range(R)]
        scr_sems = [sem("scr%d" % i) for i in range(R)]
        xl_sems = [sem("xl%d" % i) for i in range(4)]
        st_sems = [sem("store%d" % i) for i in range(4)]
        rope_sems = [sem("rope%d" % i) for i in range(2)]
        c_sems = [sem("const%d" % i) for i in range(10)]
        sg_sems = [sem("sguc%d" % i) for i in range(3)]

        block = es.enter_context(nc.Block())
        S = Sched()

        hT = hm[:, 0:4096].bitcast(BF16).rearrange("p (c t) -> p c t", c=8)
        mT = hm[:].rearrange("p (g c t) -> p g c t", g=2, c=8)

        class Carver:
            def __init__(self):
                self.off = 0

            def take(self, nelem, dt):
                nb = nelem * (4 if dt == F32 else 2)
                a = self.off // 2
                self.off += nb
                assert self.off <= ARENA * 2, "arena overflow"
                v = arena[:, a:a + nb // 2]
                return v.bitcast(F32) if dt == F32 else v

        def mm(out, lhsT, rhs, start, stop, reads, writes):
            S.add("pe", lambda e, o=out, l=lhsT, r=rhs, a=start, b=stop: e.matmul(o, l, r, start=a, stop=b),
                  reads, writes)

        def act(out, in_, func, reads, writes, scale=None, bias=None):
            kw = {}
            if scale is not None:
                kw["scale"] = scale
            if bias is not None:
                kw["bias"] = bias
            S.add("act", lambda e, o=out, i=in_, f=func, kw=kw: e.activation(out=o, in_=i, func=f, **kw),
                  reads, writes)

        def tt(out, in0, in1, op, reads, writes, eng="dve"):
            S.add(eng, lambda e, o=out, a=in0, b=in1, p=op: e.tensor_tensor(out=o, in0=a, in1=b, op=p),
                  reads, writes)

        def stt(out, in0, scalar, in1, op0, op1, reads, writes):
            S.add("dve", lambda e, o=out, a=in0, s=scalar, b=in1, p0=op0, p1=op1:
                  e.scalar_tensor_tensor(out=o, in0=a, scalar=s, in1=b, op0=p0, op1=p1), reads, writes)

        def ts(out, in0, s1, s2, op0, op1, reads, writes):
            S.add("dve", lambda e, o=out, a=in0, x=s1, y=s2, p0=op0, p1=op1:
                  e.tensor_scalar(out=o, in0=a, scalar1=x, scalar2=y, op0=p0, op1=p1), reads, writes)

        def dma(q, out, in_, reads, writes, s, extra_deps=None):
            op = S.add(q, lambda e, o=out, i=in_: e.dma_start(out=o, in_=i), reads, writes, dma_sem=s)
            if extra_deps:
                op.deps = list(set(op.deps) | set(extra_deps))
            return op

        vcol = lambda name, c: vecs[:, VC[name] + c:VC[name] + c + 1]

        wstate = {"issued": 0, "used": 0}
        SEQ = (list(range(0, 10)) + list(range(10, 32)) + list(range(32, 48)) * 2 +
               list(range(48, 60)) + list(range(60, 82)) + list(range(82, 98)) * 2)
        NSEQ = len(SEQ)
        TOTAL_SLABS = NSEQ * n_st
        scr_stored = set()

        def slab_width(j):
            if (32 <= j < 48) or (82 <= j < 98):
                return 11 * 128
            return SLAB

        def w_issue():
            i = wstate["issued"]
            if i >= TOTAL_SLABS:
                return
            slot = i % R
            j = SEQ[i % NSEQ]
            wd = slab_width(j)
            if j not in scr_stored:
                dma("pool", ring[:, slot, 0:wd], wst_d[j, :, 0:wd], [], [("w", slot)], ring_sems[slot])
            else:
                dma("pool", ring[:, slot, 0:wd], scr_d[j, :, 0:wd], [("scr", j)], [("w", slot)], ring_sems[slot])
            wstate["issued"] += 1

        def w_get(count):
            n = wstate["used"]
            while wstate["issued"] < min(n + R, TOTAL_SLABS):
                w_issue()
            wstate["used"] += count
            for i in range(n, n + count):
                j = SEQ[i % NSEQ]
                if j not in scr_stored:
                    wd = slab_width(j)
                    dma("sp", scr_d[j, :, 0:wd], ring[:, i % R, 0:wd], [("w", i % R)], [("scr", j)],
                        scr_sems[i % R])
                    scr_stored.add(j)
            return [(ring[:, (n + i) % R, :], ("w", (n + i) % R)) for i in range(count)]

        def w_next():
            return w_get(1)[0]

        dma("sp", vecs[:, :], vecs_d[:, :], [], [("vecs",)], c_sems[0])
        dma("sp", bvt[:, :], rows_d[0:1, 0:256].partition_broadcast(128), [], [("bvt",)], c_sems[1])
        dma("sp", sinks[:, :], rows_d[0:1, 256:272].partition_broadcast(128), [], [("sinks",)], c_sems[2])
        _cv0 = Carver()
        wstage = _cv0.take(1024, F32)
        tril = _cv0.take(128, F32)
        dma("sp", wstage[:, :], wsT_d[:, :], [], [("wstage",)], c_sems[6])
        dma("sp", tril[:, :], cm_d[:, 1024:1152], [], [("tril",)], c_sems[7])
        dma("pool", maskT[:].rearrange("p a b -> p (a b)"), cm_d[:, 0:1024], [], [("maskT",)], c_sems[8])
        for _ in range(R):
            w_issue()
        S.add("dve", lambda e: e.memset(ones[:], 1.0), [], [("ones",)])
        act(esink[:, :], sinks[:, :], AF.Exp, [("sinks",)], [("esink",)])
        S.add("dve", lambda e: e.memset(sel[0:1, 0:64], 0.0), [], [("sel",)])
        S.add("dve", lambda e: e.memset(sel[0:1, 64:128], 1.0), [], [("sel",)])
        for h in range(16):
            ts(esrow[0:1, h * 128:(h + 1) * 128], ones[0:1, 0:128], esink[0:1, h:h + 1], None, ALU.mult, ALU.bypass,
               [("ones",), ("esink",)], [("esrow",)])
        for g in range(8):
            tt(WsT[:, g, :], wstage[:, g * 128:(g + 1) * 128], tril[:, :], ALU.mult,
               [("wstage",), ("tril",)], [("WsT",)])
        tt(vecs[:, VC["bog"]:VC["bog"] + 8], vecs[:, VC["bo"]:VC["bo"] + 8], vecs[:, VC["gmo0"]:VC["gmo0"] + 8],
           ALU.mult, [("vecs",)], [("vecs",)])
        S.fence()

        sq_rot = Rot([0, 1, 2])
        cur = {"st": 0}

        def xslot(grp):
            return (2 * cur["st"] + grp) % 4

        def X(grp, c):
            return xs[:, xslot(grp), c, :]

        def xkey(grp, c):
            return ("x", xslot(grp), c)

        pre_done = {"v": False}
        deferred = []

        post_q = []

        def run_post(n):
            for _ in range(min(n, len(post_q))):
                post_q.pop(0)()

        def run_deferred():
            run_post(99)
            while deferred:
                deferred.pop(0)()

        def prenorm_group(gname, grp, st=None, bank=None):
            if st is None:
                st = cur["st"]
            if bank is None:
                bank = 6 + grp
            slot = (2 * st + grp) % 4
            cs = slice(grp * 512, (grp + 1) * 512)
            for c in range(8):
                r = sq_rot.next()
                act(sq[:, r, :], xs[:, slot, c, :], AF.Square, [("x", slot, c)], [("sq", r)])
                mm(ps[bank][:, :], ones[:, :], sq[:, r, :], c == 0, c == 7,
                   [("ones",), ("sq", r)], [("ps", bank)])
            act(rstd[:, grp, :], ps[bank][:, :], AF.Ln, [("ps", bank)], [("rstd", grp)],
                scale=1.0 / 1024.0, bias=EPS)
            act(rstd[:, grp, :], rstd[:, grp, :], AF.Exp, [("rstd", grp)], [("rstd", grp)], scale=-0.5)
            for c in range(8):
                stt(hT[:, c, cs], xs[:, slot, c, :], vcol(gname, c), rstd[:, grp, :], ALU.mult, ALU.mult,
                    [("x", slot, c), ("rstd", grp), ("vecs",)], [("h", grp, c), ("m", 0, c)])

        class PreTask:
            def __init__(self, gname, grp, st, bank):
                self.gname, self.grp, self.st, self.bank = gname, grp, st, bank
                self.slot = (2 * st + grp) % 4

            def square(self, c):
                r = sq_rot.next()
                act(sq[:, r, :], xs[:, self.slot, c, :], AF.Square, [("x", self.slot, c)], [("sq", r)])
                return r

            def stats(self, c, r):
                mm(ps[self.bank][:, :], ones[:, :], sq[:, r, :], c == 0, c == 7,
                   [("ones",), ("sq", r)], [("ps", self.bank)])

            def fin(self):
                grp = self.grp
                cs = slice(grp * 512, (grp + 1) * 512)
                act(rstd[:, grp, :], ps[self.bank][:, :], AF.Ln, [("ps", self.bank)], [("rstd", grp)],
                    scale=1.0 / 1024.0, bias=EPS)
                act(rstd[:, grp, :], rstd[:, grp, :], AF.Exp, [("rstd", grp)], [("rstd", grp)], scale=-0.5)
                for c in range(8):
                    stt(hT[:, c, cs], xs[:, self.slot, c, :], vcol(self.gname, c), rstd[:, grp, :],
                        ALU.mult, ALU.mult, [("x", self.slot, c), ("rstd", grp), ("vecs",)],
                        [("h", grp, c), ("m", 0, c)])

        def prenorm(gname):
            if pre_done["v"]:
                pre_done["v"] = False
                return
            for grp in range(2):
                prenorm_group(gname, grp)

        class PostNorm:
            def __init__(self, gname, bias_name=None, bg_name=None):
                self.gname = gname
                self.bias_name = bias_name
                self.bg_name = bg_name
                self.pending = None
                self.count = [0, 0]

            def tile(self, bank, grp, c):
                self.flush()
                r = sq_rot.next()
                b_sq = vcol(self.bias_name, c) if self.bias_name else None
                b_mg = vcol(self.bg_name, c) if self.bg_name else None
                act(sq[:, r, :], ps[bank][:, :], AF.Square, [("ps", bank), ("vecs",)], [("sq", r)], bias=b_sq)
                mkeys = [("m", grp, c)] + ([("h", 0, c), ("h", 1, c)] if grp == 0 else [])
                act(mT[:, grp, c, :], ps[bank][:, :], AF.Identity, [("ps", bank), ("vecs",)], mkeys,
                    scale=vcol(self.gname, c), bias=b_mg)
                self.pending = (r, grp)

            def flush(self):
                if self.pending is None:
                    return
                r, grp = self.pending
                n = self.count[grp]
                mm(ps[6 + grp][:, :], ones[:, :], sq[:, r, :], n == 0, n == 7,
                   [("ones",), ("sq", r)], [("ps", 6 + grp)])
                self.count[grp] += 1
                self.pending = None

            def post_group(self, grp, final_store=None, defer=False):
                act(rstd[:, grp, :], ps[6 + grp][:, :], AF.Ln, [("ps", 6 + grp)], [("rstd", grp)],
                    scale=1.0 / 1024.0, bias=EPS)
                act(rstd[:, grp, :], rstd[:, grp, :], AF.Exp, [("rstd", grp)], [("rstd", grp)], scale=-0.5)
                st_now = cur["st"]

                def chunk(c):
                    slot = (2 * st_now + grp) % 4
                    xa = xs[:, slot, c, :]
                    tt(mT[:, grp, c, :], mT[:, grp, c, :], rstd[:, grp, :], ALU.mult,
                       [("m", grp, c), ("rstd", grp)], [("m", grp, c)])
                    tt(xa, xa, mT[:, grp, c, :], ALU.add, [("x", slot, c), ("m", grp, c)], [("x", slot, c)])
                    if c == 7 and final_store is not None:
                        final_store(grp)

                for c in range(8):
                    if defer:
                        post_q.append(lambda c=c: chunk(c))
                    else:
                        chunk(c)

        def out_phase(pn, tile_fn, nxt, final_store=None, c_pre=4):
            for c in range(8):
                pn.tile(tile_fn(0, c), 0, c)
            pn.tile(tile_fn(1, 0), 1, 0)
            pn.post_group(0, final_store)
            early = nxt is not None and nxt[1] != cur["st"]
            tasks, plan = [], {}
            if nxt is not None:
                t0 = 1 if early else 2
                tasks.append(PreTask(nxt[0], 0, nxt[1], 6))
                for i in range(4):
                    plan.setdefault(t0 + i, []).extend([(0, 2 * i), (0, 2 * i + 1)])
                if early:
                    tasks.append(PreTask(nxt[0], 1, nxt[1], 6))
                    for i in range(3):
                        plan.setdefault(5 + i, []).extend([(1, 2 * i), (1, 2 * i + 1)])
            for t in range(1, 8):
                issued = [(ti, c, tasks[ti].square(c)) for ti, c in plan.get(t, [])]
                bank = tile_fn(1, t)
                for ti, c, r in issued:
                    tasks[ti].stats(c, r)
                    if c == 7:
                        tasks[ti].fin()
                pn.tile(bank, 1, t)
            pn.flush()
            if early:
                for c in (6, 7):
                    tasks[1].stats(c, tasks[1].square(c))
                tasks[1].fin()
            pn.post_group(1, final_store, defer=early)
            if nxt is not None:
                if not early:
                    deferred.append(lambda: prenorm_group(nxt[0], 1, nxt[1]))
                pre_done["v"] = True

        def attention(st):
            cv = Carver()
            qoT = cv.take(8 * NT, BF16).rearrange("p (s t) -> p s t", s=8)
            kT = cv.take(2 * NT, BF16).rearrange("p (s t) -> p s t", s=2)
            Vx = cv.take(NB * 4 * 128, BF16).rearrange("p (b g d) -> p b g d", b=NB, g=4)
            eT = cv.take(3 * 2 * 512, BF16).rearrange("p (u a t) -> p u a t", u=3, a=2)
            PT = cv.take(3 * 2 * 512, BF16).rearrange("p (u a t) -> p u a t", u=3, a=2)
            qs = cv.take(2 * 512, F32).rearrange("p (u t) -> p u t", u=2)
            t1 = cv.take(2 * 512, F32).rearrange("p (u t) -> p u t", u=2)
            t2 = cv.take(2 * 512, F32).rearrange("p (u t) -> p u t", u=2)
            lnd = cv.take(2 * 512, F32).rearrange("p (u t) -> p u t", u=2)
            Rr = cv.take(2 * 512, F32).rearrange("p (u t) -> p u t", u=2)

            prenorm("gmp0")
            S.add("dve", lambda e: e.memset(Vx[:, :, :, 64:128], 1.0), [], [("v", b) for b in range(NB)])
            if st == 0:
                S.add("dve", lambda e: e.memset(carryV[:, :, :], 0.0), [], [("vc",)])
                S.add("dve", lambda e: e.memset(carryK[:, :, :], 0.0), [], [("kc",)])

            bank_rot = Rot([0, 1, 2, 3])
            tmp_rot = Rot([0, 1])

            def qk_tile(wv, wk, ti, grp, bias_col, bias_sw_col, out_ap, out_keys):
                cs = slice(grp * 512, (grp + 1) * 512)
                bank = bank_rot.next()
                for k in range(8):
                    mm(ps[bank][:, :], wv[:, (ti * 8 + k) * 128:(ti * 8 + k + 1) * 128], hT[:, k, cs],
                       k == 0, k == 7, [wk, ("h", grp, k)], [("ps", bank)])
                u = tmp_rot.next()
                for a in range(4):
                    b = a ^ 1
                    act(qs[32 * a:32 * a + 32, u, :], ps[bank][32 * b:32 * b + 32, :], AF.Copy,
                        [("ps", bank)], [("qs", u, a)])
                stt(t1[:, u, :], ps[bank][:, :], bias_col, ropeb[:, grp, 0, :], ALU.add, ALU.mult,
                    [("ps", bank), ("rope", grp), ("vecs",)] + [("qs", u, a) for a in range(4)], [("t1", u)])
                stt(t2[:, u, :], qs[:, u, :], bias_sw_col, ropeb[:, grp, 1, :], ALU.add, ALU.mult,
                    [("qs", u, a) for a in range(4)] + [("rope", grp), ("vecs",)], [("t2", u)])
                tt(out_ap, t1[:, u, :], t2[:, u, :], ALU.add, [("t1", u), ("t2", u)], out_keys)

            def q_tile(wv, wk, j, ti, grp):
                s = 2 * j + ti
                cs = slice(grp * 512, (grp + 1) * 512)
                keys = [("q", s, b) for b in range(grp * 4, grp * 4 + 4)]
                qk_tile(wv, wk, ti, grp, vcol("bq", s), vcol("bqs", s), qoT[:, s, cs], keys)

            first = w_get(4)
            for grp in range(2):
                for j in range(4):
                    for ti in range(2):
                        q_tile(first[j][0], first[j][1], j, ti, grp)
                        run_post(2)
                    if grp == 0 and j == 0:
                        run_deferred()
            (kwv, kwk), (vwv, vwk) = w_get(2)

            def v_round(b0):
                vb_banks = [4, 5, 6, 7]
                grp = b0 // 4
                for k in range(8):
                    for bb in range(4):
                        b = b0 + bb
                        mm(ps[vb_banks[bb]][:, 0:256], hT[:, k, b * 128:(b + 1) * 128], vwv[:, k * 256:(k + 1) * 256],
                           k == 0, k == 7, [vwk, ("h", grp, k)], [("ps", vb_banks[bb])])
                for bb in range(4):
                    b = b0 + bb
                    tt(Vx[:, b, :, 0:64], ps[vb_banks[bb]][:, 0:256].rearrange("p (g d) -> p g d", g=4),
                       bvt[:, :].rearrange("p (g d) -> p g d", g=4), ALU.add,
                       [("ps", vb_banks[bb]), ("bvt",)], [("v", b)])

            v_round(0)
            for ti in range(2):
                for grp in range(2):
                    cs = slice(grp * 512, (grp + 1) * 512)
                    keys = [("k", ti, b) for b in range(grp * 4, grp * 4 + 4)]
                    qk_tile(kwv, kwk, ti, grp, vcol("bk", ti), vcol("bks", ti), kT[:, ti, cs], keys)
            v_round(4)

            sc_rot = Rot([(0, 1), (2, 3)])
            pv_rot = Rot([4, 5])
            iters = [(b, g) for b in range(NB) for g in range(4)]

            def stage1(i):
                b, g = iters[i]
                has_prev = (st * NB + b) > 0
                half = g % 2
                hs = slice(half * 64, half * 64 + 64)
                s0 = (g // 2) * 4
                kt = g // 2
                bs = slice(b * 128, (b + 1) * 128)
                q_rhs = qoT[hs, s0:s0 + 4, bs]
                q_keys = [("q", s0 + jj, b) for jj in range(4)]
                bp, bc = sc_rot.next()
                u = i % 3
                if has_prev:
                    if b > 0:
                        kprev, kpk = kT[hs, kt, (b - 1) * 128:b * 128], ("k", kt, b - 1)
                    else:
                        kprev, kpk = carryK[hs, kt, :], ("kc",)
                    mm(ps[bp][:, :], kprev, q_rhs, True, True, [kpk] + q_keys, [("ps", bp)])
                mm(ps[bc][:, :], kT[hs, kt, bs], q_rhs, True, True, [("k", kt, b)] + q_keys, [("ps", bc)])
                if has_prev:
                    act(eT[:, u, :, :], ps_all[:, bp:bp + 2, :], AF.Exp, [("ps", bp), ("ps", bc)],
                        [("e", u, 0), ("e", u, 1)], scale=0.125)
                else:
                    act(eT[:, u, 1, :], ps[bc][:, :], AF.Exp, [("ps", bc)], [("e", u, 1)], scale=0.125)
                if has_prev:
                    tt(PT[:, u, :, :], eT[:, u, :, :], maskT[:, :, :], ALU.mult,
                       [("e", u, 0), ("e", u, 1), ("maskT",)], [("P", u)])
                else:
                    tt(PT[:, u, 1, :], eT[:, u, 1, :], maskT[:, 1, :], ALU.mult,
                       [("e", u, 1), ("maskT",)], [("P", u)])

            def stage2(i):
                b, g = iters[i]
                has_prev = (st * NB + b) > 0
                half = g % 2
                hs = slice(half * 64, half * 64 + 64)
                s0 = (g // 2) * 4
                bs = slice(b * 128, (b + 1) * 128)
                q_keys = [("q", s0 + jj, b) for jj in range(4)]
                u = i % 3
                w = i % 2
                pvb = pv_rot.next()
                if has_prev:
                    if b > 0:
                        vprev, vpk = Vx[:, b - 1, g, :], ("v", b - 1)
                    else:
                        vprev, vpk = carryV[:, g, :], ("vc",)
                    mm(ps[pvb][:, :], vprev, PT[:, u, 0, :], True, False, [vpk, ("P", u)], [("ps", pvb)])
                mm(ps[pvb][:, :], Vx[:, b, g, :], PT[:, u, 1, :], not has_prev, False,
                   [("v", b), ("P", u)], [("ps", pvb)])
                mm(ps[pvb][:, :], sel[0:1, :], esrow[0:1, g * 512:(g + 1) * 512], False, True,
                   [("sel",), ("esrow",)], [("ps", pvb)])
                act(lnd[64:128, w, :], ps[pvb][64:128, :], AF.Ln, [("ps", pvb)], [("lnd", w)])
                act(Rr[0:64, w, :], lnd[64:128, w, :], AF.Exp, [("lnd", w)], [("R", w)], scale=-1.0)
                tt(qoT[hs, s0:s0 + 4, bs], ps[pvb][0:64, :].rearrange("p (j t) -> p j t", j=4),
                   Rr[0:64, w, :].rearrange("p (j t) -> p j t", j=4), ALU.mult,
                   [("ps", pvb), ("R", w)], q_keys)

            for i in range(len(iters)):
                stage1(i)
                if i >= 1:
                    stage2(i - 1)
            stage2(len(iters) - 1)
            S.add("dve", lambda e: e.tensor_copy(out=carryK[:, :, :], in_=kT[:, :, (NB - 1) * 128:NB * 128]),
                  [("k", 0, NB - 1), ("k", 1, NB - 1)], [("kc",)])
            S.add("dve", lambda e: e.tensor_copy(out=carryV[:, :, :], in_=Vx[:, NB - 1, :, :]),
                  [("v", NB - 1)], [("vc",)])

            pn = PostNorm("gmo0", "bo", "bog")
            orot = Rot([0, 1, 2, 3])
            wo_slabs = w_get(4)

            def wo_tile(grp, c):
                wv, wk = wo_slabs[c // 2]
                mi = c % 2
                cs = slice(grp * 512, (grp + 1) * 512)
                bank = orot.next()
                for k in range(8):
                    mm(ps[bank][:, :], wv[:, (mi * 8 + k) * 128:(mi * 8 + k + 1) * 128], qoT[:, k, cs],
                       k == 0, k == 7, [wk] + [("q", k, b) for b in range(grp * 4, grp * 4 + 4)],
                       [("ps", bank)])
                return bank

            out_phase(pn, wo_tile, ("gfp0", st))
            S.fence_pe()

        def ffn(st, layer, final_store=None):
            cv = Carver()
            gT = cv.take(NF * NT, BF16).rearrange("p (f t) -> p f t", f=NF)
            sg = cv.take(2 * 512, F32).rearrange("p (u t) -> p u t", u=2)
            prenorm("gfp%d" % layer)
            grot = Rot([(0, 1), (2, 3)])
            urot = Rot([0, 1])
            def gu_tile(wv, wk, f, grp):
                cs = slice(grp * 512, (grp + 1) * 512)
                bg_, bu_ = grot.next()
                u = urot.next()
                for k in range(8):
                    mm(ps[bg_][:, :], wv[:, k * 128:(k + 1) * 128], hT[:, k, cs], k == 0, k == 7,
                       [wk, ("h", grp, k)], [("ps", bg_)])
                for k in range(8):
                    mm(ps[bu_][:, :], wv[:, (8 + k) * 128:(9 + k) * 128], hT[:, k, cs], k == 0, k == 7,
                       [wk, ("h", grp, k)], [("ps", bu_)])
                act(sg[:, u, :], ps[bg_][:, :], AF.Silu, [("ps", bg_)], [("sg", u)])
                tt(gT[:, f, cs], ps[bu_][:, :], sg[:, u, :], ALU.mult, [("ps", bu_), ("sg", u)],
                   [("g", f, grp)])

            NSK = 4
            first = w_get(NSK)
            for grp in range(2):
                for f in range(NSK):
                    gu_tile(first[f][0], first[f][1], f, grp)
                    if grp == 0 and f == 1:
                        run_deferred()
            for f in range(NSK, NF):
                wv, wk = w_next()
                for grp in range(2):
                    gu_tile(wv, wk, f, grp)
            pn = PostNorm("gfo%d" % layer)
            orot = Rot([0, 1, 2, 3])

            def dn_tile(grp, c):
                (wv0, wk0), (wv1, wk1) = w_get(2)
                cs = slice(grp * 512, (grp + 1) * 512)
                bank = orot.next()
                for f in range(NF):
                    wv, wk = (wv0, wk0) if f < 11 else (wv1, wk1)
                    kk = f % 11
                    mm(ps[bank][:, :], wv[:, kk * 128:(kk + 1) * 128], gT[:, f, cs], f == 0, f == NF - 1,
                       [wk, ("g", f, grp)], [("ps", bank)])
                return bank

            if layer == 0:
                nxt = ("gmp1", st)
            else:
                nxt = ("gmp0", st + 1) if st + 1 < n_st else None
            out_phase(pn, dn_tile, nxt, final_store, c_pre=3)
            S.fence_pe()

        def sgu(st):
            cv = Carver()
            uT = cv.take(8 * NT, BF16).rearrange("p (c t) -> p c t", c=8)
            vf = cv.take(4 * 1024, F32).rearrange("p (u t) -> p u t", u=4)
            vb = cv.take(4 * 1024, BF16).rearrange("p (u t) -> p u t", u=4)
            tmp = cv.take(2 * 512, F32).rearrange("p (u t) -> p u t", u=2)
            lng = cv.take(1024, F32)
            lnb = cv.take(1024, F32)
            bsp = cv.take(1024, F32)
            fdeps = list(S.last_fence)
            dma("sp", lng[:, :], rows_d[0:1, 272:1296].partition_broadcast(128), [], [("lng",)], sg_sems[0], fdeps)
            dma("sp", lnb[:, :], rows_d[0:1, 1296:2320].partition_broadcast(128), [], [("lnb",)], sg_sems[1], fdeps)
            dma("sp", bsp[:, :], rows_d[0:1, 2320:3344].partition_broadcast(128), [], [("bsp",)], sg_sems[2], fdeps)
            prenorm("gmp1")
            brot = Rot([0, 1, 2, 3])

            def u_tile(wv, wk, i, mi, grp):
                c = 2 * i + mi
                cs = slice(grp * 512, (grp + 1) * 512)
                bank = brot.next()
                for k in range(8):
                    mm(ps[bank][:, :], wv[:, (mi * 8 + k) * 128:(mi * 8 + k + 1) * 128], hT[:, k, cs],
                       k == 0, k == 7, [wk, ("h", grp, k)], [("ps", bank)])
                act(uT[:, c, cs], ps[bank][:, :], AF.Gelu_apprx_tanh, [("ps", bank)],
                    [("u", c, b) for b in range(grp * 4, grp * 4 + 4)])

            first = w_get(4)
            for grp in range(2):
                for i in range(4):
                    for mi in range(2):
                        u_tile(first[i][0], first[i][1], i, mi, grp)
                    if grp == 0 and i == 0:
                        run_deferred()
            vslabs = w_get(4)
            mrot = Rot([(4, 5), (6, 7)])
            trot = Rot([0, 1])

            def stageA(b):
                grp = b // 4
                bs = slice(b * 128, (b + 1) * 128)
                u = b % 4
                for vh in range(2):
                    bank = brot.next()
                    for k in range(8):
                        wv, wk = vslabs[vh * 2 + k // 4]
                        kk = k % 4
                        mm(ps[bank][:, :], hT[:, k, bs], wv[:, kk * 512:(kk + 1) * 512], k == 0, k == 7,
                           [wk, ("h", grp, k)], [("ps", bank)])
                    act(vf[:, u, vh * 512:(vh + 1) * 512], ps[bank][:, :], AF.Gelu_apprx_tanh,
                        [("ps", bank)], [("vf", u, vh)])
                o = 16 * u
                S.add("dve", lambda e, o=o, u=u: e.bn_stats(out=small[:, o:o + 6], in_=vf[:, u, 0:512]),
                      [("vf", u, 0)], [("bn", u, 0)])
                S.add("dve", lambda e, o=o, u=u: e.bn_stats(out=small[:, o + 6:o + 12], in_=vf[:, u, 512:1024]),
                      [("vf", u, 1)], [("bn", u, 1)])
                S.add("dve", lambda e, o=o: e.bn_aggr(out=small[:, o + 12:o + 14], in_=small[:, o:o + 12]),
                      [("bn", u, 0), ("bn", u, 1)], [("mv", u)])
                o2 = 64 + 4 * u
                act(small[:, o2:o2 + 1], small[:, o + 13:o + 14], AF.Ln, [("mv", u)], [("lnv", u)], bias=EPS)
                act(small[:, o2 + 1:o2 + 2], small[:, o2:o2 + 1], AF.Exp, [("lnv", u)], [("rs", u)], scale=-0.5)
                ts(small[:, o2 + 2:o2 + 3], small[:, o + 12:o + 13], -1.0, small[:, o2 + 1:o2 + 2],
                   ALU.mult, ALU.mult, [("mv", u), ("rs", u)], [("nmr", u)])
                vkeys = [("vf", u, 0), ("vf", u, 1)]
                act(vf[:, u, :], vf[:, u, :], AF.Identity, vkeys + [("rs", u), ("nmr", u)], vkeys,
                    scale=small[:, o2 + 1:o2 + 2], bias=small[:, o2 + 2:o2 + 3])
                tt(vf[:, u, :], vf[:, u, :], lng[:, :], ALU.mult, vkeys + [("lng",)], vkeys, eng="pool")
                tt(vb[:, u, :], vf[:, u, :], lnb[:, :], ALU.add, vkeys + [("lnb",)], [("vb", u)], eng="pool")

            def stageB(b):
                bs = slice(b * 128, (b + 1) * 128)
                u = b % 4
                banks = mrot.next()
                for gg in range(8):
                    bank = banks[gg // 4]
                    col = (gg % 4) * 128
                    mm(ps[bank][:, col:col + 128], vb[:, u, gg * 128:(gg + 1) * 128], WsT[:, gg, :], True, True,
                       [("vb", u), ("WsT",)], [("ps", bank)])
                for hb in range(2):
                    bank = banks[hb]
                    tu = trot.next()
                    tt(tmp[:, tu, :], ps[bank][:, :], bsp[:, hb * 512:(hb + 1) * 512], ALU.add,
                       [("ps", bank), ("bsp",)], [("tmp", tu)])
                    ukeys = [("u", 4 * hb + jj, b) for jj in range(4)]
                    tt(uT[:, 4 * hb:4 * hb + 4, bs], uT[:, 4 * hb:4 * hb + 4, bs],
                       tmp[:, tu, :].rearrange("p (j t) -> p j t", j=4), ALU.mult,
                       [("tmp", tu)] + ukeys, ukeys)

            for b in range(NB):
                stageA(b)
                if b >= 3:
                    stageB(b - 3)
            stageB(NB - 3)
            stageB(NB - 2)
            stageB(NB - 1)
            pn = PostNorm("gmo1")
            orot = Rot([0, 1, 2, 3])
            wo_slabs = w_get(4)

            def so_tile(grp, c):
                wv, wk = wo_slabs[c // 2]
                mi = c % 2
                cs = slice(grp * 512, (grp + 1) * 512)
                bank = orot.next()
                for k in range(8):
                    mm(ps[bank][:, :], wv[:, (mi * 8 + k) * 128:(mi * 8 + k + 1) * 128], uT[:, k, cs],
                       k == 0, k == 7, [wk] + [("u", k, b) for b in range(grp * 4, grp * 4 + 4)],
                       [("ps", bank)])
                return bank

            out_phase(pn, so_tile, ("gfp1", st))
            S.fence_pe()

        store_ops = []

        def x_load(st, grp):
            slot = (2 * st + grp) % 4
            t0 = st * NT + grp * 512
            dma("sp", xs[:, slot, :, :], xT_d[:, :, t0:t0 + 512].rearrange("c p t -> p c t"),
                [], [("x", slot, c) for c in range(8)], xl_sems[slot])

        def rope_load(st):
            for grp in range(2):
                t0 = st * NT + grp * 512
                dma("sp", ropeb[:, grp, :, :], rope_d[:, :, t0:t0 + 512].rearrange("a p t -> p a t"),
                    [], [("rope", grp)], rope_sems[grp])

        x_load(0, 0)
        x_load(0, 1)
        rope_load(0)
        for st in range(n_st):
            cur["st"] = st

            def final_store(grp, st=st):
                slot = (2 * st + grp) % 4
                t0 = st * NT + grp * 512
                dma("sp", yT_d[:, :, t0:t0 + 512].rearrange("c p t -> p c t"), xs[:, slot, :, :],
                    [("x", slot, c) for c in range(8)], [], st_sems[slot])
                store_ops.append(S.lastop["sp"])

            def prefetch_next():
                if st + 1 < n_st:
                    x_load(st + 1, 0)
                    x_load(st + 1, 1)
                    rope_load(st + 1)

            def last_phase():
                prefetch_next()
                ffn(st, 1, final_store)

            phases = [lambda: attention(st), lambda: ffn(st, 0), lambda: sgu(st), last_phase]
            nph = 4 if stop_after is None else stop_after
            for pi in range(nph):
                phases[pi]()
            if stop_after is not None and stop_after < 4:
                pre_done["v"] = False
                deferred.clear()
                run_post(99)
                while wstate["used"] < NSEQ * (st + 1):
                    w_next()
                prefetch_next()
                for grp in range(2):
                    final_store(grp)
        fin = S.add("sp", lambda e: None, [], [])
        fin.deps = list(store_ops)

        S.finalize(esems)

        @block.tensor
        def _(e):
            S.run("pe", e)

        @block.scalar
        def _(e):
            S.run("act", e)

        @block.vector
        def _(e):
            S.run("dve", e)

        @block.gpsimd
        def _(e):
            S.run("pool", e)

        @block.sync
        def _(e):
            S.run("sp", e)
    return nc


def _qcols(s):
    hA = 8 * (s // 4) + (s % 4)
    return np.concatenate([np.arange(hA * 64, hA * 64 + 64), np.arange((hA + 4) * 64, (hA + 4) * 64 + 64)])


def _tileB(W):
    nk = W.shape[0] // 128
    return W.reshape(nk, 128, W.shape[1]).transpose(1, 0, 2)


def _prep(inp):
    f = np.float32
    wqkv = np.asarray(inp["attn_w_qkv"], f)[0]
    bqkv = np.asarray(inp["attn_b_qkv"], f)[0]
    wo = np.asarray(inp["attn_w_o"], f)[0]
    w_in = np.asarray(inp["sgu_w_in"], f)[0]
    w_out = np.asarray(inp["sgu_w_out"], f)[0]
    wgu = np.asarray(inp["ffn_w_gate_up"], f)
    wdn = np.asarray(inp["ffn_w_down"], f)
    slabs = np.zeros((NSLAB, 128, SLAB), f)
    idx = 0

    def put(i, arr):
        a = np.ascontiguousarray(arr).reshape(128, -1)
        slabs[i, :, :a.shape[1]] = a

    def ffn_slabs(layer, idx):
        for fi in range(NF):
            g = _tileB(wgu[layer][:, fi * 128:(fi + 1) * 128])
            u = _tileB(wgu[layer][:, 2816 + fi * 128:2816 + (fi + 1) * 128])
            put(idx, np.stack([g, u], axis=1))
            idx += 1
        for c in range(8):
            t = _tileB(wdn[layer][:, c * 128:(c + 1) * 128])
            put(idx, t[:, 0:11])
            idx += 1
            put(idx, t[:, 11:22])
            idx += 1
        return idx

    for j in range(4):
        tiles = [_tileB(wqkv[:, _qcols(2 * j + ti)]) for ti in range(2)]
        put(idx, np.stack(tiles, axis=1))
        idx += 1
    tiles = [_tileB(wqkv[:, 1024 + t * 128:1024 + (t + 1) * 128]) for t in range(2)]
    put(idx, np.stack(tiles, axis=1))
    idx += 1
    put(idx, wqkv[:, 1280:1536].reshape(8, 128, 256).transpose(1, 0, 2))
    idx += 1
    rowperm = np.concatenate([_qcols(k) for k in range(8)])
    wo_p = wo[rowperm, :]
    for i in range(4):
        tiles = [_tileB(wo_p[:, (2 * i + mi) * 128:(2 * i + mi + 1) * 128]) for mi in range(2)]
        put(idx, np.stack(tiles, axis=1))
        idx += 1
    idx = ffn_slabs(0, idx)
    for i in range(4):
        tiles = [_tileB(w_in[:, (2 * i + mi) * 128:(2 * i + mi + 1) * 128]) for mi in range(2)]
        put(idx, np.stack(tiles, axis=1))
        idx += 1
    for vh in range(2):
        for kh in range(2):
            blk = w_in[kh * 512:(kh + 1) * 512, 1024 + vh * 512:1024 + (vh + 1) * 512]
            put(idx, blk.reshape(4, 128, 512).transpose(1, 0, 2))
            idx += 1
    for i in range(4):
        tiles = [_tileB(w_out[:, (2 * i + mi) * 128:(2 * i + mi + 1) * 128]) for mi in range(2)]
        put(idx, np.stack(tiles, axis=1))
        idx += 1
    idx = ffn_slabs(1, idx)
    assert idx == NSLAB

    vecs = np.zeros((128, NV), f)

    def putv(name, vec, n):
        vecs[:, VC[name]:VC[name] + n] = np.asarray(vec, f).reshape(n, 128).T

    for i in range(2):
        putv("gmp%d" % i, inp["norm_mix_pre"][i], 8)
        putv("gmo%d" % i, inp["norm_mix_post"][i], 8)
        putv("gfp%d" % i, inp["norm_ffn_pre"][i], 8)
        putv("gfo%d" % i, inp["norm_ffn_post"][i], 8)
    putv("bo", inp["attn_b_o"][0], 8)
    swp = np.arange(128) ^ 32
    for s in range(8):
        cols = _qcols(s)
        vecs[:, VC["bq"] + s] = bqkv[cols]
        vecs[:, VC["bqs"] + s] = bqkv[cols[swp]]
    for t in range(2):
        cols = 1024 + t * 128 + np.arange(128)
        vecs[:, VC["bk"] + t] = bqkv[cols]
        vecs[:, VC["bks"] + t] = bqkv[cols[swp]]

    rows = np.zeros((1, NROW), f)
    rows[0, 0:256] = bqkv[1280:1536]
    rows[0, 256:272] = np.asarray(inp["attn_sinks"], f)[0]
    rows[0, 272:1296] = np.asarray(inp["sgu_ln_g"], f)[0]
    rows[0, 1296:2320] = np.asarray(inp["sgu_ln_b"], f)[0]
    rows[0, 2320:3344] = np.asarray(inp["sgu_b_spatial"], f)[0].reshape(-1)
    wsp = np.asarray(inp["sgu_w_spatial"], f)[0]
    wsT = np.ascontiguousarray(wsp.transpose(2, 0, 1)).reshape(128, 1024)

    half = 32
    inv_freq = 10000.0 ** (-(np.arange(half, dtype=np.float64) * 2.0) / 64.0)
    pos = np.arange(4096, dtype=np.float64)
    ang = pos[None, :] * inv_freq[:, None]
    cos = np.cos(ang).astype(f)
    sin = np.sin(ang).astype(f)
    p = np.arange(128)
    rope = np.zeros((2, 128, 4096), f)
    rope[0] = cos[p % 32]
    sgn = np.where((p % 64) < 32, -1.0, 1.0).astype(f)
    rope[1] = sin[p % 32] * sgn[:, None]
    cm = np.zeros((128, 1024 + 128), f)
    j = np.arange(128)[:, None]
    i = np.arange(128)[None, :]
    prev = (j > i).astype(f)
    cur = (j <= i).astype(f)
    cm[:, 0:512] = np.tile(prev, (1, 4))
    cm[:, 512:1024] = np.tile(cur, (1, 4))
    cm[:, 1024:1152] = cur
    return dict(wst=slabs, vecs=vecs, rows=rows, wsT=wsT, rope=rope, cmask=cm)


_CACHE = {}


def _run(inputs, n_st, stop_after=None, n_cores=8):
    x = np.asarray(inputs["x"], np.float32)
    shared = _prep(inputs)
    TOK = n_st * NT
    key = (n_st, stop_after)
    if key not in _CACHE:
        _CACHE[key] = build(n_st, stop_after)
    nc = _CACHE[key]
    in_maps = []
    for b in range(n_cores):
        xT = np.ascontiguousarray(x[b, :TOK, :].T).reshape(8, 128, TOK)
        m = dict(shared)
        m["rope"] = np.ascontiguousarray(shared["rope"][:, :, :TOK])
        m["xT"] = xT
        in_maps.append(m)
    res = run_bass_kernel_spmd(nc, in_maps, core_ids=list(range(n_cores)))
    out = np.empty((n_cores, TOK, 1024), np.float32)
    for b in range(n_cores):
        out[b] = res.results[b]["yT"].reshape(1024, TOK).T
    return out


def kernel(**inputs):
    return _run(inputs, 4)
```

```python
import numpy as np
import concourse.bass as bass
import concourse.mybir as mybir
from concourse.bass_utils import run_bass_kernel_spmd

F32 = mybir.dt.float32
BF16 = mybir.dt.bfloat16
AF = mybir.ActivationFunctionType
ALU = mybir.AluOpType

NT = 1024
NB = 8
R = 6
SLAB = 2048
NF = 22
EPS = 1e-6
NSLAB = 98
NROW = 256 + 16 + 3 * 1024

VC = {}
_c = 0
for _n, _w in [("gmp0", 8), ("gmo0", 8), ("gfp0", 8), ("gfo0", 8),
               ("gmp1", 8), ("gmo1", 8), ("gfp1", 8), ("gfo1", 8),
               ("bo", 8), ("bq", 8), ("bqs", 8), ("bk", 2), ("bks", 2), ("bog", 8)]:
    VC[_n] = _c
    _c += _w
NV = _c


class Sem:
    def __init__(self, h):
        self.h = h
        self.count = 0


class Op:
    __slots__ = ("eng", "fn", "deps", "tok", "used", "sem")


class Sched:
    def __init__(self):
        self.ops = {e: [] for e in ("pe", "act", "dve", "pool", "sp")}
        self.lastw = {}
        self.rd = {}
        self.pending = {}
        self.lastop = {}

    def add(self, eng, fn, reads=(), writes=(), dma_sem=None):
        op = Op()
        op.eng = eng
        op.fn = fn
        op.used = False
        op.sem = dma_sem
        op.tok = None
        deps = []
        for k in reads:
            w = self.lastw.get(k)
            if w is not None:
                deps.append(w)
        for k in writes:
            w = self.lastw.get(k)
            if w is not None:
                deps.append(w)
            r = self.rd.get(k)
            if r:
                deps.extend(r.values())
        p = self.pending.pop(eng, None)
        if p:
            deps.extend(p)
        op.deps = list(set(deps))
        rkey = eng if dma_sem is None else id(op)
        for k in reads:
            self.rd.setdefault(k, {})[rkey] = op
        for k in writes:
            self.lastw[k] = op
            self.rd[k] = {}
        if dma_sem is not None:
            dma_sem.count += 16
            op.tok = (dma_sem, dma_sem.count)
        self.ops[eng].append(op)
        self.lastop[eng] = op
        if eng == "pool" and dma_sem is None:
            self.lastop["poolc"] = op
        return op

    def fence(self):
        snap = [self.lastop[e] for e in ("pe", "act", "dve", "poolc") if e in self.lastop]
        self.last_fence = list(snap)
        for e in ("pe", "act", "dve", "pool"):
            self.pending[e] = list(snap)

    def fence_pe(self):
        snap = [self.lastop["pe"]]
        self.last_fence = list(snap)
        for e in ("act", "dve", "pool"):
            self.pending[e] = list(snap)

    def finalize(self, esems):
        for e, lst in self.ops.items():
            for op in lst:
                for d in op.deps:
                    if d.eng == "pe" and e == "pe" and d.sem is None:
                        continue
                    d.used = True
        for e, lst in self.ops.items():
            cnt = 0
            for op in lst:
                if op.sem is None and op.used:
                    cnt += 1
                    op.tok = (esems[e], cnt)

    def run(self, e, eng):
        waited = {}
        for op in self.ops[e]:
            need = {}
            for d in op.deps:
                if d.eng == "pe" and e == "pe" and d.sem is None:
                    continue
                s, v = d.tok
                if need.get(s, 0) < v:
                    need[s] = v
            for s, v in need.items():
                if waited.get(s, 0) < v:
                    eng.wait_ge(s.h, v)
                    waited[s] = v
            ins = op.fn(eng)
            if ins is None:
                continue
            if op.sem is not None:
                ins.then_inc(op.sem.h, 16)
            elif op.used:
                ins.then_inc(op.tok[0].h, 1)


class Rot:
    def __init__(self, items):
        self.items = list(items)
        self.i = 0

    def next(self):
        v = self.items[self.i % len(self.items)]
        self.i += 1
        return v


def build(n_st, stop_after=None):
    nc = bass.Bass("TRN2", target_bir_lowering=False)
    TOK = n_st * NT
    xT_d = nc.dram_tensor("xT", [8, 128, TOK], F32, kind="ExternalInput").ap()
    wst_d = nc.dram_tensor("wst", [NSLAB, 128, SLAB], F32, kind="ExternalInput").ap()
    vecs_d = nc.dram_tensor("vecs", [128, NV], F32, kind="ExternalInput").ap()
    rope_d = nc.dram_tensor("rope", [2, 128, TOK], F32, kind="ExternalInput").ap()
    rows_d = nc.dram_tensor("rows", [1, NROW], F32, kind="ExternalInput").ap()
    wsT_d = nc.dram_tensor("wsT", [128, 1024], F32, kind="ExternalInput").ap()
    cm_d = nc.dram_tensor("cmask", [128, 1024 + 128], F32, kind="ExternalInput").ap()
    yT_d = nc.dram_tensor("yT", [8, 128, TOK], F32, kind="ExternalOutput").ap()
    scr_d = nc.dram_tensor("wscr", [NSLAB, 128, SLAB], BF16, kind="Internal").ap()

    ARENA = 30720
    import contextlib
    with contextlib.ExitStack() as es:
        def sb(name, shape, dt):
            return es.enter_context(nc.sbuf_tensor(name, shape, dt))

        def sem(name):
            return Sem(es.enter_context(nc.semaphore(name)))

        xs = sb("xT_sb", [128, 4, 8, 512], F32)
        hm = sb("hm", [128, 8192], F32)
        sq = sb("sq", [128, 3, 512], BF16)
        rstd = sb("rstd", [128, 2, 512], F32)
        ring = sb("ring", [128, R, SLAB], BF16)
        ropeb = sb("ropeb", [128, 2, 2, 512], F32)
        vecs = sb("vecs_sb", [128, NV], F32)
        sinks = sb("sinks", [128, 16], F32)
        esink = sb("esink", [128, 16], F32)
        bvt = sb("bvt", [128, 256], F32)
        maskT = sb("maskT", [128, 2, 512], BF16)
        ones = sb("ones", [128, 128], BF16)
        sel = sb("sel", [1, 128], BF16)
        esrow = sb("esrow", [1, 2048], BF16)
        carryK = sb("carryK", [128, 2, 128], BF16)
        carryV = sb("carryV", [128, 4, 128], BF16)
        WsT = sb("WsT", [128, 8, 128], BF16)
        small = sb("small", [128, 96], F32)
        arena = sb("arena", [128, ARENA], BF16)
        ps_all = es.enter_context(nc.psum_tensor("psall", [128, 8, 512], F32))
        ps = [ps_all[:, i, :] for i in range(8)]

        esems = {e: sem("s_" + e) for e in ("pe", "act", "dve", "pool", "sp")}
        ring_sems = [sem("ring%d" % i) for i in range(R)]
        scr_sems = [sem("scr%d" % i) for i in range(R)]
        xl_sems = [sem("xl%d" % i) for i in range(4)]
        st_sems = [sem("store%d" % i) for i in range(4)]
        rope_sems = [sem("rope%d" % i) for i in range(2)]
        c_sems = [sem("const%d" % i) for i in range(10)]
        sg_sems = [sem("sguc%d" % i) for i in range(3)]

        block = es.enter_context(nc.Block())
        S = Sched()

        hT = hm[:, 0:4096].bitcast(BF16).rearrange("p (c t) -> p c t", c=8)
        mT = hm[:].rearrange("p (g c t) -> p g c t", g=2, c=8)

        class Carver:
            def __init__(self):
                self.off = 0

            def take(self, nelem, dt):
                nb = nelem * (4 if dt == F32 else 2)
                a = self.off // 2
                self.off += nb
                assert self.off <= ARENA * 2, "arena overflow"
                v = arena[:, a:a + nb // 2]
                return v.bitcast(F32) if dt == F32 else v

        def mm(out, lhsT, rhs, start, stop, reads, writes):
            S.add("pe", lambda e, o=out, l=lhsT, r=rhs, a=start, b=stop: e.matmul(o, l, r, start=a, stop=b),
                  reads, writes)

        def act(out, in_, func, reads, writes, scale=None, bias=None):
            kw = {}
            if scale is not None:
                kw["scale"] = scale
            if bias is not None:
                kw["bias"] = bias
            S.add("act", lambda e, o=out, i=in_, f=func, kw=kw: e.activation(out=o, in_=i, func=f, **kw),
                  reads, writes)

        def tt(out, in0, in1, op, reads, writes, eng="dve"):
            S.add(eng, lambda e, o=out, a=in0, b=in1, p=op: e.tensor_tensor(out=o, in0=a, in1=b, op=p),
                  reads, writes)

        def stt(out, in0, scalar, in1, op0, op1, reads, writes):
            S.add("dve", lambda e, o=out, a=in0, s=scalar, b=in1, p0=op0, p1=op1:
                  e.scalar_tensor_tensor(out=o, in0=a, scalar=s, in1=b, op0=p0, op1=p1), reads, writes)

        def ts(out, in0, s1, s2, op0, op1, reads, writes):
            S.add("dve", lambda e, o=out, a=in0, x=s1, y=s2, p0=op0, p1=op1:
                  e.tensor_scalar(out=o, in0=a, scalar1=x, scalar2=y, op0=p0, op1=p1), reads, writes)

        def dma(q, out, in_, reads, writes, s, extra_deps=None):
            op = S.add(q, lambda e, o=out, i=in_: e.dma_start(out=o, in_=i), reads, writes, dma_sem=s)
            if extra_deps:
                op.deps = list(set(op.deps) | set(extra_deps))
            return op

        vcol = lambda name, c: vecs[:, VC[name] + c:VC[name] + c + 1]

        wstate = {"issued": 0, "used": 0}
        SEQ = (list(range(0, 10)) + list(range(10, 32)) + list(range(32, 48)) * 2 +
               list(range(48, 60)) + list(range(60, 82)) + list(range(82, 98)) * 2)
        NSEQ = len(SEQ)
        TOTAL_SLABS = NSEQ * n_st
        scr_stored = set()

        def slab_width(j):
            if (32 <= j < 48) or (82 <= j < 98):
                return 11 * 128
            return SLAB

        def w_issue():
            i = wstate["issued"]
            if i >= TOTAL_SLABS:
                return
            slot = i % R
            j = SEQ[i % NSEQ]
            wd = slab_width(j)
            if j not in scr_stored:
                dma("pool", ring[:, slot, 0:wd], wst_d[j, :, 0:wd], [], [("w", slot)], ring_sems[slot])
            else:
                dma("pool", ring[:, slot, 0:wd], scr_d[j, :, 0:wd], [("scr", j)], [("w", slot)], ring_sems[slot])
            wstate["issued"] += 1

        def w_get(count):
            n = wstate["used"]
            while wstate["issued"] < min(n + R, TOTAL_SLABS):
                w_issue()
            wstate["used"] += count
            for i in range(n, n + count):
                j = SEQ[i % NSEQ]
                if j not in scr_stored:
                    wd = slab_width(j)
                    dma("sp", scr_d[j, :, 0:wd], ring[:, i % R, 0:wd], [("w", i % R)], [("scr", j)],
                        scr_sems[i % R])
                    scr_stored.add(j)
            return [(ring[:, (n + i) % R, :], ("w", (n + i) % R)) for i in range(count)]

        def w_next():
            return w_get(1)[0]

        dma("sp", vecs[:, :], vecs_d[:, :], [], [("vecs",)], c_sems[0])
        dma("sp", bvt[:, :], rows_d[0:1, 0:256].partition_broadcast(128), [], [("bvt",)], c_sems[1])
        dma("sp", sinks[:, :], rows_d[0:1, 256:272].partition_broadcast(128), [], [("sinks",)], c_sems[2])
        _cv0 = Carver()
        wstage = _cv0.take(1024, F32)
        tril = _cv0.take(128, F32)
        dma("sp", wstage[:, :], wsT_d[:, :], [], [("wstage",)], c_sems[6])
        dma("sp", tril[:, :], cm_d[:, 1024:1152], [], [("tril",)], c_sems[7])
        dma("pool", maskT[:].rearrange("p a b -> p (a b)"), cm_d[:, 0:1024], [], [("maskT",)], c_sems[8])
        for _ in range(R):
            w_issue()
        S.add("dve", lambda e: e.memset(ones[:], 1.0), [], [("ones",)])
        act(esink[:, :], sinks[:, :], AF.Exp, [("sinks",)], [("esink",)])
        S.add("dve", lambda e: e.memset(sel[0:1, 0:64], 0.0), [], [("sel",)])
        S.add("dve", lambda e: e.memset(sel[0:1, 64:128], 1.0), [], [("sel",)])
        for h in range(16):
            ts(esrow[0:1, h * 128:(h + 1) * 128], ones[0:1, 0:128], esink[0:1, h:h + 1], None, ALU.mult, ALU.bypass,
               [("ones",), ("esink",)], [("esrow",)])
        for g in range(8):
            tt(WsT[:, g, :], wstage[:, g * 128:(g + 1) * 128], tril[:, :], ALU.mult,
               [("wstage",), ("tril",)], [("WsT",)])
        tt(vecs[:, VC["bog"]:VC["bog"] + 8], vecs[:, VC["bo"]:VC["bo"] + 8], vecs[:, VC["gmo0"]:VC["gmo0"] + 8],
           ALU.mult, [("vecs",)], [("vecs",)])
        S.fence()

        sq_rot = Rot([0, 1, 2])
        cur = {"st": 0}

        def xslot(grp):
            return (2 * cur["st"] + grp) % 4

        def X(grp, c):
            return xs[:, xslot(grp), c, :]

        def xkey(grp, c):
            return ("x", xslot(grp), c)

        pre_done = {"v": False}
        deferred = []

        post_q = []

        def run_post(n):
            for _ in range(min(n, len(post_q))):
                post_q.pop(0)()

        def run_deferred():
            run_post(99)
            while deferred:
                deferred.pop(0)()

        def prenorm_group(gname, grp, st=None, bank=None):
            if st is None:
                st = cur["st"]
            if bank is None:
                bank = 6 + grp
            slot = (2 * st + grp) % 4
            cs = slice(grp * 512, (grp + 1) * 512)
            for c in range(8):
                r = sq_rot.next()
                act(sq[:, r, :], xs[:, slot, c, :], AF.Square, [("x", slot, c)], [("sq", r)])
                mm(ps[bank][:, :], ones[:, :], sq[:, r, :], c == 0, c == 7,
                   [("ones",), ("sq", r)], [("ps", bank)])
            act(rstd[:, grp, :], ps[bank][:, :], AF.Ln, [("ps", bank)], [("rstd", grp)],
                scale=1.0 / 1024.0, bias=EPS)
            act(rstd[:, grp, :], rstd[:, grp, :], AF.Exp, [("rstd", grp)], [("rstd", grp)], scale=-0.5)
            for c in range(8):
                stt(hT[:, c, cs], xs[:, slot, c, :], vcol(gname, c), rstd[:, grp, :], ALU.mult, ALU.mult,
                    [("x", slot, c), ("rstd", grp), ("vecs",)], [("h", grp, c), ("m", 0, c)])

        class PreTask:
            def __init__(self, gname, grp, st, bank):
                self.gname, self.grp, self.st, self.bank = gname, grp, st, bank
                self.slot = (2 * st + grp) % 4

            def square(self, c):
                r = sq_rot.next()
                act(sq[:, r, :], xs[:, self.slot, c, :], AF.Square, [("x", self.slot, c)], [("sq", r)])
                return r

            def stats(self, c, r):
                mm(ps[self.bank][:, :], ones[:, :], sq[:, r, :], c == 0, c == 7,
                   [("ones",), ("sq", r)], [("ps", self.bank)])

            def fin(self):
                grp = self.grp
                cs = slice(grp * 512, (grp + 1) * 512)
                act(rstd[:, grp, :], ps[self.bank][:, :], AF.Ln, [("ps", self.bank)], [("rstd", grp)],
                    scale=1.0 / 1024.0, bias=EPS)
                act(rstd[:, grp, :], rstd[:, grp, :], AF.Exp, [("rstd", grp)], [("rstd", grp)], scale=-0.5)
                for c in range(8):
                    stt(hT[:, c, cs], xs[:, self.slot, c, :], vcol(self.gname, c), rstd[:, grp, :],
                        ALU.mult, ALU.mult, [("x", self.slot, c), ("rstd", grp), ("vecs",)],
                        [("h", grp, c), ("m", 0, c)])

        def prenorm(gname):
            if pre_done["v"]:
                pre_done["v"] = False
                return
            for grp in range(2):
                prenorm_group(gname, grp)

        class PostNorm:
            def __init__(self, gname, bias_name=None, bg_name=None):
                self.gname = gname
                self.bias_name = bias_name
                self.bg_name = bg_name
                self.pending = None
                self.count = [0, 0]

            def tile(self, bank, grp, c):
                self.flush()
                r = sq_rot.next()
                b_sq = vcol(self.bias_name, c) if self.bias_name else None
                b_mg = vcol(self.bg_name, c) if self.bg_name else None
                act(sq[:, r, :], ps[bank][:, :], AF.Square, [("ps", bank), ("vecs",)], [("sq", r)], bias=b_sq)
                mkeys = [("m", grp, c)] + ([("h", 0, c), ("h", 1, c)] if grp == 0 else [])
                act(mT[:, grp, c, :], ps[bank][:, :], AF.Identity, [("ps", bank), ("vecs",)], mkeys,
                    scale=vcol(self.gname, c), bias=b_mg)
                self.pending = (r, grp)

            def flush(self):
                if self.pending is None:
                    return
                r, grp = self.pending
                n = self.count[grp]
                mm(ps[6 + grp][:, :], ones[:, :], sq[:, r, :], n == 0, n == 7,
                   [("ones",), ("sq", r)], [("ps", 6 + grp)])
                self.count[grp] += 1
                self.pending = None

            def post_group(self, grp, final_store=None, defer=False):
                act(rstd[:, grp, :], ps[6 + grp][:, :], AF.Ln, [("ps", 6 + grp)], [("rstd", grp)],
                    scale=1.0 / 1024.0, bias=EPS)
                act(rstd[:, grp, :], rstd[:, grp, :], AF.Exp, [("rstd", grp)], [("rstd", grp)], scale=-0.5)
                st_now = cur["st"]

                def chunk(c):
                    slot = (2 * st_now + grp) % 4
                    xa = xs[:, slot, c, :]
                    tt(mT[:, grp, c, :], mT[:, grp, c, :], rstd[:, grp, :], ALU.mult,
                       [("m", grp, c), ("rstd", grp)], [("m", grp, c)])
                    tt(xa, xa, mT[:, grp, c, :], ALU.add, [("x", slot, c), ("m", grp, c)], [("x", slot, c)])
                    if c == 7 and final_store is not None:
                        final_store(grp)

                for c in range(8):
                    if defer:
                        post_q.append(lambda c=c: chunk(c))
                    else:
                        chunk(c)

        def out_phase(pn, tile_fn, nxt, final_store=None, c_pre=4):
            for c in range(8):
                pn.tile(tile_fn(0, c), 0, c)
            pn.tile(tile_fn(1, 0), 1, 0)
            pn.post_group(0, final_store)
            early = nxt is not None and nxt[1] != cur["st"]
            tasks, plan = [], {}
            if nxt is not None:
                t0 = 1 if early else 2
                tasks.append(PreTask(nxt[0], 0, nxt[1], 6))
                for i in range(4):
                    plan.setdefault(t0 + i, []).extend([(0, 2 * i), (0, 2 * i + 1)])
                if early:
                    tasks.append(PreTask(nxt[0], 1, nxt[1], 6))
                    for i in range(3):
                        plan.setdefault(5 + i, []).extend([(1, 2 * i), (1, 2 * i + 1)])
            for t in range(1, 8):
                issued = [(ti, c, tasks[ti].square(c)) for ti, c in plan.get(t, [])]
                bank = tile_fn(1, t)
                for ti, c, r in issued:
                    tasks[ti].stats(c, r)
                    if c == 7:
                        tasks[ti].fin()
                pn.tile(bank, 1, t)
            pn.flush()
            if early:
                for c in (6, 7):
                    tasks[1].stats(c, tasks[1].square(c))
                tasks[1].fin()
            pn.post_group(1, final_store, defer=early)
            if nxt is not None:
                if not early:
                    deferred.append(lambda: prenorm_group(nxt[0], 1, nxt[1]))
                pre_done["v"] = True

        def attention(st):
            cv = Carver()
            qoT = cv.take(8 * NT, BF16).rearrange("p (s t) -> p s t", s=8)
            kT = cv.take(2 * NT, BF16).rearrange("p (s t) -> p s t", s=2)
            Vx = cv.take(NB * 4 * 128, BF16).rearrange("p (b g d) -> p b g d", b=NB, g=4)
            eT = cv.take(3 * 2 * 512, BF16).rearrange("p (u a t) -> p u a t", u=3, a=2)
            PT = cv.take(3 * 2 * 512, BF16).rearrange("p (u a t) -> p u a t", u=3, a=2)
            qs = cv.take(2 * 512, F32).rearrange("p (u t) -> p u t", u=2)
            t1 = cv.take(2 * 512, F32).rearrange("p (u t) -> p u t", u=2)
            t2 = cv.take(2 * 512, F32).rearrange("p (u t) -> p u t", u=2)
            lnd = cv.take(2 * 512, F32).rearrange("p (u t) -> p u t", u=2)
            Rr = cv.take(2 * 512, F32).rearrange("p (u t) -> p u t", u=2)

            prenorm("gmp0")
            S.add("dve", lambda e: e.memset(Vx[:, :, :, 64:128], 1.0), [], [("v", b) for b in range(NB)])
            if st == 0:
                S.add("dve", lambda e: e.memset(carryV[:, :, :], 0.0), [], [("vc",)])
                S.add("dve", lambda e: e.memset(carryK[:, :, :], 0.0), [], [("kc",)])

            bank_rot = Rot([0, 1, 2, 3])
            tmp_rot = Rot([0, 1])

            def qk_tile(wv, wk, ti, grp, bias_col, bias_sw_col, out_ap, out_keys):
                cs = slice(grp * 512, (grp + 1) * 512)
                bank = bank_rot.next()
                for k in range(8):
                    mm(ps[bank][:, :], wv[:, (ti * 8 + k) * 128:(ti * 8 + k + 1) * 128], hT[:, k, cs],
                       k == 0, k == 7, [wk, ("h", grp, k)], [("ps", bank)])
                u = tmp_rot.next()
                for a in range(4):
                    b = a ^ 1
                    act(qs[32 * a:32 * a + 32, u, :], ps[bank][32 * b:32 * b + 32, :], AF.Copy,
                        [("ps", bank)], [("qs", u, a)])
                stt(t1[:, u, :], ps[bank][:, :], bias_col, ropeb[:, grp, 0, :], ALU.add, ALU.mult,
                    [("ps", bank), ("rope", grp), ("vecs",)] + [("qs", u, a) for a in range(4)], [("t1", u)])
                stt(t2[:, u, :], qs[:, u, :], bias_sw_col, ropeb[:, grp, 1, :], ALU.add, ALU.mult,
                    [("qs", u, a) for a in range(4)] + [("rope", grp), ("vecs",)], [("t2", u)])
                tt(out_ap, t1[:, u, :], t2[:, u, :], ALU.add, [("t1", u), ("t2", u)], out_keys)

            def q_tile(wv, wk, j, ti, grp):
                s = 2 * j + ti
                cs = slice(grp * 512, (grp + 1) * 512)
                keys = [("q", s, b) for b in range(grp * 4, grp * 4 + 4)]
                qk_tile(wv, wk, ti, grp, vcol("bq", s), vcol("bqs", s), qoT[:, s, cs], keys)

            first = w_get(4)
            for grp in range(2):
                for j in range(4):
                    for ti in range(2):
                        q_tile(first[j][0], first[j][1], j, ti, grp)
                        run_post(2)
                    if grp == 0 and j == 0:
                        run_deferred()
            (kwv, kwk), (vwv, vwk) = w_get(2)

            def v_round(b0):
                vb_banks = [4, 5, 6, 7]
                grp = b0 // 4
                for k in range(8):
                    for bb in range(4):
                        b = b0 + bb
                        mm(ps[vb_banks[bb]][:, 0:256], hT[:, k, b * 128:(b + 1) * 128], vwv[:, k * 256:(k + 1) * 256],
                           k == 0, k == 7, [vwk, ("h", grp, k)], [("ps", vb_banks[bb])])
                for bb in range(4):
                    b = b0 + bb
                    tt(Vx[:, b, :, 0:64], ps[vb_banks[bb]][:, 0:256].rearrange("p (g d) -> p g d", g=4),
                       bvt[:, :].rearrange("p (g d) -> p g d", g=4), ALU.add,
                       [("ps", vb_banks[bb]), ("bvt",)], [("v", b)])

            v_round(0)
            for ti in range(2):
                for grp in range(2):
                    cs = slice(grp * 512, (grp + 1) * 512)
                    keys = [("k", ti, b) for b in range(grp * 4, grp * 4 + 4)]
                    qk_tile(kwv, kwk, ti, grp, vcol("bk", ti), vcol("bks", ti), kT[:, ti, cs], keys)
            v_round(4)

            sc_rot = Rot([(0, 1), (2, 3)])
            pv_rot = Rot([4, 5])
            iters = [(b, g) for b in range(NB) for g in range(4)]

            def stage1(i):
                b, g = iters[i]
                has_prev = (st * NB + b) > 0
                half = g % 2
                hs = slice(half * 64, half * 64 + 64)
                s0 = (g // 2) * 4
                kt = g // 2
                bs = slice(b * 128, (b + 1) * 128)
                q_rhs = qoT[hs, s0:s0 + 4, bs]
                q_keys = [("q", s0 + jj, b) for jj in range(4)]
                bp, bc = sc_rot.next()
                u = i % 3
                if has_prev:
                    if b > 0:
                        kprev, kpk = kT[hs, kt, (b - 1) * 128:b * 128], ("k", kt, b - 1)
                    else:
                        kprev, kpk = carryK[hs, kt, :], ("kc",)
                    mm(ps[bp][:, :], kprev, q_rhs, True, True, [kpk] + q_keys, [("ps", bp)])
                mm(ps[bc][:, :], kT[hs, kt, bs], q_rhs, True, True, [("k", kt, b)] + q_keys, [("ps", bc)])
                if has_prev:
                    act(eT[:, u, :, :], ps_all[:, bp:bp + 2, :], AF.Exp, [("ps", bp), ("ps", bc)],
                        [("e", u, 0), ("e", u, 1)], scale=0.125)
                else:
                    act(eT[:, u, 1, :], ps[bc][:, :], AF.Exp, [("ps", bc)], [("e", u, 1)], scale=0.125)
                if has_prev:
                    tt(PT[:, u, :, :], eT[:, u, :, :], maskT[:, :, :], ALU.mult,
                       [("e", u, 0), ("e", u, 1), ("maskT",)], [("P", u)])
                else:
                    tt(PT[:, u, 1, :], eT[:, u, 1, :], maskT[:, 1, :], ALU.mult,
                       [("e", u, 1), ("maskT",)], [("P", u)])

            def stage2(i):
                b, g = iters[i]
                has_prev = (st * NB + b) > 0
                half = g % 2
                hs = slice(half * 64, half * 64 + 64)
                s0 = (g // 2) * 4
                bs = slice(b * 128, (b + 1) * 128)
                q_keys = [("q", s0 + jj, b) for jj in range(4)]
                u = i % 3
                w = i % 2
                pvb = pv_rot.next()
                if has_prev:
                    if b > 0:
                        vprev, vpk = Vx[:, b - 1, g, :], ("v", b - 1)
                    else:
                        vprev, vpk = carryV[:, g, :], ("vc",)
                    mm(ps[pvb][:, :], vprev, PT[:, u, 0, :], True, False, [vpk, ("P", u)], [("ps", pvb)])
                mm(ps[pvb][:, :], Vx[:, b, g, :], PT[:, u, 1, :], not has_prev, False,
                   [("v", b), ("P", u)], [("ps", pvb)])
                mm(ps[pvb][:, :], sel[0:1, :], esrow[0:1, g * 512:(g + 1) * 512], False, True,
                   [("sel",), ("esrow",)], [("ps", pvb)])
                act(lnd[64:128, w, :], ps[pvb][64:128, :], AF.Ln, [("ps", pvb)], [("lnd", w)])
                act(Rr[0:64, w, :], lnd[64:128, w, :], AF.Exp, [("lnd", w)], [("R", w)], scale=-1.0)
                tt(qoT[hs, s0:s0 + 4, bs], ps[pvb][0:64, :].rearrange("p (j t) -> p j t", j=4),
                   Rr[0:64, w, :].rearrange("p (j t) -> p j t", j=4), ALU.mult,
                   [("ps", pvb), ("R", w)], q_keys)

            for i in range(len(iters)):
                stage1(i)
                if i >= 1:
                    stage2(i - 1)
            stage2(len(iters) - 1)
            S.add("dve", lambda e: e.tensor_copy(out=carryK[:, :, :], in_=kT[:, :, (NB - 1) * 128:NB * 128]),
                  [("k", 0, NB - 1), ("k", 1, NB - 1)], [("kc",)])
            S.add("dve", lambda e: e.tensor_copy(out=carryV[:, :, :], in_=Vx[:, NB - 1, :, :]),
                  [("v", NB - 1)], [("vc",)])

            pn = PostNorm("gmo0", "bo", "bog")
            orot = Rot([0, 1, 2, 3])
            wo_slabs = w_get(4)

            def wo_tile(grp, c):
                wv, wk = wo_slabs[c // 2]
                mi = c % 2
                cs = slice(grp * 512, (grp + 1) * 512)
                bank = orot.next()
                for k in range(8):
                    mm(ps[bank][:, :], wv[:, (mi * 8 + k) * 128:(mi * 8 + k + 1) * 128], qoT[:, k, cs],
                       k == 0, k == 7, [wk] + [("q", k, b) for b in range(grp * 4, grp * 4 + 4)],
                       [("ps", bank)])
                return bank

            out_phase(pn, wo_tile, ("gfp0", st))
            S.fence_pe()

        def ffn(st, layer, final_store=None):
            cv = Carver()
            gT = cv.take(NF * NT, BF16).rearrange("p (f t) -> p f t", f=NF)
            sg = cv.take(2 * 512, F32).rearrange("p (u t) -> p u t", u=2)
            prenorm("gfp%d" % layer)
            grot = Rot([(0, 1), (2, 3)])
            urot = Rot([0, 1])
            def gu_tile(wv, wk, f, grp):
                cs = slice(grp * 512, (grp + 1) * 512)
                bg_, bu_ = grot.next()
                u = urot.next()
                for k in range(8):
                    mm(ps[bg_][:, :], wv[:, k * 128:(k + 1) * 128], hT[:, k, cs], k == 0, k == 7,
                       [wk, ("h", grp, k)], [("ps", bg_)])
                for k in range(8):
                    mm(ps[bu_][:, :], wv[:, (8 + k) * 128:(9 + k) * 128], hT[:, k, cs], k == 0, k == 7,
                       [wk, ("h", grp, k)], [("ps", bu_)])
                act(sg[:, u, :], ps[bg_][:, :], AF.Silu, [("ps", bg_)], [("sg", u)])
                tt(gT[:, f, cs], ps[bu_][:, :], sg[:, u, :], ALU.mult, [("ps", bu_), ("sg", u)],
                   [("g", f, grp)])

            NSK = 4
            first = w_get(NSK)
            for grp in range(2):
                for f in range(NSK):
                    gu_tile(first[f][0], first[f][1], f, grp)
                    if grp == 0 and f == 1:
                        run_deferred()
            for f in range(NSK, NF):
                wv, wk = w_next()
                for grp in range(2):
                    gu_tile(wv, wk, f, grp)
            pn = PostNorm("gfo%d" % layer)
            orot = Rot([0, 1, 2, 3])

            def dn_tile(grp, c):
                (wv0, wk0), (wv1, wk1) = w_get(2)
                cs = slice(grp * 512, (grp + 1) * 512)
                bank = orot.next()
                for f in range(NF):
                    wv, wk = (wv0, wk0) if f < 11 else (wv1, wk1)
                    kk = f % 11
                    mm(ps[bank][:, :], wv[:, kk * 128:(kk + 1) * 128], gT[:, f, cs], f == 0, f == NF - 1,
                       [wk, ("g", f, grp)], [("ps", bank)])
                return bank

            if layer == 0:
                nxt = ("gmp1", st)
            else:
                nxt = ("gmp0", st + 1) if st + 1 < n_st else None
            out_phase(pn, dn_tile, nxt, final_store, c_pre=3)
            S.fence_pe()

        def sgu(st):
            cv = Carver()
            uT = cv.take(8 * NT, BF16).rearrange("p (c t) -> p c t", c=8)
            vf = cv.take(4 * 1024, F32).rearrange("p (u t) -> p u t", u=4)
            vb = cv.take(4 * 1024, BF16).rearrange("p (u t) -> p u t", u=4)
            tmp = cv.take(2 * 512, F32).rearrange("p (u t) -> p u t", u=2)
            lng = cv.take(1024, F32)
            lnb = cv.take(1024, F32)
            bsp = cv.take(1024, F32)
            fdeps = list(S.last_fence)
            dma("sp", lng[:, :], rows_d[0:1, 272:1296].partition_broadcast(128), [], [("lng",)], sg_sems[0], fdeps)
            dma("sp", lnb[:, :], rows_d[0:1, 1296:2320].partition_broadcast(128), [], [("lnb",)], sg_sems[1], fdeps)
            dma("sp", bsp[:, :], rows_d[0:1, 2320:3344].partition_broadcast(128), [], [("bsp",)], sg_sems[2], fdeps)
            prenorm("gmp1")
            brot = Rot([0, 1, 2, 3])

            def u_tile(wv, wk, i, mi, grp):
                c = 2 * i + mi
                cs = slice(grp * 512, (grp + 1) * 512)
                bank = brot.next()
                for k in range(8):
                    mm(ps[bank][:, :], wv[:, (mi * 8 + k) * 128:(mi * 8 + k + 1) * 128], hT[:, k, cs],
                       k == 0, k == 7, [wk, ("h", grp, k)], [("ps", bank)])
                act(uT[:, c, cs], ps[bank][:, :], AF.Gelu_apprx_tanh, [("ps", bank)],
                    [("u", c, b) for b in range(grp * 4, grp * 4 + 4)])

            first = w_get(4)
            for grp in range(2):
                for i in range(4):
                    for mi in range(2):
                        u_tile(first[i][0], first[i][1], i, mi, grp)
                    if grp == 0 and i == 0:
                        run_deferred()
            vslabs = w_get(4)
            mrot = Rot([(4, 5), (6, 7)])
            trot = Rot([0, 1])

            def stageA(b):
                grp = b // 4
                bs = slice(b * 128, (b + 1) * 128)
                u = b % 4
                for vh in range(2):
                    bank = brot.next()
                    for k in range(8):
                        wv, wk = vslabs[vh * 2 + k // 4]
                        kk = k % 4
                        mm(ps[bank][:, :], hT[:, k, bs], wv[:, kk * 512:(kk + 1) * 512], k == 0, k == 7,
                           [wk, ("h", grp, k)], [("ps", bank)])
                    act(vf[:, u, vh * 512:(vh + 1) * 512], ps[bank][:, :], AF.Gelu_apprx_tanh,
                        [("ps", bank)], [("vf", u, vh)])
                o = 16 * u
                S.add("dve", lambda e, o=o, u=u: e.bn_stats(out=small[:, o:o + 6], in_=vf[:, u, 0:512]),
                      [("vf", u, 0)], [("bn", u, 0)])
                S.add("dve", lambda e, o=o, u=u: e.bn_stats(out=small[:, o + 6:o + 12], in_=vf[:, u, 512:1024]),
                      [("vf", u, 1)], [("bn", u, 1)])
                S.add("dve", lambda e, o=o: e.bn_aggr(out=small[:, o + 12:o + 14], in_=small[:, o:o + 12]),
                      [("bn", u, 0), ("bn", u, 1)], [("mv", u)])
                o2 = 64 + 4 * u
                act(small[:, o2:o2 + 1], small[:, o + 13:o + 14], AF.Ln, [("mv", u)], [("lnv", u)], bias=EPS)
                act(small[:, o2 + 1:o2 + 2], small[:, o2:o2 + 1], AF.Exp, [("lnv", u)], [("rs", u)], scale=-0.5)
                ts(small[:, o2 + 2:o2 + 3], small[:, o + 12:o + 13], -1.0, small[:, o2 + 1:o2 + 2],
                   ALU.mult, ALU.mult, [("mv", u), ("rs", u)], [("nmr", u)])
                vkeys = [("vf", u, 0), ("vf", u, 1)]
                act(vf[:, u, :], vf[:, u, :], AF.Identity, vkeys + [("rs", u), ("nmr", u)], vkeys,
                    scale=small[:, o2 + 1:o2 + 2], bias=small[:, o2 + 2:o2 + 3])
                tt(vf[:, u, :], vf[:, u, :], lng[:, :], ALU.mult, vkeys + [("lng",)], vkeys, eng="pool")
                tt(vb[:, u, :], vf[:, u, :], lnb[:, :], ALU.add, vkeys + [("lnb",)], [("vb", u)], eng="pool")

            def stageB(b):
                bs = slice(b * 128, (b + 1) * 128)
                u = b % 4
                banks = mrot.next()
                for gg in range(8):
                    bank = banks[gg // 4]
                    col = (gg % 4) * 128
                    mm(ps[bank][:, col:col + 128], vb[:, u, gg * 128:(gg + 1) * 128], WsT[:, gg, :], True, True,
                       [("vb", u), ("WsT",)], [("ps", bank)])
                b0_ = banks[0]
                tt(tmp[:, :, :], ps_all[:, b0_:b0_ + 2, :], bsp[:, :].rearrange("p (a t) -> p a t", a=2), ALU.add,
                   [("ps", banks[0]), ("ps", banks[1]), ("bsp",)], [("tmp", 0), ("tmp", 1)])
                ukeys = [("u", jj, b) for jj in range(8)]
                tt(uT[:, 0:8, bs], uT[:, 0:8, bs],
                   tmp[:, :, :].rearrange("p a (j t) -> p (a j) t", j=4), ALU.mult,
                   [("tmp", 0), ("tmp", 1)] + ukeys, ukeys)

            for b in range(NB):
                stageA(b)
                if b >= 3:
                    stageB(b - 3)
            stageB(NB - 3)
            stageB(NB - 2)
            stageB(NB - 1)
            pn = PostNorm("gmo1")
            orot = Rot([0, 1, 2, 3])
            wo_slabs = w_get(4)

            def so_tile(grp, c):
                wv, wk = wo_slabs[c // 2]
                mi = c % 2
                cs = slice(grp * 512, (grp + 1) * 512)
                bank = orot.next()
                for k in range(8):
                    mm(ps[bank][:, :], wv[:, (mi * 8 + k) * 128:(mi * 8 + k + 1) * 128], uT[:, k, cs],
                       k == 0, k == 7, [wk] + [("u", k, b) for b in range(grp * 4, grp * 4 + 4)],
                       [("ps", bank)])
                return bank

            out_phase(pn, so_tile, ("gfp1", st))
            S.fence_pe()

        store_ops = []

        def x_load(st, grp):
            slot = (2 * st + grp) % 4
            t0 = st * NT + grp * 512
            dma("sp", xs[:, slot, :, :], xT_d[:, :, t0:t0 + 512].rearrange("c p t -> p c t"),
                [], [("x", slot, c) for c in range(8)], xl_sems[slot])

        def rope_load(st):
            for grp in range(2):
                t0 = st * NT + grp * 512
                dma("sp", ropeb[:, grp, :, :], rope_d[:, :, t0:t0 + 512].rearrange("a p t -> p a t"),
                    [], [("rope", grp)], rope_sems[grp])

        x_load(0, 0)
        x_load(0, 1)
        rope_load(0)
        for st in range(n_st):
            cur["st"] = st

            def final_store(grp, st=st):
                slot = (2 * st + grp) % 4
                t0 = st * NT + grp * 512
                dma("sp", yT_d[:, :, t0:t0 + 512].rearrange("c p t -> p c t"), xs[:, slot, :, :],
                    [("x", slot, c) for c in range(8)], [], st_sems[slot])
                store_ops.append(S.lastop["sp"])

            def prefetch_next():
                if st + 1 < n_st:
                    x_load(st + 1, 0)
                    x_load(st + 1, 1)
                    rope_load(st + 1)

            def last_phase():
                prefetch_next()
                ffn(st, 1, final_store)

            phases = [lambda: attention(st), lambda: ffn(st, 0), lambda: sgu(st), last_phase]
            nph = 4 if stop_after is None else stop_after
            for pi in range(nph):
                phases[pi]()
            if stop_after is not None and stop_after < 4:
                pre_done["v"] = False
                deferred.clear()
                run_post(99)
                while wstate["used"] < NSEQ * (st + 1):
                    w_next()
                prefetch_next()
                for grp in range(2):
                    final_store(grp)
        fin = S.add("sp", lambda e: None, [], [])
        fin.deps = list(store_ops)

        S.finalize(esems)

        @block.tensor
        def _(e):
            S.run("pe", e)

        @block.scalar
        def _(e):
            S.run("act", e)

        @block.vector
        def _(e):
            S.run("dve", e)

        @block.gpsimd
        def _(e):
            S.run("pool", e)

        @block.sync
        def _(e):
            S.run("sp", e)
    return nc


def _qcols(s):
    hA = 8 * (s // 4) + (s % 4)
    return np.concatenate([np.arange(hA * 64, hA * 64 + 64), np.arange((hA + 4) * 64, (hA + 4) * 64 + 64)])


def _tileB(W):
    nk = W.shape[0] // 128
    return W.reshape(nk, 128, W.shape[1]).transpose(1, 0, 2)


def _prep(inp):
    f = np.float32
    wqkv = np.asarray(inp["attn_w_qkv"], f)[0]
    bqkv = np.asarray(inp["attn_b_qkv"], f)[0]
    wo = np.asarray(inp["attn_w_o"], f)[0]
    w_in = np.asarray(inp["sgu_w_in"], f)[0]
    w_out = np.asarray(inp["sgu_w_out"], f)[0]
    wgu = np.asarray(inp["ffn_w_gate_up"], f)
    wdn = np.asarray(inp["ffn_w_down"], f)
    slabs = np.zeros((NSLAB, 128, SLAB), f)
    idx = 0

    def put(i, arr):
        a = np.ascontiguousarray(arr).reshape(128, -1)
        slabs[i, :, :a.shape[1]] = a

    def ffn_slabs(layer, idx):
        for fi in range(NF):
            g = _tileB(wgu[layer][:, fi * 128:(fi + 1) * 128])
            u = _tileB(wgu[layer][:, 2816 + fi * 128:2816 + (fi + 1) * 128])
            put(idx, np.stack([g, u], axis=1))
            idx += 1
        for c in range(8):
            t = _tileB(wdn[layer][:, c * 128:(c + 1) * 128])
            put(idx, t[:, 0:11])
            idx += 1
            put(idx, t[:, 11:22])
            idx += 1
        return idx

    for j in range(4):
        tiles = [_tileB(wqkv[:, _qcols(2 * j + ti)]) for ti in range(2)]
        put(idx, np.stack(tiles, axis=1))
        idx += 1
    tiles = [_tileB(wqkv[:, 1024 + t * 128:1024 + (t + 1) * 128]) for t in range(2)]
    put(idx, np.stack(tiles, axis=1))
    idx += 1
    put(idx, wqkv[:, 1280:1536].reshape(8, 128, 256).transpose(1, 0, 2))
    idx += 1
    rowperm = np.concatenate([_qcols(k) for k in range(8)])
    wo_p = wo[rowperm, :]
    for i in range(4):
        tiles = [_tileB(wo_p[:, (2 * i + mi) * 128:(2 * i + mi + 1) * 128]) for mi in range(2)]
        put(idx, np.stack(tiles, axis=1))
        idx += 1
    idx = ffn_slabs(0, idx)
    for i in range(4):
        tiles = [_tileB(w_in[:, (2 * i + mi) * 128:(2 * i + mi + 1) * 128]) for mi in range(2)]
        put(idx, np.stack(tiles, axis=1))
        idx += 1
    for vh in range(2):
        for kh in range(2):
            blk = w_in[kh * 512:(kh + 1) * 512, 1024 + vh * 512:1024 + (vh + 1) * 512]
            put(idx, blk.reshape(4, 128, 512).transpose(1, 0, 2))
            idx += 1
    for i in range(4):
        tiles = [_tileB(w_out[:, (2 * i + mi) * 128:(2 * i + mi + 1) * 128]) for mi in range(2)]
        put(idx, np.stack(tiles, axis=1))
        idx += 1
    idx = ffn_slabs(1, idx)
    assert idx == NSLAB

    vecs = np.zeros((128, NV), f)

    def putv(name, vec, n):
        vecs[:, VC[name]:VC[name] + n] = np.asarray(vec, f).reshape(n, 128).T

    for i in range(2):
        putv("gmp%d" % i, inp["norm_mix_pre"][i], 8)
        putv("gmo%d" % i, inp["norm_mix_post"][i], 8)
        putv("gfp%d" % i, inp["norm_ffn_pre"][i], 8)
        putv("gfo%d" % i, inp["norm_ffn_post"][i], 8)
    putv("bo", inp["attn_b_o"][0], 8)
    swp = np.arange(128) ^ 32
    for s in range(8):
        cols = _qcols(s)
        vecs[:, VC["bq"] + s] = bqkv[cols]
        vecs[:, VC["bqs"] + s] = bqkv[cols[swp]]
    for t in range(2):
        cols = 1024 + t * 128 + np.arange(128)
        vecs[:, VC["bk"] + t] = bqkv[cols]
        vecs[:, VC["bks"] + t] = bqkv[cols[swp]]

    rows = np.zeros((1, NROW), f)
    rows[0, 0:256] = bqkv[1280:1536]
    rows[0, 256:272] = np.asarray(inp["attn_sinks"], f)[0]
    rows[0, 272:1296] = np.asarray(inp["sgu_ln_g"], f)[0]
    rows[0, 1296:2320] = np.asarray(inp["sgu_ln_b"], f)[0]
    rows[0, 2320:3344] = np.asarray(inp["sgu_b_spatial"], f)[0].reshape(-1)
    wsp = np.asarray(inp["sgu_w_spatial"], f)[0]
    wsT = np.ascontiguousarray(wsp.transpose(2, 0, 1)).reshape(128, 1024)

    half = 32
    inv_freq = 10000.0 ** (-(np.arange(half, dtype=np.float64) * 2.0) / 64.0)
    pos = np.arange(4096, dtype=np.float64)
    ang = pos[None, :] * inv_freq[:, None]
    cos = np.cos(ang).astype(f)
    sin = np.sin(ang).astype(f)
    p = np.arange(128)
    rope = np.zeros((2, 128, 4096), f)
    rope[0] = cos[p % 32]
    sgn = np.where((p % 64) < 32, -1.0, 1.0).astype(f)
    rope[1] = sin[p % 32] * sgn[:, None]
    cm = np.zeros((128, 1024 + 128), f)
    j = np.arange(128)[:, None]
    i = np.arange(128)[None, :]
    prev = (j > i).astype(f)
    cur = (j <= i).astype(f)
    cm[:, 0:512] = np.tile(prev, (1, 4))
    cm[:, 512:1024] = np.tile(cur, (1, 4))
    cm[:, 1024:1152] = cur
    return dict(wst=slabs, vecs=vecs, rows=rows, wsT=wsT, rope=rope, cmask=cm)


_CACHE = {}


def _run(inputs, n_st, stop_after=None, n_cores=8):
    x = np.asarray(inputs["x"], np.float32)
    shared = _prep(inputs)
    TOK = n_st * NT
    key = (n_st, stop_after)
    if key not in _CACHE:
        _CACHE[key] = build(n_st, stop_after)
    nc = _CACHE[key]
    in_maps = []
    for b in range(n_cores):
        xT = np.ascontiguousarray(x[b, :TOK, :].T).reshape(8, 128, TOK)
        m = dict(shared)
        m["rope"] = np.ascontiguousarray(shared["rope"][:, :, :TOK])
        m["xT"] = xT
        in_maps.append(m)
    res = run_bass_kernel_spmd(nc, in_maps, core_ids=list(range(n_cores)))
    out = np.empty((n_cores, TOK, 1024), np.float32)
    for b in range(n_cores):
        out[b] = res.results[b]["yT"].reshape(1024, TOK).T
    return out


def kernel(**inputs):
    return _run(inputs, 4)
```
